# Optimizing a Trainium2 kernel written in Bass

```python
import jax, jax.numpy as jnp
from jax import lax
import numpy as np

D_MODEL = 2048
BATCH = 4
SEQ = 2048
DEPTH = 2
DEC_BATCH = 128
DEC_SEQ = 8
PAST_LEN = 8192
PAGE_SIZE = 128

N_BRANCH = 4
BRANCH_W = D_MODEL // 2
A_HD = 64
A_HEADS = BRANCH_W // A_HD
A_DECAY_LORA = 64
A_AAA_LORA = 64
B_HD = 64
B_HEADS = BRANCH_W // B_HD
B_KV_HEADS = B_HEADS // 8
WINDOW = 128
ROPE_THETA = 10000.0
C_HEADS = 8
C_V = BRANCH_W // C_HEADS
C_QK = C_V // 2
C_CONV = 4
D_HEADS = 8
D_QK = BRANCH_W // D_HEADS
D_V = BRANCH_W // D_HEADS
CHUNK = 64
LN_EPS = 1e-5
NEG_INF = -1e30
ALPHA = (2 * DEPTH) ** 0.25
BETA = (8 * DEPTH) ** -0.25

A_COLS = 3 * BRANCH_W + A_DECAY_LORA + A_AAA_LORA
B_COLS = (B_HEADS + 2 * B_KV_HEADS) * B_HD
C_CONV_COLS = 2 * C_HEADS * C_QK
C_COLS = C_CONV_COLS + C_HEADS * C_V + 2 * C_HEADS
D_COLS = D_HEADS * (2 * D_QK + D_V)
MIX_COLS = A_COLS + B_COLS + C_COLS + D_COLS
IN_COLS = MIX_COLS + N_BRANCH * BRANCH_W + N_BRANCH * D_MODEL

kernel_name = 'hybrid_rwkv7_swa_mlstm_retention_step'


def _layer_norm(x, g, b):
    xf = x.astype(jnp.float32)
    mu = jnp.mean(xf, -1, keepdims=True)
    var = jnp.mean(jnp.square(xf - mu), -1, keepdims=True)
    return ((xf - mu) * lax.rsqrt(var + LN_EPS) * g + b).astype(x.dtype)


def _head_norm(y, g, b):
    mu = jnp.mean(y, -1, keepdims=True)
    var = jnp.mean(jnp.square(y - mu), -1, keepdims=True)
    yn = (y - mu) * lax.rsqrt(var + LN_EPS)
    return yn.reshape(y.shape[0], y.shape[1], -1) * g + b


def _rope(x, pos):
    d = x.shape[-1]
    inv = ROPE_THETA ** (-jnp.arange(0, d, 2, dtype=jnp.float32) / d)
    ang = pos.astype(jnp.float32)[:, None] * inv[None, :]
    cos = jnp.cos(ang)[None, :, None, :]
    sin = jnp.sin(ang)[None, :, None, :]
    xf = x.astype(jnp.float32)
    x1, x2 = xf[..., : d // 2], xf[..., d // 2:]
    return jnp.concatenate([x1 * cos - x2 * sin, x1 * sin + x2 * cos], -1).astype(x.dtype)


def _chunk_len(L):
    return CHUNK if L % CHUNK == 0 else L


def _to_chunks(t, Lc):
    Bn, L = t.shape[:2]
    return jnp.moveaxis(t.reshape((Bn, L // Lc, Lc) + t.shape[2:]), 1, 0)


def _from_chunks(t):
    nc, Bn, Lc = t.shape[:3]
    return jnp.moveaxis(t, 0, 1).reshape((Bn, nc * Lc) + t.shape[3:])


def _rwkv7_branch(pa, shift_prev, S0, mu, w0, w_up, a0, a_up, k_k, k_a, r_k, gn_g, gn_b):
    Bn, L, _ = pa.shape
    W = BRANCH_W
    prev = jnp.concatenate([shift_prev.astype(pa.dtype)[:, None, :], pa[:, :-1]], axis=1)
    xs = (pa + (prev - pa) * mu).astype(jnp.float32)
    r = xs[..., :W]
    k = xs[..., W:2 * W]
    v = xs[..., 2 * W:3 * W]
    w_lo = xs[..., 3 * W:3 * W + A_DECAY_LORA]
    a_lo = xs[..., 3 * W + A_DECAY_LORA:]
    w_log = -jax.nn.softplus(-(w0 + jnp.tanh(w_lo) @ w_up)) - 0.5
    decay = jnp.exp(-jnp.exp(w_log))
    a = jax.nn.sigmoid(a0 + a_lo @ a_up)
    heads = lambda t: t.reshape(Bn, L, A_HEADS, A_HD)
    kk = heads(k * k_k)
    kk = kk / jnp.maximum(jnp.sqrt(jnp.sum(jnp.square(kk), -1, keepdims=True)), 1e-12)
    k = k * (1.0 + (a - 1.0) * k_a)
    r, k, v, decay, a = heads(r), heads(k), heads(v), heads(decay), heads(a)

    def step(S, inp):
        r_t, w_t, k_t, v_t, kk_t, a_t = inp
        sa = jnp.einsum('bhvk,bhk->bhv', S, -kk_t)
        S = (S * w_t[:, :, None, :] + sa[..., None] * (kk_t * a_t)[:, :, None, :]
             + v_t[..., None] * k_t[:, :, None, :])
        return S, jnp.einsum('bhvk,bhk->bhv', S, r_t)

    seq = tuple(jnp.moveaxis(t, 1, 0) for t in (r, decay, k, v, kk, a))
    S, y = lax.scan(step, S0.astype(jnp.float32), seq)
    y = jnp.moveaxis(y, 0, 1)
    bonus = jnp.sum(r * k * r_k, -1, keepdims=True) * v
    y = _head_norm(y, gn_g, gn_b) + bonus.reshape(Bn, L, W)
    return y, S.astype(S0.dtype), pa[:, -1].astype(shift_prev.dtype)


def _swa_branch(pb, pos, pos0, k_buf, v_buf, sinks):
    Bn, L, _ = pb.shape
    G = B_HEADS // B_KV_HEADS
    qw = B_HEADS * B_HD
    kw = B_KV_HEADS * B_HD
    q = _rope(pb[..., :qw].reshape(Bn, L, B_HEADS, B_HD), pos)
    k = _rope(pb[..., qw:qw + kw].reshape(Bn, L, B_KV_HEADS, B_HD), pos)
    v = pb[..., qw + kw:].reshape(Bn, L, B_KV_HEADS, B_HD)
    k_buf = k_buf.astype(k.dtype)
    v_buf = v_buf.astype(v.dtype)
    k_ext = jnp.concatenate([k_buf, k], 1)
    v_ext = jnp.concatenate([v_buf, v], 1)
    Lb = WINDOW if L % WINDOW == 0 else L
    nb = L // Lb
    if Lb == WINDOW:
        prev_k = k_ext[:, :L].reshape(Bn, nb, WINDOW, B_KV_HEADS, B_HD)
        prev_v = v_ext[:, :L].reshape(Bn, nb, WINDOW, B_KV_HEADS, B_HD)
    else:
        prev_k = k_buf[:, None]
        prev_v = v_buf[:, None]
    kb = jnp.concatenate([prev_k, k.reshape(Bn, nb, Lb, B_KV_HEADS, B_HD)], 2)
    vb = jnp.concatenate([prev_v, v.reshape(Bn, nb, Lb, B_KV_HEADS, B_HD)], 2)
    qb = q.reshape(Bn, nb, Lb, B_KV_HEADS, G, B_HD)
    a_idx = jnp.arange(Lb)[:, None]
    c_idx = jnp.arange(WINDOW + Lb)[None, :]
    key_pos = pos0 + jnp.arange(nb)[:, None, None] * Lb - WINDOW + c_idx
    mask = (c_idx >= a_idx) & (c_idx <= WINDOW + a_idx) & (key_pos >= 0)
    s = jnp.einsum('bnqhgd,bnkhd->bnhgqk', qb, kb).astype(jnp.float32) * (B_HD ** -0.5)
    s = jnp.where(mask[None, :, None, None], s, NEG_INF)
    sink = sinks.astype(jnp.float32).reshape(B_KV_HEADS, G)[:, :, None, None]
    m = jnp.maximum(jnp.max(s, -1, keepdims=True), sink)
    p = jnp.exp(s - m)
    p = p / (jnp.sum(p, -1, keepdims=True) + jnp.exp(sink - m))
    o = jnp.einsum('bnhgqk,bnkhd->bnqhgd', p.astype(vb.dtype), vb).reshape(Bn, L, BRANCH_W)
    return o, k_ext[:, -WINDOW:], v_ext[:, -WINDOW:]


def _mlstm_chunkwise(q, k, v, logi, logf, C0, n0, m0):
    L = q.shape[1]
    Lc = _chunk_len(L)
    causal = jnp.tril(jnp.ones((Lc, Lc), dtype=bool))

    def step(carry, inp):
        C, n, m = carry
        qc, kc, vc, li, lf = inp
        b = jnp.moveaxis(jnp.cumsum(lf, axis=1), 1, -1)
        li = jnp.moveaxis(li, 1, -1)
        dmat = jnp.where(causal, b[..., :, None] - b[..., None, :] + li[..., None, :], NEG_INF)
        inter = b + m[..., None]
        mt = jnp.maximum(inter, jnp.max(dmat, -1))
        wts = jnp.exp(dmat - mt[..., None])
        sc = jnp.exp(inter - mt)
        A = jnp.einsum('bqhd,bkhd->bhqk', qc, kc) * wts
        num = jnp.einsum('bhqk,bkhv->bhqv', A, vc) + sc[..., None] * jnp.einsum('bqhd,bhdv->bhqv', qc, C)
        den = jnp.sum(A, -1) + sc * jnp.einsum('bqhd,bhd->bhq', qc, n)
        h = num / jnp.maximum(jnp.abs(den), jnp.exp(-mt))[..., None]
        gl = b[..., -1:] - b + li
        m_new = jnp.maximum(b[..., -1] + m, jnp.max(gl, -1))
        wj = jnp.exp(gl - m_new[..., None])
        s_old = jnp.exp(b[..., -1] + m - m_new)
        C_new = s_old[..., None, None] * C + jnp.einsum('bhk,bkhd,bkhv->bhdv', wj, kc, vc)
        n_new = s_old[..., None] * n + jnp.einsum('bhk,bkhd->bhd', wj, kc)
        return (C_new, n_new, m_new), jnp.moveaxis(h, 1, 2)

    (C, n, m), hs = lax.scan(step, (C0, n0, m0), tuple(_to_chunks(t, Lc) for t in (q, k, v, logi, logf)))
    return _from_chunks(hs), C, n, m


def _mlstm_branch(pc, conv_buf, C0, n0, m0, conv_w, conv_b, i_bias, f_bias, gn_g, gn_b):
    Bn, L, _ = pc.shape
    u = pc[..., :C_CONV_COLS]
    ext = jnp.concatenate([conv_buf.astype(u.dtype), u], 1)
    conv = conv_b + ext[:, 0:L] * conv_w[0]
    for j in range(1, C_CONV):
        conv = conv + ext[:, j:j + L] * conv_w[j]
    conv = jax.nn.silu(conv.astype(jnp.float32))
    qk_w = C_HEADS * C_QK
    q = conv[..., :qk_w].reshape(Bn, L, C_HEADS, C_QK)
    k = conv[..., qk_w:].reshape(Bn, L, C_HEADS, C_QK) * (C_QK ** -0.5)
    v = pc[..., C_CONV_COLS:C_CONV_COLS + C_HEADS * C_V].astype(jnp.float32).reshape(Bn, L, C_HEADS, C_V)
    gates = pc[..., C_CONV_COLS + C_HEADS * C_V:].astype(jnp.float32)
    logi = gates[..., :C_HEADS] + i_bias
    logf = jax.nn.log_sigmoid(gates[..., C_HEADS:] + f_bias)
    h, C, n, m = _mlstm_chunkwise(q, k, v, logi, logf, C0.astype(jnp.float32),
                                  n0.astype(jnp.float32), m0.astype(jnp.float32))
    y = _head_norm(h, gn_g, gn_b)
    return (y, C.astype(C0.dtype), n.astype(n0.dtype), m.astype(m0.dtype),
            ext[:, -(C_CONV - 1):].astype(conv_buf.dtype))


def _retention_branch(pd, pos, S0, gn_g, gn_b):
    Bn, L, _ = pd.shape
    qw = D_HEADS * D_QK
    q = _rope(pd[..., :qw].reshape(Bn, L, D_HEADS, D_QK), pos).astype(jnp.float32)
    k = _rope(pd[..., qw:2 * qw].reshape(Bn, L, D_HEADS, D_QK), pos).astype(jnp.float32) * (D_QK ** -0.5)
    v = pd[..., 2 * qw:].astype(jnp.float32).reshape(Bn, L, D_HEADS, D_V)
    log_gamma = jnp.log1p(-jnp.exp2(-5.0 - jnp.arange(D_HEADS, dtype=jnp.float32)))
    Lc = _chunk_len(L)
    idx = jnp.arange(Lc, dtype=jnp.float32)
    rel = idx[:, None] - idx[None, :]
    decay_mat = jnp.where(rel >= 0, jnp.exp(jnp.maximum(rel, 0.0) * log_gamma[:, None, None]), 0.0)
    q_dec = jnp.exp((idx + 1.0)[None, :] * log_gamma[:, None])
    k_dec = jnp.exp((Lc - 1.0 - idx)[None, :] * log_gamma[:, None])
    c_dec = jnp.exp(Lc * log_gamma)

    def step(S, inp):
        qc, kc, vc = inp
        inner = jnp.einsum('bqhd,bkhd->bhqk', qc, kc) * decay_mat
        o = (jnp.einsum('bhqk,bkhv->bqhv', inner, vc)
             + jnp.einsum('bqhd,bhdv->bqhv', qc, S) * q_dec.T[None, :, :, None])
        S = c_dec[None, :, None, None] * S + jnp.einsum('bkhd,bkhv,hk->bhdv', kc, vc, k_dec)
        return S, o

    S, os_ = lax.scan(step, S0.astype(jnp.float32), tuple(_to_chunks(t, Lc) for t in (q, k, v)))
    return _head_norm(_from_chunks(os_), gn_g, gn_b), S.astype(S0.dtype)


def _hybrid_layer(x, pos0, S_a, shift_a, k_buf, v_buf, C_m, n_m, m_m, conv_m, S_d,
                  w_in, a_mu, a_w0, a_w_up, a_a0, a_a_up, a_k_k, a_k_a, a_r_k, a_ln_g, a_ln_b,
                  b_sinks, c_conv_w, c_conv_b, c_i_bias, c_f_bias, c_ln_g, c_ln_b,
                  d_ln_g, d_ln_b, w_branch, w_out, ln_g, ln_b):
    Bn, L, _ = x.shape
    pos = pos0 + jnp.arange(L)
    proj = x @ w_in
    o1 = A_COLS
    o2 = o1 + B_COLS
    o3 = o2 + C_COLS
    o4 = MIX_COLS
    o5 = o4 + N_BRANCH * BRANCH_W
    y_a, S_a, shift_a = _rwkv7_branch(proj[..., :o1], shift_a, S_a, a_mu, a_w0, a_w_up, a_a0, a_a_up,
                                      a_k_k, a_k_a, a_r_k, a_ln_g, a_ln_b)
    y_b, k_buf, v_buf = _swa_branch(proj[..., o1:o2], pos, pos0, k_buf, v_buf, b_sinks)
    y_c, C_m, n_m, m_m, conv_m = _mlstm_branch(proj[..., o2:o3], conv_m, C_m, n_m, m_m, c_conv_w, c_conv_b,
                                               c_i_bias, c_f_bias, c_ln_g, c_ln_b)
    y_d, S_d = _retention_branch(proj[..., o3:o4], pos, S_d, d_ln_g, d_ln_b)
    z = jax.nn.silu(proj[..., o4:o5].astype(jnp.float32)).reshape(Bn, L, N_BRANCH, BRANCH_W)
    gate = jax.nn.sigmoid(proj[..., o5:].astype(jnp.float32)).reshape(Bn, L, N_BRANCH, D_MODEL)
    merged = jnp.zeros((Bn, L, D_MODEL), jnp.float32)
    for i, y_i in enumerate((y_a, y_b, y_c, y_d)):
        branch = (y_i * z[:, :, i]).astype(x.dtype) @ w_branch[i]
        merged = merged + gate[:, :, i] * branch
    out = merged.astype(x.dtype) @ w_out
    x_new = _layer_norm(ALPHA * x + out, ln_g, ln_b)
    return x_new, (S_a, shift_a, k_buf, v_buf, C_m, n_m, m_m, conv_m, S_d)


def _trunk(x, pos0, states, weights):
    new = []
    for l in range(DEPTH):
        x, st = _hybrid_layer(x, pos0, *[s[l] for s in states], *[w[l] for w in weights])
        new.append(st)
    stacked = tuple(jnp.stack([st[j] for st in new]) for j in range(len(states)))
    return x, stacked


def setup_inputs(seed: int = 0) -> dict:
    key = jax.random.key(seed)
    ks = iter(jax.random.split(key, 48))
    f32 = jnp.float32
    nrm = lambda shape, scale: scale * jax.random.normal(next(ks), shape, f32)
    W = BRANCH_W
    inp = {}
    inp['x_prompt'] = nrm((BATCH, SEQ, D_MODEL), 1.0)
    inp['x_sample'] = nrm((DEC_BATCH, DEC_SEQ, D_MODEL), 1.0)
    inp['state_rwkv_S'] = nrm((DEPTH, DEC_BATCH, A_HEADS, A_HD, A_HD), 0.3)
    inp['state_rwkv_shift'] = nrm((DEPTH, DEC_BATCH, A_COLS), 1.0)
    inp['cache_swa_k'] = nrm((DEPTH, DEC_BATCH, WINDOW, B_KV_HEADS, B_HD), 1.0)
    inp['cache_swa_v'] = nrm((DEPTH, DEC_BATCH, WINDOW, B_KV_HEADS, B_HD), 1.0)
    inp['state_mlstm_C'] = nrm((DEPTH, DEC_BATCH, C_HEADS, C_QK, C_V), 0.3)
    inp['state_mlstm_n'] = nrm((DEPTH, DEC_BATCH, C_HEADS, C_QK), 0.3)
    inp['state_mlstm_m'] = nrm((DEPTH, DEC_BATCH, C_HEADS), 0.5)
    inp['state_mlstm_conv'] = nrm((DEPTH, DEC_BATCH, C_CONV - 1, C_CONV_COLS), 1.0)
    inp['state_ret_S'] = nrm((DEPTH, DEC_BATCH, D_HEADS, D_QK, D_V), 0.3)
    inp['w_in'] = nrm((DEPTH, D_MODEL, IN_COLS), D_MODEL ** -0.5)
    inp['a_mu'] = jax.random.uniform(next(ks), (DEPTH, A_COLS), f32)
    inp['a_w0'] = nrm((DEPTH, W), 1.0) - 1.0
    inp['a_w_up'] = nrm((DEPTH, A_DECAY_LORA, W), 0.1 * A_DECAY_LORA ** -0.5)
    inp['a_a0'] = nrm((DEPTH, W), 0.1)
    inp['a_a_up'] = nrm((DEPTH, A_AAA_LORA, W), A_AAA_LORA ** -0.5)
    inp['a_k_k'] = 0.85 + nrm((DEPTH, W), 0.05)
    inp['a_k_a'] = 1.0 + nrm((DEPTH, W), 0.05)
    inp['a_r_k'] = nrm((DEPTH, A_HEADS, A_HD), 0.1)
    inp['a_ln_g'] = 1.0 + nrm((DEPTH, W), 0.05)
    inp['a_ln_b'] = nrm((DEPTH, W), 0.02)
    inp['b_sinks'] = nrm((DEPTH, B_HEADS), 0.5)
    inp['c_conv_w'] = nrm((DEPTH, C_CONV, C_CONV_COLS), C_CONV ** -0.5)
    inp['c_conv_b'] = nrm((DEPTH, C_CONV_COLS), 0.02)
    inp['c_i_bias'] = nrm((DEPTH, C_HEADS), 0.5) - 2.0
    inp['c_f_bias'] = jnp.linspace(3.0, 6.0, C_HEADS, dtype=f32)[None, :] + nrm((DEPTH, C_HEADS), 0.1)
    inp['c_ln_g'] = 1.0 + nrm((DEPTH, W), 0.05)
    inp['c_ln_b'] = nrm((DEPTH, W), 0.02)
    inp['d_ln_g'] = 1.0 + nrm((DEPTH, W), 0.05)
    inp['d_ln_b'] = nrm((DEPTH, W), 0.02)
    inp['w_branch'] = nrm((DEPTH, N_BRANCH, W, D_MODEL), BETA * W ** -0.5)
    inp['w_out'] = nrm((DEPTH, D_MODEL, D_MODEL), BETA * D_MODEL ** -0.5)
    inp['ln_g'] = 1.0 + nrm((DEPTH, D_MODEL), 0.05)
    inp['ln_b'] = nrm((DEPTH, D_MODEL), 0.02)
    return inp


def reference(x_prompt, x_sample, state_rwkv_S, state_rwkv_shift, cache_swa_k, cache_swa_v,
              state_mlstm_C, state_mlstm_n, state_mlstm_m, state_mlstm_conv, state_ret_S,
              w_in, a_mu, a_w0, a_w_up, a_a0, a_a_up, a_k_k, a_k_a, a_r_k, a_ln_g, a_ln_b,
              b_sinks, c_conv_w, c_conv_b, c_i_bias, c_f_bias, c_ln_g, c_ln_b,
              d_ln_g, d_ln_b, w_branch, w_out, ln_g, ln_b):
    weights = (w_in, a_mu, a_w0, a_w_up, a_a0, a_a_up, a_k_k, a_k_a, a_r_k, a_ln_g, a_ln_b,
               b_sinks, c_conv_w, c_conv_b, c_i_bias, c_f_bias, c_ln_g, c_ln_b,
               d_ln_g, d_ln_b, w_branch, w_out, ln_g, ln_b)
    sample_states = (state_rwkv_S, state_rwkv_shift, cache_swa_k, cache_swa_v,
                     state_mlstm_C, state_mlstm_n, state_mlstm_m, state_mlstm_conv, state_ret_S)
    n_prompt = x_prompt.shape[0]
    zero_states = tuple(jnp.zeros((DEPTH, n_prompt) + s.shape[2:], s.dtype) for s in sample_states)
    y_prompt, p = _trunk(x_prompt, 0, zero_states, weights)
    y_sample, s = _trunk(x_sample, PAST_LEN, sample_states, weights)
    return (y_prompt, y_sample, p[0], s[0], p[1], s[1], p[2], s[2], p[3], s[3], p[4], s[4],
            p[5], s[5], p[6], s[6], p[7], s[7], p[8], s[8])
```

```python
import contextlib
import numpy as np
import ml_dtypes
import concourse.bass as bass
import concourse.mybir as mybir
from concourse.bass_utils import run_bass_kernel_spmd

F32 = mybir.dt.float32
BF16 = mybir.dt.bfloat16
AF = mybir.ActivationFunctionType
ALU = mybir.AluOpType
AX = mybir.AxisListType

D_MODEL = 2048
DEPTH = 2
NCORES = 8
LP = 2048
NSS = 16
LS = 8
NTOK = LP + NSS * LS
NTT = NTOK // 128
NSEQ = 1 + NSS
W = 1024
IN_COLS = 21904
OA, OB, OC, OD, OZ, OG = 0, 3200, 4480, 6544, 9616, 13712
ALPHA = (2 * DEPTH) ** 0.25
LN_EPS = 1e-5
C0 = -float(np.exp(-0.5))
NPOSROW = LP + LS


class Res:
    __slots__ = ("w", "r", "ps")

    def __init__(self, ps=False):
        self.w = None
        self.r = {}
        self.ps = ps


class V:
    __slots__ = ("res", "ap")

    def __init__(self, res, ap):
        self.res = res
        self.ap = ap

    def __getitem__(self, idx):
        return V(self.res, self.ap[idx])

    def un(self, axis):
        return V(self.res, self.ap.unsqueeze(axis))

    def bc(self, shape):
        return V(self.res, self.ap.broadcast_to(list(shape)))

    def re(self, pat, **kw):
        return V(self.res, self.ap.rearrange(pat, **kw))

    def bits(self, dt):
        return V(self.res, self.ap.bitcast(dt))


class Stream(list):
    pos = 0


class Sched:
    NDMA = 40

    def __init__(self, nc, es):
        self.nc = nc
        self.es = es
        self.eng = {"pe": nc.tensor, "act": nc.scalar, "dve": nc.vector, "pool": nc.gpsimd, "sp": nc.sync}
        self.sems = []
        self.key = {}
        for n in self.eng:
            self.key[n] = len(self.sems)
            self.sems.append(es.enter_context(nc.semaphore("e_" + n)))
        self.dkeys = []
        for i in range(self.NDMA):
            self.dkeys.append(len(self.sems))
            self.sems.append(es.enter_context(nc.semaphore("d%d" % i)))
        self.val = [0] * len(self.sems)
        self.known = {n: {} for n in self.eng}
        self.dnext = 0
        self.uid = 0
        self.ninst = 0
        self.limit = 10 ** 9
        self.rec = None
        self.bundle = None
        self.fresh_swdge = False
        self._signaled = set()
        self.mhalf = None

    def sb(self, shape, dt=F32, name=None):
        self.uid += 1
        t = self.es.enter_context(self.nc.sbuf_tensor("%s_%d" % (name or "t", self.uid), list(shape), dt))
        return V(Res(), t[tuple(slice(None) for _ in shape)])

    def sb_in(self, es, shape, dt=F32, name=None):
        self.uid += 1
        t = es.enter_context(self.nc.sbuf_tensor("%s_%d" % (name or "t", self.uid), list(shape), dt))
        return V(Res(), t[tuple(slice(None) for _ in shape)])

    def _waits(self, en, reads, writes):
        deps = {}

        def add(k, v):
            if deps.get(k, 0) < v:
                deps[k] = v
        for r in reads:
            if r is not None and r.w is not None:
                add(*r.w)
        for w in writes:
            if w is None:
                continue
            if w.w is not None:
                add(*w.w)
            for k, v in w.r.items():
                add(k, v)
        e = self.eng[en]
        kn = self.known[en]
        for k, v in deps.items():
            if en == "pe" and k == self.key["pe"]:
                continue
            if kn.get(k, 0) < v:
                e.wait_ge(self.sems[k], v)
                kn[k] = v

    def _mark(self, ev, reads, writes):
        for r in reads:
            if r is not None and r.r.get(ev[0], 0) < ev[1]:
                r.r[ev[0]] = ev[1]
        for w in writes:
            if w is not None:
                w.w = ev
                w.r = {}

    def op(self, en, fn, reads, writes):
        if self.rec is not None:
            (self.bundle if self.bundle is not None else self.rec).append((0, (en, fn, reads, writes), None))
            return
        if self.ninst >= self.limit:
            return
        writes = [x.res for x in writes if x is not None] + [x.res for x in reads if x is not None and x.res.ps]
        reads = [x.res for x in reads if x is not None and not x.res.ps]
        self._waits(en, reads, writes)
        inst = fn(self.eng[en])
        k = self.key[en]
        self.val[k] += 1
        inst.then_inc(self.sems[k], 1)
        self.ninst += 1
        self._mark((k, self.val[k]), reads, writes)

    def dma(self, q, out, in_, **kw):
        if self.rec is not None:
            (self.bundle if self.bundle is not None else self.rec).append((1, (q, out, in_), kw))
            return
        if self.ninst >= self.limit:
            return
        if kw.pop("_nc", False):
            with self.nc.allow_non_contiguous_dma(reason="tiny strided state transfer"):
                return self.dma(q, out, in_, **kw)
        reads = [in_.res]
        writes = [out.res]
        if q == "pool" and self.fresh_swdge:
            k = len(self.sems)
            self.sems.append(self.es.enter_context(self.nc.semaphore("w%d" % k)))
            self.val.append(0)
        else:
            j = self.dnext
            self.dnext = (self.dnext + 1) % self.NDMA
            k = self.dkeys[j]
        if self.known[q].get(k, 0) < self.val[k]:
            self.eng[q].wait_ge(self.sems[k], self.val[k])
            self.known[q][k] = self.val[k]
        self._waits(q, reads, writes)
        inst = self.eng[q].dma_start(out=out.ap, in_=in_.ap, **kw)
        self.val[k] += 16
        inst.then_inc(self.sems[k], 16)
        self.ninst += 1
        self._mark((k, self.val[k]), reads, writes)

    def _est(self, kind, args):
        if kind == 2:
            en0, dur, rd, wr = None, 0.0, [], []
            for k2, a2, _ in args:
                e, d, r, w = self._est(k2, a2)
                en0 = en0 or e
                dur += d
                rd += r
                wr += w
            return en0, dur, rd, wr
        if kind == 1:
            q, out, in_ = args
            n = 1
            for d in out.ap.shape:
                n *= d
            return q, 2000.0 + n * 4 / 150.0, [in_.res], [out.res]
        en, fn, reads, writes = args
        n = 64
        if writes:
            n = 1
            for d in writes[0].ap.shape[1:]:
                n *= d
        nbig = 0
        for x in reads:
            if x is not None:
                m = 1
                for d in x.ap.shape[1:]:
                    m *= d
                if m >= n:
                    nbig += 1
        dve_rate = 2.1 if nbig >= 2 else 1.04
        dur = {"pe": 35 + n / 2.4, "act": 200 + n / 1.2, "dve": 80 + n * dve_rate, "pool": 150 + n * 2.2, "sp": 50}[en]
        return en, dur, [x.res for x in reads if x is not None], [x.res for x in writes if x is not None]

    @contextlib.contextmanager
    def atomic(self):
        if self.rec is None or self.bundle is not None:
            yield
            return
        self.bundle = []
        try:
            yield
        finally:
            b, self.bundle = self.bundle, None
            if b:
                self.rec.append((2, b, None))

    @contextlib.contextmanager
    def sub(self, name):
        if self.rec is None:
            yield
            return
        main = self.rec
        if not hasattr(main, "subs"):
            main.subs = {}
        self.rec = main.subs.setdefault(name, Stream())
        try:
            yield
        finally:
            self.rec = main

    def signal(self, tok):
        if self.rec is not None:
            self.rec.append((4, tok, None))

    def wait(self, tok):
        if self.rec is not None:
            self.rec.append((3, tok, None))

    def record(self, fn):
        old = self.rec
        self.rec = Stream()
        fn()
        out = self.rec
        self.rec = old
        return out

    def play(self, streams):
        if not hasattr(self, "_tfree"):
            self._tfree = {n: 0.0 for n in self.eng}
            self._wready = {}
            self._rdone = {}
        tfree, wready, rdone = self._tfree, self._wready, self._rdone
        streams = [s if isinstance(s, Stream) else Stream(s) for s in streams]
        for s in list(streams):
            streams += list(getattr(s, "subs", {}).values())
        sig = self._signaled
        ests = [None] * len(streams)
        while True:
            best = None
            bstart = None
            for i, s in enumerate(streams):
                while s.pos < len(s) and s[s.pos][0] in (3, 4):
                    if s[s.pos][0] == 4:
                        sig.add(s[s.pos][1])
                    elif s[s.pos][1] not in sig:
                        break
                    s.pos += 1
                if s.pos >= len(s) or s[s.pos][0] == 3:
                    continue
                if ests[i] is None:
                    kind, args, kw = s[s.pos]
                    en, dur, rd, wr = self._est(kind, args)
                    t = tfree[en]
                    for r in rd:
                        if r is None:
                            continue
                        t = max(t, wready.get(id(r), 0.0))
                        if r.ps:
                            t = max(t, rdone.get(id(r), 0.0))
                    for w in wr:
                        if w is None:
                            continue
                        t = max(t, wready.get(id(w), 0.0), rdone.get(id(w), 0.0))
                    ests[i] = (t, en, dur, rd, wr)
                if bstart is None or ests[i][0] < bstart:
                    best, bstart = i, ests[i][0]
            if best is None:
                if any(s.pos < len(s) for s in streams):
                    if any(s.pos < len(s) and (s[s.pos][0] == 4 or (s[s.pos][0] == 3 and s[s.pos][1] in sig))
                           for s in streams):
                        continue
                    raise RuntimeError("stream deadlock: every stream waits on an unsignalled token")
                break
            t, en, dur, rd, wr = ests[best]
            kind, args, kw = streams[best][streams[best].pos]
            streams[best].pos += 1
            if kind == 2:
                for k2, a2, kw2 in args:
                    if k2 == 0:
                        self.op(*a2)
                    else:
                        self.dma(*a2, **kw2)
                tfree[en] = t + dur
                tend = t + dur + 60.0
            elif kind == 0:
                self.op(*args)
                tfree[en] = t + dur
                tend = t + dur + 60.0
            else:
                self.dma(*args, **kw)
                tfree[en] = t + 60.0
                tend = t + dur
            for r in rd:
                if r is not None:
                    rdone[id(r)] = max(rdone.get(id(r), 0.0), tend)
            for w in wr:
                if w is not None:
                    wready[id(w)] = tend
            ests = [None] * len(streams)

    def barrier(self):
        for en, e in self.eng.items():
            kn = self.known[en]
            for k, v in enumerate(self.val):
                if v > 0 and kn.get(k, 0) < v:
                    e.wait_ge(self.sems[k], v)
                    kn[k] = v

    def finish(self):
        e = self.eng["sp"]
        kn = self.known["sp"]
        for k, v in enumerate(self.val):
            if v > 0 and kn.get(k, 0) < v:
                e.wait_ge(self.sems[k], v)
                kn[k] = v

    def mm(self, out, lhsT, rhs, start=True, stop=True):
        self.op("pe", lambda e: e.matmul(out.ap, lhsT=lhsT.ap, rhs=rhs.ap, start=start, stop=stop),
                [lhsT, rhs], [out])

    def tr(self, out, in_, ident):
        self.op("pe", lambda e: e.transpose(out.ap, in_.ap, ident.ap), [in_, ident], [out])

    def cp(self, en, out, in_):
        if en == "act":
            self.op(en, lambda e: e.copy(out=out.ap, in_=in_.ap), [in_], [out])
        else:
            self.op(en, lambda e: e.tensor_copy(out=out.ap, in_=in_.ap), [in_], [out])

    def tt(self, en, out, a, b, op):
        self.op(en, lambda e: e.tensor_tensor(out=out.ap, in0=a.ap, in1=b.ap, op=op), [a, b], [out])

    def ts(self, en, out, a, s1, op0, s2=None, op1=None):
        rd = [a]
        s1a = s1
        s2a = s2
        if isinstance(s1, V):
            rd.append(s1)
            s1a = s1.ap
        if isinstance(s2, V):
            rd.append(s2)
            s2a = s2.ap
        if op1 is None:
            self.op(en, lambda e: e.tensor_scalar(out=out.ap, in0=a.ap, scalar1=s1a, scalar2=None, op0=op0), rd, [out])
        else:
            self.op(en, lambda e: e.tensor_scalar(out=out.ap, in0=a.ap, scalar1=s1a, scalar2=s2a, op0=op0, op1=op1),
                    rd, [out])

    def stt(self, out, a, s, b, op0, op1):
        rd = [a, b]
        sa = s
        if isinstance(s, V):
            rd.append(s)
            sa = s.ap
        self.op("dve", lambda e: e.scalar_tensor_tensor(out=out.ap, in0=a.ap, scalar=sa, in1=b.ap, op0=op0, op1=op1),
                rd, [out])

    def act(self, out, in_, func, bias=None, scale=None, accum=None):
        rd = [in_]
        kw = {}
        if bias is not None:
            if isinstance(bias, V):
                rd.append(bias)
                kw["bias"] = bias.ap
            else:
                kw["bias"] = float(bias)
        if scale is not None:
            if isinstance(scale, V):
                rd.append(scale)
                kw["scale"] = scale.ap
            else:
                kw["scale"] = float(scale)
        wr = [out]
        if accum is not None:
            kw["accum_out"] = accum.ap
            wr.append(accum)
        self.op("act", lambda e: e.activation(out=out.ap, in_=in_.ap, func=func, **kw), rd, wr)

    def red(self, out, in_, op):
        self.op("dve", lambda e: e.tensor_reduce(out=out.ap, in_=in_.ap, axis=AX.X, op=op), [in_], [out])

    def memset(self, en, out, val):
        self.op(en, lambda e: e.memset(out.ap, val), [], [out])

    def scan(self, out, d0, d1, init, op0, op1):
        rd = [d0, d1]
        ia = init
        if isinstance(init, V):
            rd.append(init)
            ia = init.ap
        self.op("dve", lambda e: e.tensor_tensor_scan(out=out.ap, data0=d0.ap, data1=d1.ap, initial=ia, op0=op0, op1=op1),
                rd, [out])

    def recip(self, out, in_):
        self.op("dve", lambda e: e.reciprocal(out=out.ap, in_=in_.ap), [in_], [out])

    def rsqrt(self, out, in_, c, op):
        self.ts("dve", out, in_, c, op)
        if self.mhalf is None:
            self.act(out, out, AF.Sqrt)
            self.recip(out, out)
        else:
            shp = list(out.ap.shape)
            self.tt("pool", out, out, self.mhalf[:shp[0], :shp[1]], ALU.pow)


def lockstep(gens):
    gens = list(gens)
    while gens:
        nxt = []
        for g in gens:
            try:
                next(g)
                nxt.append(g)
            except StopIteration:
                pass
        gens = nxt


def dram(ap):
    return V(None, ap)


class Prog:
    def __init__(self, dbg=None):
        self.dbg = dbg or {}
        nc = bass.Bass("TRN2", target_bir_lowering=False)
        self.nc = nc
        self.es = contextlib.ExitStack()
        self.S = Sched(nc, self.es)
        self.S.limit = self.dbg.get('limit', 10 ** 9)
        self.S.fresh_swdge = bool(self.dbg.get('fresh_swdge'))
        self.declare_io()
        self.build()
        self.S.finish()
        self.es.close()

    def din(self, name, shape, dt=F32):
        return dram(self.nc.dram_tensor(name, list(shape), dt, kind="ExternalInput").ap())

    def dout(self, name, shape, dt=F32):
        return dram(self.nc.dram_tensor(name, list(shape), dt, kind="ExternalOutput").ap())

    def dscr(self, name, shape, dt=F32):
        kind = "ExternalOutput" if name in self.dbg.get("expose", ()) else "Internal"
        return dram(self.nc.dram_tensor(name, list(shape), dt, kind=kind).ap())

    def declare_io(self):
        I = {}
        I["xin"] = self.din("xin", [NTOK, D_MODEL])
        I["rS"] = self.din("rS", [DEPTH, NSEQ, 16, 64, 64])
        I["rshift"] = self.din("rshift", [DEPTH, NSEQ, 3200])
        I["ck"] = self.din("ck", [DEPTH, NSEQ, 128, 128])
        I["cv"] = self.din("cv", [DEPTH, NSEQ, 128, 128])
        I["mC"] = self.din("mC", [DEPTH, NSEQ, 8, 64, 128])
        I["mn"] = self.din("mn", [DEPTH, NSEQ, 8, 64])
        I["mm"] = self.din("mm", [DEPTH, NSEQ, 8])
        I["mconv"] = self.din("mconv", [DEPTH, NSEQ, 3, 1024])
        I["dS"] = self.din("dS", [DEPTH, NSEQ, 8, 128, 128])
        for n, s in WSHAPES.items():
            if self.dbg.get("tinyw") and n in ("w_in", "w_branch", "w_out"):
                s = [DEPTH, 128, 128]
            I[n] = self.din(n, s)
        for n, s in CONST_SHAPES.items():
            I[n] = self.din(n, s)
        self.I = I
        O = {}
        O["y"] = self.dout("y", [NTOK, D_MODEL])
        O["o_rS"] = self.dout("o_rS", [DEPTH, NSEQ, 16, 64, 64])
        O["o_rshift"] = self.dout("o_rshift", [DEPTH, NSEQ, 3200])
        O["o_ck"] = self.dout("o_ck", [DEPTH, NSEQ, 128, 128])
        O["o_cv"] = self.dout("o_cv", [DEPTH, NSEQ, 128, 128])
        O["o_mC"] = self.dout("o_mC", [DEPTH, NSEQ, 8, 64, 128])
        O["o_mn"] = self.dout("o_mn", [DEPTH, NSEQ, 8, 64])
        O["o_mm"] = self.dout("o_mm", [DEPTH, NSEQ, 8])
        O["o_mconv"] = self.dout("o_mconv", [DEPTH, NSEQ, 3, 1024])
        O["o_dS"] = self.dout("o_dS", [DEPTH, NSEQ, 8, 128, 128])
        self.O = O
        self.P = self.dscr("P", [NTOK, IN_COLS])
        self.YZ = self.dscr("YZ", [NTOK, 4 * W])
        self.MG = self.dscr("MG", [NTOK, D_MODEL], BF16)
        self.X1 = self.dscr("X1", [NTOK, D_MODEL])

    def build(self):
        S = self.S
        es = self.es
        self.PS = []
        for i in range(8):
            t = es.enter_context(self.nc.psum_tensor("ps%d" % i, [128, 512], F32))
            self.PS.append(V(Res(ps=True), t[:, :]))
        self.ident = S.sb([128, 128], F32, "ident")
        S.dma("sp", self.ident, self.I["c_ident"])
        self.identb = S.sb([128, 128], BF16, "identb")
        S.cp("dve", self.identb, self.ident)
        if not self.dbg.get("nopow"):
            mh = S.sb([128, 16], F32, "mhalf")
            S.memset("pool", mh, -0.5)
            S.mhalf = mh
        nl = self.dbg.get("layers", DEPTH)
        for l in range(nl):
            xsrc = self.I["xin"] if l == 0 else self.X1
            xdst = self.O["y"] if l == nl - 1 else self.X1
            self.b_done = False
            if self.dbg.get("p1", True):
                self.phase1(l, xsrc)
            S.barrier()
            if self.dbg.get("p2", True):
                self.phase2(l)
            S.barrier()
            if self.dbg.get("p3", True):
                self.phase3a(l)
                S.barrier()
                self.phase3b(l, xsrc, xdst)
            S.barrier()

    def phase1_setup(self, l, xsrc, es):
        S = self.S
        XT = S.sb_in(es, [128, 16, NTOK], BF16, "XT")
        with contextlib.ExitStack() as es2:
            xrow = [S.sb_in(es2, [128, D_MODEL], F32, "xrow") for _ in range(2)]
            for tt in range(NTT):
                xr = xrow[tt % 2]
                S.dma("sp", xr, xsrc[tt * 128:(tt + 1) * 128, :])
                for g in range(4):
                    ps = self.PS[(tt * 4 + g) % 8]
                    for j in range(4):
                        k = g * 4 + j
                        S.tr(ps[:, j * 128:(j + 1) * 128], xr[:, k * 128:(k + 1) * 128], self.ident)
                    S.cp("act" if g % 2 else "dve", XT[:, g * 4:(g + 1) * 4, tt * 128:(tt + 1) * 128],
                         ps.re("p (j c) -> p j c", j=4))
            S.barrier()
        WB = [S.sb_in(es, [128, 16, 512], BF16, "WB") for _ in range(3)]
        stg = [S.sb_in(es, [128, 512], F32, "stg") for _ in range(4)]
        return dict(XT=XT, WB=WB, stg=stg, n=0, c=0)

    @staticmethod
    def col_tiles(a, b):
        n = -(-(b - a) // 512)
        w = -(-(b - a) // n)
        w = -(-w // 8) * 8
        out = []
        while a < b:
            out.append((a, min(w, b - a)))
            a += w
        return out

    def phase1_cols(self, ctx, l, tiles, banks):
        S = self.S
        XT, WB, stg = ctx["XT"], ctx["WB"], ctx["stg"]
        wv = self.I["w_in"][l].re("(k p) c -> p k c", p=128)
        for (c0, cw) in tiles:
            wb = WB[ctx["c"] % 3]
            ctx["c"] += 1
            S.dma("pool", wb[:, :, :cw], wv[:, :, c0:c0 + cw])
            for tt in range(NTT):
                n = ctx["n"]
                ctx["n"] += 1
                ps = banks[n % len(banks)]
                with S.atomic():
                    for k in range(16):
                        S.mm(ps[:, :cw], XT[:, k, tt * 128:(tt + 1) * 128], wb[:, k, :cw], start=(k == 0), stop=(k == 15))
                sg = stg[n % 4]
                S.cp("act" if n % 2 else "dve", sg[:, :cw], ps[:, :cw])
                S.dma("sp", self.P[tt * 128:(tt + 1) * 128, c0:c0 + cw], sg[:, :cw])

    def phase1(self, l, xsrc):
        S = self.S
        first = self.col_tiles(OB, OC) + self.col_tiles(OZ + W, OZ + 2 * W)
        rest = self.col_tiles(0, OB) + self.col_tiles(OC, OZ + W) + self.col_tiles(OZ + 2 * W, IN_COLS)
        if self.dbg.get("ncol") is not None:
            rest = rest[:self.dbg["ncol"]]
        with contextlib.ExitStack() as es:
            ctx = self.phase1_setup(l, xsrc, es)
            self.phase1_cols(ctx, l, first, self.PS)
            S.barrier()
            if not self.dbg.get("p2", True) or "B" not in self.dbg.get("branches", "ABCD") or self.dbg.get("nooverlapB"):
                self.phase1_cols(ctx, l, rest, self.PS)
                self.b_done = False
                return
            with contextlib.ExitStack() as esB:
                sB = S.record(lambda: self.branchB(l, esB))
                sP = S.record(lambda: self.phase1_cols(ctx, l, rest, [self.PS[6], self.PS[7]]))
                S.play([sP, sB])
            self.b_done = True

    def seqs(self):
        out = [(0, LP, 64, 0, 0)]
        for j in range(NSS):
            out.append((LP + j * LS, LS, LS, j + 1, LP))
        ns = self.dbg.get("nseq", len(out))
        return out[:ns]

    def bload(self, es, src_row, n=W, rows=64):
        t = self.S.sb_in(es, [rows, n], F32, "bc")
        self.S.dma("sp", t, src_row.bc([rows, n]))
        return t

    def headnorm(self, x, T, H, dv, gt, bt, sc, on_act=False):
        S = self.S
        x3 = x[:T, :].re("p (h d) -> p h d", h=H)
        sq, ssum, ssq, mean, msq, var, rstd = sc["sq"], sc["s0"], sc["s1"], sc["s2"], sc["s3"], sc["s4"], sc["s5"]
        S.red(ssum[:T, :H], x3, ALU.add)
        S.act(sq[:T, :], x[:T, :], AF.Square)
        S.red(ssq[:T, :H], sq[:T, :].re("p (h d) -> p h d", h=H), ALU.add)
        S.ts("dve", mean[:T, :H], ssum[:T, :H], 1.0 / dv, ALU.mult)
        S.tt("dve", msq[:T, :H], mean[:T, :H], mean[:T, :H], ALU.mult)
        S.stt(var[:T, :H], ssq[:T, :H], 1.0 / dv, msq[:T, :H], ALU.mult, ALU.subtract)
        S.rsqrt(rstd[:T, :H], var[:T, :H], LN_EPS, ALU.add)
        if on_act:
            S.stt(msq[:T, :H], mean[:T, :H], -1.0, rstd[:T, :H], ALU.mult, ALU.mult)
            for h in range(H):
                S.act(x3[:, h, :], x3[:, h, :], AF.Identity, bias=msq[:T, h:h + 1], scale=rstd[:T, h:h + 1])
        else:
            S.tt("pool", x3, x3, mean[:T, :H].un(2).bc([T, H, dv]), ALU.subtract)
            S.tt("pool", x3, x3, rstd[:T, :H].un(2).bc([T, H, dv]), ALU.mult)
        S.tt("dve", x[:T, :], x[:T, :], gt[:T, :], ALU.mult)
        S.tt("pool", x[:T, :], x[:T, :], bt[:T, :], ALU.add)

    def gate_store(self, x, z, T, t0, bi):
        S = self.S
        S.act(z[:T, :], z[:T, :], AF.Silu)
        S.tt("dve", x[:T, :], x[:T, :], z[:T, :], ALU.mult)
        S.dma("sp", self.YZ[t0:t0 + T, bi * W:(bi + 1) * W], x[:T, :])

    def rope(self, out, x, cos, sin, T, nh, hd, t1, t2):
        S = self.S
        n = nh * hd
        t1 = out
        x4 = x[:T, :n].re("p (h two d) -> p h two d", h=nh, two=2)
        c4 = cos[:T, :hd].re("p (two d) -> p two d", two=2).un(1).bc([T, nh, 2, hd // 2])
        s4 = sin[:T, :hd].re("p (two d) -> p two d", two=2).un(1).bc([T, nh, 2, hd // 2])
        S.tt("pool", t1[:T, :n].re("p (h two d) -> p h two d", h=nh, two=2), x4, c4, ALU.mult)
        o4 = t2[:T, :n].re("p (h two d) -> p h two d", h=nh, two=2)
        S.tt("dve", o4[:, :, 0, :], x4[:, :, 1, :], s4[:, :, 0, :], ALU.mult)
        S.tt("dve", o4[:, :, 1, :], x4[:, :, 0, :], s4[:, :, 1, :], ALU.mult)
        S.tt("pool", out[:T, :n], t1[:T, :n], t2[:T, :n], ALU.add)

    def small(self, es, n=8, cols=16):
        return {"s%d" % i: self.S.sb_in(es, [128, cols], F32, "sm") for i in range(n)}

    def phase2(self, l):
        S = self.S
        br = self.dbg.get("branches", "ABCD")
        P_ = self.PS
        self.psmap = {}
        for name in br:
            if name in "CD" and "C" in br and "D" in br and not self.dbg.get("nointer"):
                continue
            if name == "B" and getattr(self, "b_done", False):
                continue
            with contextlib.ExitStack() as es:
                getattr(self, "branch" + name)(l, es)
            S.barrier()
        if "C" in br and "D" in br and not self.dbg.get("nointer"):
            self.psmap["D"] = [P_[0], P_[1], P_[0], P_[0], P_[1], P_[2], P_[3], None]
            self.psmap["C"] = [P_[4], P_[4], P_[4], P_[5], P_[6], P_[7], P_[5], P_[6]]
            with contextlib.ExitStack() as es:
                recs = []
                for name in "CD":
                    S.rec = Stream()
                    getattr(self, "branch" + name)(l, es)
                    recs.append(S.rec)
                S.rec = None
                S.play(recs)
            self.psmap = {}
            S.barrier()

    def branchD(self, l, es):
        S, I, O, PS = self.S, self.I, self.O, self.psmap.get("D", self.PS)
        sb = lambda shape, dt=F32, n="d": S.sb_in(es, shape, dt, n)
        dmt = sb([64, 8, 64]); S.dma("sp", dmt, I["c_dmt"])
        qdec = sb([128, 8, 64]); S.dma("sp", qdec, I["c_qdec"])
        kdec = {64: sb([64, W]), 8: sb([64, W])}
        cdec = {64: sb([128, W]), 8: sb([128, W])}
        for T in (64, 8):
            S.dma("sp", kdec[T], I["c_kdec%d" % T]); S.dma("sp", cdec[T], I["c_cdec%d" % T])
        gng = self.bload(es, I["d_ln_g"][l:l + 1, :]); gnb = self.bload(es, I["d_ln_b"][l:l + 1, :])
        Sf = sb([128, 8, 128]); Sb = sb([128, 8, 128], BF16)
        PD = sb([64, 3072]); cosT = sb([64, 128]); sinT = sb([64, 128]); zz = [sb([64, W]) for _ in range(3)]
        t1 = None; t2 = sb([64, 2048]); qkr = sb([64, 2048])
        vb = sb([64, 8, 128], BF16); ktb = sb([64, 8, 128], BF16)
        qT = sb([128, 8, 64], BF16); qdT = sb([128, 8, 64], BF16); kT = sb([128, 8, 64], BF16)
        inT = sb([64, 8, 64], BF16)
        os_ = [sb([64, W]) for _ in range(2)]
        sc = self.small(es); sc["sq"] = sb([64, W])
        items = []
        for (tok0, L, T, sidx, prow0) in self.seqs():
            nch = min(L // T, self.dbg.get('maxch', 10 ** 9))
            for ci in range(nch):
                items.append((tok0, L, T, sidx, prow0, ci, ci == 0, ci == nch - 1))

        def loads(k):
            tok0, L, T, sidx, prow0, ci, first, last = items[k]
            t0 = tok0 + ci * T
            pr = prow0 + ci * T
            S.dma("sp", PD[:T, :], self.P[t0:t0 + T, OD:OD + 3072])
            S.dma("sp", cosT[:T, :], I["c_rdc"][pr:pr + T, :])
            S.dma("sp", sinT[:T, :], I["c_rds"][pr:pr + T, :])
            S.dma("sp", zz[k % 3][:T, :], self.P[t0:t0 + T, OZ + 3 * W:OZ + 4 * W])

        loads(0)
        for k, (tok0, L, T, sidx, prow0, ci, first, last) in enumerate(items):
            t0 = tok0 + ci * T
            z = zz[k % 3]
            o = os_[k % 2]
            if k >= 2:
                S.wait(("Ddone", l, k - 2))
            if first:
                S.dma("sp", Sf, I["dS"][l, sidx].re("h d v -> d h v"))
                S.cp("act", Sb, Sf)
            self.rope(qkr, PD, cosT, sinT, T, 16, 128, t1, t2)
            S.cp("act", vb[:T].re("p h d -> p (h d)"), PD[:T, 2048:3072])
            S.tt("pool", ktb[:T].re("p h d -> p (h d)"), qkr[:T, W:2 * W], kdec[T][:T, :], ALU.mult)
            for h in range(8):
                S.tr(PS[0][:, h * T:(h + 1) * T], qkr[:T, h * 128:(h + 1) * 128], self.ident[:T, :T])
                S.tr(PS[1][:, h * T:(h + 1) * T], qkr[:T, W + h * 128:W + (h + 1) * 128], self.ident[:T, :T])
            q3 = PS[0][:, :8 * T].re("p (h t) -> p h t", h=8)
            S.cp("act", qT[:, :, :T], q3)
            S.tt("dve", qdT[:, :, :T], q3, qdec[:, :, :T], ALU.mult)
            S.cp("act", kT[:, :, :T], PS[1][:, :8 * T].re("p (h t) -> p h t", h=8))
            if k + 1 < len(items):
                loads(k + 1)
            for h in range(8):
                S.mm(PS[2][:T, h * T:(h + 1) * T], kT[:, h, :T], qT[:, h, :T])
            S.tt("dve", inT[:T, :, :T], PS[2][:T, :8 * T].re("p (h t) -> p h t", h=8), dmt[:T, :, :T], ALU.mult)
            for h in range(8):
                pso = PS[3 + h // 4][:T, (h % 4) * 128:(h % 4 + 1) * 128]
                with S.atomic():
                    S.mm(pso, inT[:T, h, :T], vb[:T, h, :], start=True, stop=False)
                    S.mm(pso, qdT[:, h, :T], Sb[:, h, :], start=False, stop=True)
            S.cp("act", o[:T, 0:512], PS[3][:T, :])
            S.cp("act", o[:T, 512:1024], PS[4][:T, :])
            for h in range(8):
                S.mm(PS[5 + h // 4][:, (h % 4) * 128:(h % 4 + 1) * 128], ktb[:T, h, :], vb[:T, h, :])
            Sf2 = Sf.re("p h d -> p (h d)")
            S.tt("pool", Sf2, Sf2, cdec[T], ALU.mult)
            S.tt("dve", Sf2[:, 0:512], Sf2[:, 0:512], PS[5], ALU.add)
            S.tt("dve", Sf2[:, 512:1024], Sf2[:, 512:1024], PS[6], ALU.add)
            S.cp("act", Sb, Sf)
            S.signal(("Dmain", l, k))
            with S.sub("post"):
                S.wait(("Dmain", l, k))
                self.headnorm(o, T, 8, 128, gng, gnb, sc, on_act=True)
                self.gate_store(o, z, T, t0, 3)
                S.signal(("Ddone", l, k))
            if last:
                S.dma("sp", O["o_dS"][l, sidx].re("h d v -> d h v"), Sf)

    def branchB(self, l, es):
        S, I, O, PS = self.S, self.I, self.O, self.PS
        sb = lambda shape, dt=F32, n="b": S.sb_in(es, shape, dt, n)
        mp = sb([128, 256]); S.dma("sp", mp, I["c_mp"])
        mp0 = sb([128, 256]); S.dma("sp", mp0, I["c_mp0"])
        ms = sb([128, 256]); S.dma("sp", ms, I["c_ms"])
        sink = self.bload(es, I["b_sinks"][l:l + 1, :], 16, 128)
        kTp = [sb([64, 2, 128], BF16) for _ in range(2)]
        vp = [sb([128, 2, 64], BF16) for _ in range(2)]
        ckf = sb([128, 128]); ckb = sb([128, 128], BF16)
        PB = sb([128, 1280]); cosT = sb([128, 64]); sinT = sb([128, 64]); zz = [sb([128, W]) for _ in range(2)]
        t1 = None; t2 = sb([128, 1152]); qkr = sb([128, 1152]); qkb = sb([128, 1152], BF16)
        qT = sb([64, 16, 128], BF16)
        s_sbs = [sb([128, 2, 256]) for _ in range(4)]; p_bfs = [sb([128, 2, 256], BF16) for _ in range(4)]
        pTs = [sb([128, 4, 128], BF16) for _ in range(4)]
        sms = [self.small(es, 6, 2) for _ in range(4)]
        rden = sb([128, 16])
        yb = sb([128, W])
        bitems = []
        for (tok0, L, T, sidx, prow0) in self.seqs():
            Lb = 128 if L % 128 == 0 else L
            for bi in range(min(L // Lb, self.dbg.get('maxch', 10 ** 9))):
                bitems.append((tok0 + bi * Lb, prow0 + bi * Lb, Lb))

        def loadsB(k):
            t0_, pr_, Lb_ = bitems[k]
            S.dma("sp", PB[:Lb_, :], self.P[t0_:t0_ + Lb_, OB:OB + 1280])
            S.dma("sp", cosT[:Lb_, :], I["c_rbc"][pr_:pr_ + Lb_, :])
            S.dma("sp", sinT[:Lb_, :], I["c_rbs"][pr_:pr_ + Lb_, :])
            S.dma("sp", zz[k % 2][:Lb_, :], self.P[t0_:t0_ + Lb_, OZ + W:OZ + 2 * W])

        loadsB(0)
        kB = 0
        for (tok0, L, T, sidx, prow0) in self.seqs():
            Lb = 128 if L % 128 == 0 else L
            nb = L // Lb
            prompt = (sidx == 0)
            if prompt:
                S.memset("pool", kTp[0], 0.0)
                S.memset("pool", vp[0], 0.0)
            else:
                S.dma("sp", ckf, I["ck"][l, sidx])
                S.cp("act", ckb, ckf)
                psb = PS[0].bits(BF16)
                for g in range(2):
                    S.tr(psb[:64, g * 128:(g + 1) * 128], ckb[:, g * 64:(g + 1) * 64], self.identb)
                S.cp("dve", kTp[0], psb[:64, 0:256].re("p (g t) -> p g t", g=2))
                S.dma("sp", ckf, I["cv"][l, sidx])
                S.cp("act", vp[0].re("p g d -> p (g d)"), ckf)
            for bi in range(min(nb, self.dbg.get('maxch', 10 ** 9))):
                t0 = tok0 + bi * Lb
                pr = prow0 + bi * Lb
                prev, cur = (bi % 2, (bi + 1) % 2)
                Wk = 128 + Lb
                mask = (mp0 if bi == 0 else mp) if prompt else ms
                z = zz[kB % 2]
                self.rope(qkr, PB, cosT, sinT, Lb, 18, 64, t1, t2)
                S.cp("act", qkb[:Lb, :], qkr[:Lb, :])
                S.cp("act", vp[cur][:Lb].re("p g d -> p (g d)"), PB[:Lb, 1152:1280])
                psq = [PS[0].bits(BF16), PS[1].bits(BF16), PS[2].bits(BF16)]
                for h in range(18):
                    S.tr(psq[h // 8][:64, (h % 8) * 128:(h % 8) * 128 + Lb], qkb[:Lb, h * 64:(h + 1) * 64],
                         self.identb[:Lb, :Lb])
                for g in range(2):
                    S.cp("act" if g else "dve", qT[:, g * 8:(g + 1) * 8, :Lb],
                         psq[g][:64, :].re("p (h t) -> p h t", h=8)[:, :, :Lb])
                S.cp("dve", kTp[cur][:, :, :Lb], psq[2][:64, 0:256].re("p (g t) -> p g t", g=2)[:, :, :Lb])
                sbanks = [PS[0], PS[1], PS[3], PS[4]]

                def pair_chain(hp, slot):
                    g = hp // 4
                    pss = sbanks[slot]
                    s_sb, p_bf, pT, sm = s_sbs[slot], p_bfs[slot], pTs[slot], sms[slot]
                    for j in range(2):
                        h = hp * 2 + j
                        S.mm(pss[:Lb, j * 256:j * 256 + 128], qT[:, h, :Lb], kTp[prev][:, g, :])
                        S.mm(pss[:Lb, j * 256 + 128:j * 256 + 128 + Lb], qT[:, h, :Lb], kTp[cur][:, g, :Lb])
                    yield
                    ps3 = pss[:Lb, :].re("p (j c) -> p j c", j=2)[:, :, :Wk]
                    S.stt(s_sb[:Lb, :, :Wk], ps3, 0.125, mask[:Lb, :Wk].un(1).bc([Lb, 2, Wk]), ALU.mult, ALU.add)
                    yield
                    mx, m, negm, rs, dd, den = sm["s0"], sm["s1"], sm["s2"], sm["s3"], sm["s4"], sm["s5"]
                    S.red(mx[:Lb, :], s_sb[:Lb, :, :Wk], ALU.max)
                    yield
                    S.tt("dve", m[:Lb, :], mx[:Lb, :], sink[:Lb, hp * 2:hp * 2 + 2], ALU.max)
                    yield
                    S.ts("dve", negm[:Lb, :], m[:Lb, :], -1.0, ALU.mult)
                    S.tt("pool", dd[:Lb, :], sink[:Lb, hp * 2:hp * 2 + 2], m[:Lb, :], ALU.subtract)
                    yield
                    for j in range(2):
                        S.act(p_bf[:Lb, j, :Wk], s_sb[:Lb, j, :Wk], AF.Exp, bias=negm[:Lb, j:j + 1], accum=rs[:Lb, j:j + 1])
                    S.act(dd[:Lb, :], dd[:Lb, :], AF.Exp)
                    yield
                    pst = sbanks[slot].bits(BF16)
                    po = 0
                    for j in range(2):
                        S.tr(pst[:128, po + (2 * j) * 128:po + (2 * j) * 128 + Lb], p_bf[:Lb, j, 0:128], self.identb[:Lb, :Lb])
                        S.tr(pst[:Lb, po + (2 * j + 1) * 128:po + (2 * j + 1) * 128 + Lb], p_bf[:Lb, j, 128:128 + Lb],
                             self.identb[:Lb, :Lb])
                    S.tt("dve", den[:Lb, :], rs[:Lb, :], dd[:Lb, :], ALU.add)
                    yield
                    S.cp("act", pT[:, :, :Lb], pst[:, po:po + 512].re("p (s t) -> p s t", s=4)[:, :, :Lb])
                    S.recip(rden[:Lb, hp * 2:hp * 2 + 2], den[:Lb, :])
                    yield
                    for j in range(2):
                        h = hp * 2 + j
                        pso = PS[5 if h >= 8 else 2][:Lb, (h % 8) * 64:(h % 8 + 1) * 64]
                        S.mm(pso, pT[:, 2 * j, :Lb], vp[prev][:, g, :], start=True, stop=False)
                        S.mm(pso, pT[:Lb, 2 * j + 1, :Lb], vp[cur][:Lb, g, :], start=False, stop=True)
                    yield

                for hp0 in range(0, 8, 4):
                    lockstep([pair_chain(hp0 + s, s) for s in range(4)])
                for hh in range(2):
                    S.tt("dve", yb[:Lb, hh * 512:(hh + 1) * 512].re("p (h d) -> p h d", h=8),
                         PS[5 if hh else 2][:Lb, :].re("p (h d) -> p h d", h=8),
                         rden[:Lb, hh * 8:(hh + 1) * 8].un(2).bc([Lb, 8, 64]), ALU.mult)
                if bi == nb - 1:
                    if prompt:
                        S.dma("sp", O["o_ck"][l, sidx], qkr[:, 1024:1152])
                        S.dma("sp", O["o_cv"][l, sidx], PB[:, 1152:1280])
                    else:
                        S.dma("sp", O["o_ck"][l, sidx][0:128 - Lb, :], I["ck"][l, sidx][Lb:128, :])
                        S.dma("sp", O["o_cv"][l, sidx][0:128 - Lb, :], I["cv"][l, sidx][Lb:128, :])
                        S.dma("sp", O["o_ck"][l, sidx][128 - Lb:128, :], qkr[:Lb, 1024:1152])
                        S.dma("sp", O["o_cv"][l, sidx][128 - Lb:128, :], PB[:Lb, 1152:1280])
                kB += 1
                if kB < len(bitems):
                    loadsB(kB)
                self.gate_store(yb, z, Lb, t0, 1)

    def branchC(self, l, es):
        S, I, O, PS = self.S, self.I, self.O, self.psmap.get("C", self.PS)
        sb = lambda shape, dt=F32, n="c": S.sb_in(es, shape, dt, n)
        tri = sb([64, 64]); S.dma("sp", tri, I["c_tri"])
        madd = sb([64, 64]); S.dma("sp", madd, I["c_madd"])
        cw = [self.bload(es, I["c_conv_w"][l, j:j + 1, :]) for j in range(4)]
        cb = self.bload(es, I["c_conv_b"][l:l + 1, :])
        gng = self.bload(es, I["c_ln_g"][l:l + 1, :]); gnb = self.bload(es, I["c_ln_b"][l:l + 1, :])
        ibias = sb([8, 1]); S.dma("sp", ibias, I["c_i_bias"][l].re("(h o) -> h o", o=1))
        fbias = sb([8, 1]); S.dma("sp", fbias, I["c_f_bias"][l].re("(h o) -> h o", o=1))
        negfb = sb([8, 1]); S.ts("dve", negfb, fbias, -1.0, ALU.mult)
        ones8 = sb([8, 64]); S.memset("pool", ones8, 1.0)
        zeros8 = sb([8, 64]); S.memset("pool", zeros8, 0.0)
        onesb = sb([64, 1], BF16); S.memset("pool", onesb, 1.0)
        Cf = sb([64, 8, 128]); Cb = sb([64, 8, 128], BF16)
        nf = sb([64, 8]); nb_ = sb([64, 8], BF16)
        mfm = sb([8, 1])
        U = [sb([64, W]) for _ in range(4)]
        Vt = sb([64, W]); gates = sb([64, 16]); zz = [sb([64, W]) for _ in range(3)]
        conv = sb([64, W]); tmp = sb([64, W])
        FR = sb([8, 8, 64]); tf1 = sb([8, 64]); tf2 = sb([8, 64]); negMl = sb([8, 1]); dg = sb([8, 8])
        BD = sb([8, 8, 64]); tm = sb([64, 4, 8])
        wpre = sb([64, 8, 64]); wts = sb([64, 8, 64])
        vb = sb([64, 8, 128], BF16); qT = sb([64, 8, 64], BF16); kT = sb([64, 8, 64], BF16)
        AT = sb([64, 8, 64], BF16); ktb = sb([64, 8, 64], BF16)
        n1s = [sb([64, W]) for _ in range(2)]; dens = [sb([128, 16]) for _ in range(2)]; dq = sb([64, 16]); sR = sb([64, 8])
        sc = self.small(es); sc["sq"] = sb([64, W])
        id8 = self.ident[:8, :8]
        citems = []
        for (tok0, L, T, sidx, prow0) in self.seqs():
            for ci in range(min(L // T, self.dbg.get('maxch', 10 ** 9))):
                citems.append((tok0, T, sidx, ci))

        def loadsC(k):
            tok0_, T_, sidx_, ci_ = citems[k]
            t0_ = tok0_ + ci_ * T_
            for j in range(4):
                sh = 3 - j
                if ci_ == 0:
                    if sh > 0:
                        S.dma("sp", U[j][0:sh, :], I["mconv"][l, sidx_][j:3, :])
                    S.dma("sp", U[j][sh:T_, :], self.P[t0_:t0_ + T_ - sh, OC:OC + W])
                else:
                    S.dma("sp", U[j][:T_, :], self.P[t0_ - sh:t0_ - sh + T_, OC:OC + W])
            S.dma("sp", Vt[:T_, :], self.P[t0_:t0_ + T_, OC + W:OC + 2 * W])
            S.dma("sp", gates[:T_, :], self.P[t0_:t0_ + T_, OC + 2 * W:OC + 2 * W + 16])
            S.dma("sp", zz[k % 3][:T_, :], self.P[t0_:t0_ + T_, OZ + 2 * W:OZ + 3 * W])

        loadsC(0)
        kC = 0
        for (tok0, L, T, sidx, prow0) in self.seqs():
            S.dma("sp", Cf, I["mC"][l, sidx].re("h d v -> d h v"))
            S.cp("act", Cb, Cf)
            S.dma("sp", nf, I["mn"][l, sidx].re("h d -> d h"), _nc=True)
            S.cp("act", nb_, nf)
            S.dma("sp", mfm, I["mm"][l, sidx].re("(h o) -> h o", o=1))
            for ci in range(min(L // T, self.dbg.get('maxch', 10 ** 9))):
                t0 = tok0 + ci * T
                z = zz[kC % 3]
                n1 = n1s[kC % 2]
                kcur = kC
                if kC >= 2:
                    S.wait(("Cdone", l, kC - 2))
                S.tt("pool", conv[:T, :], U[0][:T, :], cw[0][:T, :], ALU.mult)
                for j in range(1, 4):
                    S.tt("dve" if j % 2 else "pool", tmp[:T, :], U[j][:T, :], cw[j][:T, :], ALU.mult)
                    S.tt("pool", conv[:T, :], conv[:T, :], tmp[:T, :], ALU.add)
                S.tt("dve", conv[:T, :], conv[:T, :], cb[:T, :], ALU.add)
                S.act(conv[:T, :], conv[:T, :], AF.Silu)
                S.cp("act", vb[:T].re("p h d -> p (h d)"), Vt[:T, :])
                S.tr(PS[0][:8, 0:T], gates[:T, 0:8], self.ident[:T, :T])
                S.tr(PS[0][:8, T:2 * T], gates[:T, 8:16], self.ident[:T, :T])
                li, lf, bb, gg, MM, scf, emt, wj = [FR[:, i, :T] for i in range(8)]
                S.ts("dve", li, PS[0][:8, 0:T], ibias, ALU.add)
                S.act(tf1[:, :T], PS[0][:8, T:2 * T], AF.Exp, bias=negfb, scale=-1.0)
                S.act(tf2[:, :T], tf1[:, :T], AF.Ln, bias=1.0)
                S.ts("pool", lf, tf2[:, :T], -1.0, ALU.mult)
                S.scan(bb, lf, zeros8[:, :T], 0.0, ALU.add, ALU.add)
                S.tt("dve", gg, li, bb, ALU.subtract)
                S.scan(MM, gg, gg, mfm, ALU.max, ALU.max)
                S.act(scf, MM, AF.Exp, bias=mfm, scale=-1.0)
                S.tt("dve", tf1[:, :T], bb, MM, ALU.add)
                S.act(emt, tf1[:, :T], AF.Exp, scale=-1.0)
                S.ts("dve", negMl, FR[:, 4, T - 1:T], -1.0, ALU.mult)
                S.act(wj, gg, AF.Exp, bias=negMl)
                for i, src in enumerate((gg, scf, emt, wj)):
                    S.tr(PS[1][:T, i * 8:(i + 1) * 8], src, id8)
                S.cp("dve", tm[:T].re("p a h -> p (a h)"), PS[1][:T, 0:32])
                g_tm, sc_tm, emt_tm, wj_tm = [tm[:T, i, :] for i in range(4)]
                S.tt("pool", BD[:, :, :T], MM.un(1).bc([8, 8, T]), id8.un(2).bc([8, 8, T]), ALU.mult)
                S.mm(PS[2][:T, :8 * T], ones8[:, :T], BD[:, :, :T])
                S.stt(wpre[:T, :, :T], PS[2][:T, :8 * T].re("p (h t) -> p h t", h=8), -1.0,
                      madd[:T, :T].un(1).bc([T, 8, T]), ALU.mult, ALU.add)
                for h in range(8):
                    S.act(wts[:T, h, :T], wpre[:T, h, :T], AF.Exp, bias=g_tm[:, h:h + 1])
                for h in range(8):
                    S.tr(PS[3][:64, h * T:(h + 1) * T], conv[:T, h * 64:(h + 1) * 64], self.ident[:T, :T])
                    S.tr(PS[4][:64, h * T:(h + 1) * T], conv[:T, 512 + h * 64:512 + (h + 1) * 64], self.ident[:T, :T])
                S.cp("act", qT[:, :, :T], PS[3][:64, :8 * T].re("p (h t) -> p h t", h=8))
                S.cp("dve", kT[:, :, :T], PS[4][:64, :8 * T].re("p (h t) -> p h t", h=8))
                kC += 1
                if kC < len(citems):
                    loadsC(kC)
                for h in range(8):
                    S.mm(PS[5][:T, h * T:(h + 1) * T], kT[:, h, :T], qT[:, h, :T])
                S.stt(AT[:T, :, :T], PS[5][:T, :8 * T].re("p (h t) -> p h t", h=8), 0.125, wts[:T, :, :T], ALU.mult, ALU.mult)
                for h in range(8):
                    S.mm(PS[6 + h // 4][:T, (h % 4) * 128:(h % 4 + 1) * 128], AT[:T, h, :T], vb[:T, h, :])
                    S.mm(PS[0][:T, h:h + 1], AT[:T, h, :T], onesb[:T, :])
                    S.mm(PS[0][:T, 8 + h:9 + h], qT[:, h, :T], nb_[:, h:h + 1])
                S.cp("act", n1[:T, 0:512], PS[6][:T, :])
                S.cp("act", n1[:T, 512:1024], PS[7][:T, :])
                S.cp("dve", dq[:T, :], PS[0][:T, 0:16])
                for h in range(8):
                    S.mm(PS[6 + h // 4][:T, (h % 4) * 128:(h % 4 + 1) * 128], qT[:, h, :T], Cb[:, h, :])
                den, aden = dens[kcur % 2], sc["s7"]
                S.tt("dve", den[:T, :8], sc_tm, dq[:T, 8:16], ALU.mult)
                S.tt("dve", den[:T, :8], den[:T, :8], dq[:T, 0:8], ALU.add)
                S.ts("dve", aden[:T, :8], den[:T, :8], -1.0, ALU.mult)
                S.tt("dve", aden[:T, :8], aden[:T, :8], den[:T, :8], ALU.max)
                S.tt("dve", aden[:T, :8], aden[:T, :8], emt_tm, ALU.max)
                S.recip(den[:T, :8], aden[:T, :8])
                for hh in range(2):
                    S.tt("dve", tmp[:T, hh * 512:(hh + 1) * 512].re("p (h d) -> p h d", h=4),
                         PS[6 + hh][:T, :].re("p (h d) -> p h d", h=4),
                         sc_tm[:, hh * 4:(hh + 1) * 4].un(2).bc([T, 4, 128]), ALU.mult)
                S.tt("pool", n1[:T, :], n1[:T, :], tmp[:T, :], ALU.add)
                n13 = n1[:T, :].re("p (h d) -> p h d", h=8)
                S.tt("pool", n13, n13, den[:T, :8].un(2).bc([T, 8, 128]), ALU.mult)
                S.stt(ktb[:T], conv[:T, 512:1024].re("p (h d) -> p h d", h=8), 0.125,
                      wj_tm.un(2).bc([T, 8, 64]), ALU.mult, ALU.mult)
                for h in range(8):
                    S.mm(PS[3 + h // 4][:64, (h % 4) * 128:(h % 4 + 1) * 128], ktb[:T, h, :], vb[:T, h, :])
                    S.mm(PS[5][:64, h:h + 1], ktb[:T, h, :], onesb[:T, :])
                S.ts("dve", dg, id8, FR[:, 5, T - 1:T], ALU.mult)
                S.mm(PS[5][:64, 16:24], ones8[:, :64], dg)
                S.cp("dve", sR, PS[5][:64, 16:24])
                S.tt("pool", Cf, Cf, sR.un(2).bc([64, 8, 128]), ALU.mult)
                Cf2 = Cf.re("p h d -> p (h d)")
                S.tt("dve", Cf2[:, 0:512], Cf2[:, 0:512], PS[3][:64, :], ALU.add)
                S.tt("dve", Cf2[:, 512:1024], Cf2[:, 512:1024], PS[4][:64, :], ALU.add)
                S.cp("act", Cb, Cf)
                S.tt("dve", nf, nf, sR, ALU.mult)
                S.tt("dve", nf, nf, PS[5][:64, 0:8], ALU.add)
                S.cp("act", nb_, nf)
                S.tt("dve", mfm, FR[:, 2, T - 1:T], FR[:, 4, T - 1:T], ALU.add)
                S.signal(("Cmain", l, kcur))
                with S.sub("post"):
                    S.wait(("Cmain", l, kcur))
                    self.headnorm(n1, T, 8, 128, gng, gnb, sc, on_act=True)
                    self.gate_store(n1, z, T, t0, 2)
                    S.signal(("Cdone", l, kcur))
            S.dma("sp", O["o_mC"][l, sidx].re("h d v -> d h v"), Cf)
            S.dma("sp", O["o_mn"][l, sidx].re("h d -> d h"), nf, _nc=True)
            S.dma("sp", O["o_mm"][l, sidx].re("(h o) -> h o", o=1), mfm)
            S.dma("sp", O["o_mconv"][l, sidx], self.P[tok0 + L - 3:tok0 + L, OC:OC + W])

    def branchA(self, l, es):
        S, I, O, PS = self.S, self.I, self.O, self.PS
        sb = lambda shape, dt=F32, n="a": S.sb_in(es, shape, dt, n)
        tri = sb([64, 64]); S.dma("sp", tri, I["c_tri"])
        m5 = {64: sb([64, 320]), 8: sb([64, 320])}
        S.dma("sp", m5[64], I["c_m5_64"]); S.dma("sp", m5[8], I["c_m5_8"])
        mu = self.bload(es, I["a_mu"][l:l + 1, :], 3200)
        w0 = self.bload(es, I["a_w0"][l:l + 1, :]); a0 = self.bload(es, I["a_a0"][l:l + 1, :])
        kk_ = self.bload(es, I["a_k_k"][l:l + 1, :]); ka = self.bload(es, I["a_k_a"][l:l + 1, :])
        rk_ = self.bload(es, I["a_r_k"][l:l + 1, :])
        gng = self.bload(es, I["a_ln_g"][l:l + 1, :]); gnb = self.bload(es, I["a_ln_b"][l:l + 1, :])
        WA = sb([128, W])
        S.dma("sp", WA[0:64, :], I["a_w_up"][l]); S.dma("sp", WA[64:128, :], I["a_a_up"][l])
        onesf = sb([64, 1]); S.memset("pool", onesf, 1.0)
        Hf = sb([64, 16, 64]); Hb = sb([64, 16, 64], BF16)
        Sin = sb([64, 16, 64])
        PA = sb([64, 3200]); PV = sb([64, 3200])
        L2 = sb([64, 128]); LT = sb([128, 64])
        sw = sb([64, W]); aa = sb([64, W]); kk = sb([64, W]); km = sb([64, W]); tA = sb([64, W]); tB = sb([64, W])
        eP = sb([64, W]); eN = sb([64, W]); ePm = sb([64, W])
        psm = self.small(es, 3)
        sets = []
        for _ in range(2):
            sets.append(dict(Xb=[sb([64, W], BF16) for _ in range(4)],
                             vb=sb([64, W], BF16), FM=sb([64, 16, 4, 64], BF16), PT=sb([64, 16])))
        zb3 = [dict(z=sb([64, W]), bonus=sb([64, W])) for _ in range(3)]
        Gm = [sb([64, 5, 64], BF16) for _ in range(8)]
        Xs = [[sb([64, 64], BF16)] for _ in range(8)]
        XL = [sb([64, 192], BF16) for _ in range(8)]
        ys = [sb([64, W]) for _ in range(2)]
        sc = self.small(es); sc["sq"] = sb([64, W])
        PQ = [PS[6], PS[7]]
        YB, HB = PS[0], PS[1]
        work = [PS[2], PS[3], PS[4], PS[5]]

        items = []
        for (tok0, L, T, sidx, prow0) in self.seqs():
            nch = min(L // T, self.dbg.get('maxch', 10 ** 9))
            for ci in range(nch):
                items.append((tok0, L, T, sidx, ci, ci == 0, ci == nch - 1))

        def prep(item, B, ZB):
            tok0, L, T, sidx, ci, first, last = item
            z, bonus, Xb, vb, FM, PT = ZB["z"], ZB["bonus"], B["Xb"], B["vb"], B["FM"], B["PT"]
            t0 = tok0 + ci * T
            S.dma("sp", PA[:T, :], self.P[t0:t0 + T, 0:3200])
            if ci == 0:
                S.dma("sp", PV[0:1, :], I["rshift"][l, sidx:sidx + 1, :])
                S.dma("sp", PV[1:T, :], self.P[t0:t0 + T - 1, 0:3200])
            else:
                S.dma("sp", PV[:T, :], self.P[t0 - 1:t0 - 1 + T, 0:3200])
            S.dma("sp", z[:T, :], self.P[t0:t0 + T, OZ:OZ + W])
            def shift(en, c0, c1):
                S.tt(en, PV[:T, c0:c1], PV[:T, c0:c1], PA[:T, c0:c1], ALU.subtract)
                S.tt(en, PV[:T, c0:c1], PV[:T, c0:c1], mu[:T, c0:c1], ALU.mult)
                S.tt(en, PV[:T, c0:c1], PV[:T, c0:c1], PA[:T, c0:c1], ALU.add)
            shift("dve", 3 * W, 3200)
            r, k, v = PV[:T, 0:W], PV[:T, W:2 * W], PV[:T, 2 * W:3 * W]
            S.act(L2[:T, 0:64], PV[:T, 3072:3136], AF.Tanh)
            S.cp("dve", L2[:T, 64:128], PV[:T, 3136:3200])
            S.tr(PQ[0][:, 0:T], L2[:T, :], self.ident[:T, :T])
            S.cp("act", LT[:, :T], PQ[0][:, 0:T])
            shift("pool", 0, 3 * W)
            for hh in range(2):
                S.mm(PQ[hh][:T, :], LT[0:64, :T], WA[0:64, hh * 512:(hh + 1) * 512])
            for hh in range(2):
                cs = slice(hh * 512, (hh + 1) * 512)
                S.tt("dve", sw[:T, cs], PQ[hh][:T, :], w0[:T, cs], ALU.add)
            for hh in range(2):
                S.mm(PQ[hh][:T, :], LT[64:128, :T], WA[64:128, hh * 512:(hh + 1) * 512])
            for hh in range(2):
                cs = slice(hh * 512, (hh + 1) * 512)
                S.tt("dve", aa[:T, cs], PQ[hh][:T, :], a0[:T, cs], ALU.add)
            S.act(sw[:T, :], sw[:T, :], AF.Sigmoid)
            S.act(aa[:T, :], aa[:T, :], AF.Sigmoid)
            S.tt("pool", kk[:T, :], k, kk_[:T, :], ALU.mult)
            S.act(tA[:T, :], kk[:T, :], AF.Square)
            ss, rn, bs = psm["s0"], psm["s1"], psm["s2"]
            S.red(ss[:T, :16], tA[:T, :].re("p (h d) -> p h d", h=16), ALU.add)
            S.rsqrt(rn[:T, :16], ss[:T, :16], 1e-24, ALU.max)
            kk3 = kk[:T, :].re("p (h d) -> p h d", h=16)
            S.tt("pool", kk3, kk3, rn[:T, :16].un(2).bc([T, 16, 64]), ALU.mult)
            S.stt(tA[:T, :], aa[:T, :], -1.0, ka[:T, :], ALU.add, ALU.mult)
            S.stt(km[:T, :], tA[:T, :], 1.0, k, ALU.add, ALU.mult)
            S.tt("dve", tB[:T, :], r, km[:T, :], ALU.mult)
            S.tt("dve", tB[:T, :], tB[:T, :], rk_[:T, :], ALU.mult)
            S.red(bs[:T, :16], tB[:T, :].re("p (h d) -> p h d", h=16), ALU.add)
            S.tt("pool", bonus[:T, :].re("p (h d) -> p h d", h=16), v.re("p (h d) -> p h d", h=16),
                 bs[:T, :16].un(2).bc([T, 16, 64]), ALU.mult)
            for hh in range(2):
                S.mm(PQ[hh][:T, :], tri[:T, :T], sw[:T, hh * 512:(hh + 1) * 512])
            for hh in range(2):
                cs = slice(hh * 512, (hh + 1) * 512)
                S.act(eP[:T, cs], PQ[hh][:T, :], AF.Exp, scale=C0)
                S.act(eN[:T, cs], PQ[hh][:T, :], AF.Exp, scale=-C0)
                S.tt("dve", ePm[:T, cs], PQ[hh][:T, :], sw[:T, cs], ALU.subtract)
            S.act(ePm[:T, :], ePm[:T, :], AF.Exp, scale=C0)
            S.tt("pool", tB[:T, :], kk[:T, :], aa[:T, :], ALU.mult)
            S.tt("pool", Xb[0][:T, :], tB[:T, :], eN[:T, :], ALU.mult)
            S.tt("dve", Xb[1][:T, :], km[:T, :], eN[:T, :], ALU.mult)
            S.stt(Xb[2][:T, :], kk[:T, :], -1.0, ePm[:T, :], ALU.mult, ALU.mult)
            S.tt("pool", Xb[3][:T, :], r, eP[:T, :], ALU.mult)
            S.cp("act", vb[:T, :], v)
            for h in range(16):
                S.mm(PQ[0][:64, h:h + 1], sw[:T, h * 64:(h + 1) * 64], onesf[:T, :])
            S.act(PT, PQ[0][:64, 0:16], AF.Exp, scale=C0)
            for g in range(4):
                psb = PQ[(g + 1) % 2].bits(BF16)
                for hq in range(4):
                    h = g * 4 + hq
                    for q in range(4):
                        S.tr(psb[:64, (hq * 4 + q) * 64:(hq * 4 + q) * 64 + T], Xb[q][:T, h * 64:(h + 1) * 64],
                             self.identb[:T, :T])
                S.cp("act" if g % 2 else "dve", FM[:, g * 4:(g + 1) * 4, :, :T],
                     psb[:64, :].re("p (h q t) -> p h q t", h=4, q=4)[:, :, :, :T])

        def back(item, B, y):
            tok0, L, T, sidx, ci, first, last = item
            Xb, vb, FM, PT = B["Xb"], B["vb"], B["FM"], B["PT"]
            t0 = tok0 + ci * T
            nlev = {64: 6, 8: 3}[T]
            if first:
                S.dma("sp", Sin, I["rS"][l, sidx].re("h v k -> v h k"))
                for h in range(16):
                    S.tr(PS[h // 8][:64, (h % 8) * 64:(h % 8 + 1) * 64], Sin[:, h, :], self.ident[:64, :64])
                for hh in range(2):
                    S.cp("act", Hf[:, hh * 8:(hh + 1) * 8, :].re("p h v -> p (h v)"), PS[hh][:64, :])
                S.cp("dve", Hb, Hf)

            def head_chain(h, slot):
                cs = slice(h * 64, (h + 1) * 64)
                gm = Gm[slot]
                reg = work[slot // 2]
                xo = (slot % 2) * 256
                S.mm(reg[:T, xo:xo + 2 * T], FM[:, h, 0, :T], FM[:, h, 2:4, :T])
                S.mm(reg[:T, xo + 2 * T:xo + 4 * T], FM[:, h, 1, :T], FM[:, h, 2:4, :T])
                yield
                S.tt("dve", gm[:T, 0:4, :T], reg[:T, xo:xo + 4 * T].re("p (q t) -> p q t", q=4),
                     m5[T][:T, :4 * T].re("p (q t) -> p q t", q=4), ALU.mult)
                yield
                labT, rabT, lakT, rakT = [gm[:T, q, :T] for q in range(4)]
                S.mm(reg[:T, xo:xo + T], FM[:, h, 2, :T], FM[:, h, 0, :T])
                S.mm(reg[:T, xo + 64:xo + 128], FM[:, h, 2, :T], Hb[:, h, :], start=True, stop=False)
                S.mm(reg[:T, xo + 64:xo + 128], lakT, vb[:T, cs], start=False, stop=True)
                yield
                S.tt("dve", gm[:T, 4, :T], reg[:T, xo:xo + T], m5[T][:T, 4 * T:5 * T], ALU.mult)
                X = Xs[slot][0]
                S.cp("act", X[:T, :], reg[:T, xo + 64:xo + 128])
                yield
                Lt, Ln = labT, gm[:T, 4, :T]
                for lev in range(nlev):
                    with S.atomic():
                        S.mm(reg[:T, xo:xo + 64], self.identb[:T, :T], X[:T, :], start=True, stop=False)
                        S.mm(reg[:T, xo:xo + 64], Lt, X[:T, :], start=False, stop=True)
                    if lev < nlev - 1:
                        S.mm(reg[:T, xo + 64:xo + 64 + T], Ln, Lt)
                        S.mm(reg[:T, xo + 64 + T:xo + 64 + 2 * T], Lt, Ln)
                    yield
                    xl = XL[slot]
                    wdt = 64 + 2 * T if lev < nlev - 1 else 64
                    S.cp("act", xl[:T, :wdt], reg[:T, xo:xo + wdt])
                    X = xl[:, 0:64]
                    if lev < nlev - 1:
                        Lt, Ln = xl[:T, 64:64 + T], xl[:T, 64 + T:64 + 2 * T]
                    yield
                psy = YB[:T, (h % 8) * 64:(h % 8 + 1) * 64]
                S.mm(psy, FM[:, h, 3, :T], Hb[:, h, :], start=True, stop=False)
                S.mm(psy, rabT, X[:T, :], start=False, stop=False)
                S.mm(psy, rakT, vb[:T, cs], start=False, stop=True)
                psh = HB[:64, (h % 8) * 64:(h % 8 + 1) * 64]
                S.mm(psh, Xb[0][:T, cs], X[:T, :], start=True, stop=False)
                S.mm(psh, Xb[1][:T, cs], vb[:T, cs], start=False, stop=True)
                yield

            for hh in range(2):
                lockstep([head_chain(hh * 8 + s, s) for s in range(8)])
                S.cp("act", y[:T, hh * 512:(hh + 1) * 512], YB[:T, :])
                Hh = Hf[:, hh * 8:(hh + 1) * 8, :]
                S.tt("dve", Hh.re("p h v -> p (h v)"), Hh.re("p h v -> p (h v)"), HB[:64, :], ALU.add)
                S.tt("pool", Hh, Hh, PT[:, hh * 8:(hh + 1) * 8].un(2).bc([64, 8, 64]), ALU.mult)
                if not (hh == 1 and last):
                    S.cp("act", Hb[:, hh * 8:(hh + 1) * 8, :], Hh)
            if last:
                for h in range(16):
                    S.tr(PS[h // 8][:64, (h % 8) * 64:(h % 8 + 1) * 64], Hf[:, h, :], self.ident[:64, :64])
                for hh in range(2):
                    S.cp("act", Sin[:, hh * 8:(hh + 1) * 8, :].re("p h v -> p (h v)"), PS[hh][:64, :])
                S.dma("sp", O["o_rS"][l, sidx].re("h v k -> v h k"), Sin)
                S.dma("sp", O["o_rshift"][l, sidx:sidx + 1, :], self.P[tok0 + L - 1:tok0 + L, 0:3200])

        def post(item, ZB, y):
            tok0, L, T, sidx, ci, first, last = item
            self.headnorm(y, T, 16, 64, gng, gnb, sc)
            S.tt("pool", y[:T, :], y[:T, :], ZB["bonus"][:T, :], ALU.add)
            self.gate_store(y, ZB["z"], T, tok0 + ci * T, 0)

        prep(items[0], sets[0], zb3[0])
        for k, item in enumerate(items):
            recs = []
            if k + 1 < len(items):
                recs.append(S.record(lambda: prep(items[k + 1], sets[(k + 1) % 2], zb3[(k + 1) % 3])))
            recs.append(S.record(lambda: back(item, sets[k % 2], ys[k % 2])))
            if k >= 1:
                recs.append(S.record(lambda: post(items[k - 1], zb3[(k - 1) % 3], ys[(k - 1) % 2])))
            S.play(recs)
        post(items[-1], zb3[(len(items) - 1) % 3], ys[(len(items) - 1) % 2])


    def phase3a(self, l):
        S = self.S
        with contextlib.ExitStack() as es:
            WBR = S.sb_in(es, [128, 32, D_MODEL], BF16, "WBR")
            for i in range(4):
                S.dma("pool", WBR[:, i * 8:(i + 1) * 8, :], self.I["w_branch"][l, i].re("(k p) c -> p k c", p=128))
            yz = [S.sb_in(es, [128, 4 * W], F32, "yz") for _ in range(1)]
            yzT = [S.sb_in(es, [128, 32, 128], BF16, "yzT") for _ in range(1)]
            G = S.sb_in(es, [128, 4, D_MODEL], F32, "G")
            mg = [S.sb_in(es, [128, D_MODEL], F32, "mg") for _ in range(1)]
            mgb = [S.sb_in(es, [128, D_MODEL], BF16, "mgb") for _ in range(1)]
            tmp = [S.sb_in(es, [128, 512], F32, "tmp3") for _ in range(2)]
            n = 0
            for tt in range(NTT):
                rows = slice(tt * 128, (tt + 1) * 128)
                y_ = yz[0]
                yT = yzT[0]
                S.dma("sp", y_, self.YZ[rows, :])
                S.dma("sp", G.re("p i c -> p (i c)"), self.P[rows, OG:OG + 4 * D_MODEL])
                for g in range(8):
                    ps = self.PS[g % 8]
                    for j in range(4):
                        k = g * 4 + j
                        S.tr(ps[:, j * 128:(j + 1) * 128], y_[:, k * 128:(k + 1) * 128], self.ident)
                    S.cp("act" if g % 2 else "dve", yT[:, g * 4:(g + 1) * 4, :], ps.re("p (j c) -> p j c", j=4))
                for i in range(4):
                    S.act(G[:, i, :], G[:, i, :], AF.Sigmoid)
                m_ = mg[0]
                for i in range(4):
                    for j in range(4):
                        cs = slice(j * 512, (j + 1) * 512)
                        ps = self.PS[n % 8]
                        n += 1
                        for k in range(8):
                            S.mm(ps, yT[:, i * 8 + k, :], WBR[:, i * 8 + k, cs], start=(k == 0), stop=(k == 7))
                        if i == 0:
                            S.tt("dve", m_[:, cs], ps, G[:, i, cs], ALU.mult)
                        else:
                            t_ = tmp[n % 2]
                            S.tt("dve", t_, ps, G[:, i, cs], ALU.mult)
                            S.tt("pool", m_[:, cs], m_[:, cs], t_, ALU.add)
                mb = mgb[0]
                S.cp("act", mb, m_)
                S.dma("sp", self.MG[rows, :], mb)

    def phase3b(self, l, xsrc, xdst):
        S = self.S
        with contextlib.ExitStack() as es:
            WO = S.sb_in(es, [128, 16, D_MODEL], BF16, "WO")
            S.dma("pool", WO, self.I["w_out"][l].re("(k p) c -> p k c", p=128))
            lng = S.sb_in(es, [128, D_MODEL], F32, "lng")
            lnb = S.sb_in(es, [128, D_MODEL], F32, "lnb")
            S.dma("sp", lng, self.I["ln_g"][l:l + 1, :].bc([128, D_MODEL]))
            S.dma("sp", lnb, self.I["ln_b"][l:l + 1, :].bc([128, D_MODEL]))
            mb = [S.sb_in(es, [128, D_MODEL], BF16, "mb") for _ in range(2)]
            mT = [S.sb_in(es, [128, 16, 128], BF16, "mT") for _ in range(2)]
            xr = [S.sb_in(es, [128, D_MODEL], F32, "xr") for _ in range(2)]
            pre = [S.sb_in(es, [128, D_MODEL], F32, "pre") for _ in range(2)]
            st = S.sb_in(es, [128, 4, 6], F32, "bst")
            mv = S.sb_in(es, [128, 2], F32, "bmv")
            rstd = S.sb_in(es, [128, 1], F32, "rstd")
            n = 0
            for tt in range(NTT):
                rows = slice(tt * 128, (tt + 1) * 128)
                b_ = mb[tt % 2]
                t_ = mT[tt % 2]
                x_ = xr[tt % 2]
                p_ = pre[tt % 2]
                S.dma("sp", b_, self.MG[rows, :])
                S.dma("sp", x_, xsrc[rows, :])
                for g in range(2):
                    ps = self.PS[n % 8].bits(BF16)
                    n += 1
                    for j in range(8):
                        k = g * 8 + j
                        S.tr(ps[:, j * 128:(j + 1) * 128], b_[:, k * 128:(k + 1) * 128], self.identb)
                    S.cp("act" if g % 2 else "dve", t_[:, g * 8:(g + 1) * 8, :], ps.re("p (j c) -> p j c", j=8))
                for j in range(4):
                    cs = slice(j * 512, (j + 1) * 512)
                    ps = self.PS[n % 8]
                    n += 1
                    for k in range(16):
                        S.mm(ps, t_[:, k, :], WO[:, k, cs], start=(k == 0), stop=(k == 15))
                    S.stt(p_[:, cs], x_[:, cs], ALPHA, ps, ALU.mult, ALU.add)
                    S.op("dve", lambda e, j=j, p_=p_, cs=cs: e.bn_stats(out=st.ap[:, j, :], in_=p_.ap[:, cs]), [p_], [st])
                S.op("dve", lambda e: e.bn_aggr(out=mv.ap, in_=st.ap), [st], [mv])
                S.rsqrt(rstd, mv[:, 1:2], LN_EPS, ALU.add)
                S.ts("dve", p_, p_, mv[:, 0:1], ALU.subtract, rstd, ALU.mult)
                S.tt("pool", p_, p_, lng, ALU.mult)
                S.tt("pool", p_, p_, lnb, ALU.add)
                S.dma("sp", xdst[rows, :], p_)


WSHAPES = {
    "w_in": [DEPTH, D_MODEL, IN_COLS], "a_mu": [DEPTH, 3200], "a_w0": [DEPTH, W], "a_w_up": [DEPTH, 64, W],
    "a_a0": [DEPTH, W], "a_a_up": [DEPTH, 64, W], "a_k_k": [DEPTH, W], "a_k_a": [DEPTH, W], "a_r_k": [DEPTH, W],
    "a_ln_g": [DEPTH, W], "a_ln_b": [DEPTH, W], "b_sinks": [DEPTH, 16], "c_conv_w": [DEPTH, 4, W],
    "c_conv_b": [DEPTH, W], "c_i_bias": [DEPTH, 8], "c_f_bias": [DEPTH, 8], "c_ln_g": [DEPTH, W],
    "c_ln_b": [DEPTH, W], "d_ln_g": [DEPTH, W], "d_ln_b": [DEPTH, W], "w_branch": [DEPTH, 4, W, D_MODEL],
    "w_out": [DEPTH, D_MODEL, D_MODEL], "ln_g": [DEPTH, D_MODEL], "ln_b": [DEPTH, D_MODEL],
}
CONST_SHAPES = {
    "c_ident": [128, 128], "c_tri": [64, 64], "c_madd": [64, 64],
    "c_rbc": [NPOSROW, 64], "c_rbs": [NPOSROW, 64], "c_rdc": [NPOSROW, 128], "c_rds": [NPOSROW, 128],
    "c_dmt": [64, 8, 64], "c_qdec": [128, 8, 64], "c_kdec64": [64, 1024], "c_kdec8": [64, 1024],
    "c_cdec64": [128, 1024], "c_cdec8": [128, 1024], "c_mp": [128, 256], "c_mp0": [128, 256], "c_ms": [128, 256],
    "c_m5_64": [64, 320], "c_m5_8": [64, 320],
}


def make_consts():
    c = {}
    f32 = np.float32
    c["c_ident"] = np.eye(128, dtype=f32)
    s = np.arange(64)[:, None]
    t = np.arange(64)[None, :]
    c["c_tri"] = (s <= t).astype(f32)
    c["c_madd"] = np.where(s <= t, 0.0, -1e30).astype(f32)
    pos = np.concatenate([np.arange(LP), 8192 + np.arange(LS)]).astype(f32)

    def rope_tab(d):
        inv = (f32(10000.0) ** (-np.arange(0, d, 2, dtype=f32) / f32(d))).astype(f32)
        ang = (pos[:, None] * inv[None, :]).astype(f32)
        cs = np.cos(ang).astype(f32)
        sn = np.sin(ang).astype(f32)
        ct = np.stack([cs, cs], 1)
        st = np.stack([-sn, sn], 1)
        return ct.reshape(NPOSROW, -1), st.reshape(NPOSROW, -1)
    c["c_rbc"], c["c_rbs"] = rope_tab(64)
    c["c_rdc"], c["c_rds"] = rope_tab(128)
    ksc = float(f32(128.0) ** f32(-0.5))
    lg = np.log1p(-np.exp2(-5.0 - np.arange(8, dtype=np.float64)))
    rel = (t - s).astype(np.float64)
    dmt = np.zeros((64, 8, 64), f32)
    for h in range(8):
        dmt[:, h, :] = np.where(rel >= 0, np.exp(np.maximum(rel, 0.0) * lg[h]), 0.0) * ksc
    c["c_dmt"] = dmt
    idx = np.arange(64, dtype=np.float64)
    qd = np.exp((idx + 1.0)[None, :] * lg[:, None])
    c["c_qdec"] = np.broadcast_to(qd[None], (128, 8, 64)).astype(f32).copy()
    for Lc in (64, 8):
        kd = np.zeros((64, 8, 128), f32)
        for h in range(8):
            kd[:Lc, h, :] = (np.exp((Lc - 1.0 - idx[:Lc]) * lg[h]) * ksc)[:, None]
        c["c_kdec%d" % Lc] = kd.reshape(64, 1024)
        cd = np.zeros((128, 8, 128), f32)
        for h in range(8):
            cd[:, h, :] = np.exp(Lc * lg[h])
        c["c_cdec%d" % Lc] = cd.reshape(128, 1024)
    a = np.arange(128)[:, None]
    cc = np.arange(256)[None, :]
    ok = (cc >= a) & (cc <= 128 + a)
    c["c_mp"] = np.where(ok, 0.0, -1e30).astype(f32)
    c["c_mp0"] = np.where(ok & (cc >= 128), 0.0, -1e30).astype(f32)
    c["c_ms"] = np.where(ok & (cc < 136) & (a < 8), 0.0, -1e30).astype(f32)
    for T in (64, 8):
        ss = np.arange(64)[:, None]
        tt_ = np.arange(T)[None, :]
        su = (ss < tt_).astype(f32)
        iu = (ss <= tt_).astype(f32)
        sl = (ss > tt_).astype(f32)
        m5 = np.zeros((64, 320), f32)
        m5[:, :5 * T] = np.concatenate([su, iu, su, iu, sl], 1)
        m5[T:, :] = 0.0
        c["c_m5_%d" % T] = m5
    return c


_PROG = {}


def get_prog(dbg=None):
    key = repr(sorted((dbg or {}).items()))
    if key not in _PROG:
        _PROG[key] = Prog(dbg)
    return _PROG[key]


def make_in_maps(inp):
    f = lambda a: np.ascontiguousarray(np.asarray(a, dtype=np.float32))
    consts = make_consts()
    shared = {n: f(inp[n]).reshape(WSHAPES[n]) for n in WSHAPES}
    st_names = [("rS", "state_rwkv_S"), ("rshift", "state_rwkv_shift"), ("ck", "cache_swa_k"), ("cv", "cache_swa_v"),
                ("mC", "state_mlstm_C"), ("mn", "state_mlstm_n"), ("mm", "state_mlstm_m"),
                ("mconv", "state_mlstm_conv"), ("dS", "state_ret_S")]
    xp = f(inp["x_prompt"])
    xs = f(inp["x_sample"])
    maps = []
    for c in range(NCORES):
        m = dict(shared)
        m.update(consts)
        m["xin"] = np.concatenate([xp[c % 4], xs[c * NSS:(c + 1) * NSS].reshape(NSS * LS, D_MODEL)], 0)
        for kn, full in st_names:
            a = f(inp[full])[:, c * NSS:(c + 1) * NSS]
            z = np.zeros((DEPTH, 1) + a.shape[2:], np.float32)
            a = np.concatenate([z, a], 1)
            if kn in ("ck", "cv"):
                a = a.reshape(DEPTH, NSEQ, 128, 128)
            m[kn] = np.ascontiguousarray(a)
        maps.append(m)
    return maps


def assemble(results):
    y = [r["y"] for r in results]
    y_prompt = np.stack([y[c][:LP] for c in range(4)], 0)
    y_sample = np.concatenate([y[c][LP:].reshape(NSS, LS, D_MODEL) for c in range(NCORES)], 0)
    outs = [y_prompt, y_sample]
    shp = {"o_rS": (16, 64, 64), "o_rshift": (3200,), "o_ck": (128, 2, 64), "o_cv": (128, 2, 64),
           "o_mC": (8, 64, 128), "o_mn": (8, 64), "o_mm": (8,), "o_mconv": (3, 1024), "o_dS": (8, 128, 128)}
    for n in ["o_rS", "o_rshift", "o_ck", "o_cv", "o_mC", "o_mn", "o_mm", "o_mconv", "o_dS"]:
        p = np.stack([results[c][n][:, 0] for c in range(4)], 1).reshape((DEPTH, 4) + shp[n])
        s = np.concatenate([results[c][n][:, 1:] for c in range(NCORES)], 1).reshape((DEPTH, NCORES * NSS) + shp[n])
        outs += [np.ascontiguousarray(p, dtype=np.float32), np.ascontiguousarray(s, dtype=np.float32)]
    return tuple(outs)


def kernel(**inputs):
    prog = get_prog()
    maps = make_in_maps(inputs)
    res = run_bass_kernel_spmd(prog.nc, maps, core_ids=list(range(NCORES)))
    return assemble(res.results)
```

```python
import contextlib
import numpy as np
import ml_dtypes
import concourse.bass as bass
import concourse.mybir as mybir
from concourse.bass_utils import run_bass_kernel_spmd

F32 = mybir.dt.float32
BF16 = mybir.dt.bfloat16
AF = mybir.ActivationFunctionType
ALU = mybir.AluOpType
AX = mybir.AxisListType

D_MODEL = 2048
DEPTH = 2
NCORES = 8
LP = 2048
NSS = 16
LS = 8
NTOK = LP + NSS * LS
NTT = NTOK // 128
NSEQ = 1 + NSS
W = 1024
IN_COLS = 21904
OA, OB, OC, OD, OZ, OG = 0, 3200, 4480, 6544, 9616, 13712
ALPHA = (2 * DEPTH) ** 0.25
LN_EPS = 1e-5
C0 = -float(np.exp(-0.5))
NPOSROW = LP + LS


class Res:
    __slots__ = ("w", "r", "ps")

    def __init__(self, ps=False):
        self.w = None
        self.r = {}
        self.ps = ps


class V:
    __slots__ = ("res", "ap")

    def __init__(self, res, ap):
        self.res = res
        self.ap = ap

    def __getitem__(self, idx):
        return V(self.res, self.ap[idx])

    def un(self, axis):
        return V(self.res, self.ap.unsqueeze(axis))

    def bc(self, shape):
        return V(self.res, self.ap.broadcast_to(list(shape)))

    def re(self, pat, **kw):
        return V(self.res, self.ap.rearrange(pat, **kw))

    def bits(self, dt):
        return V(self.res, self.ap.bitcast(dt))


class Stream(list):
    pos = 0


class Sched:
    NDMA = 40

    def __init__(self, nc, es):
        self.nc = nc
        self.es = es
        self.eng = {"pe": nc.tensor, "act": nc.scalar, "dve": nc.vector, "pool": nc.gpsimd, "sp": nc.sync}
        self.sems = []
        self.key = {}
        for n in self.eng:
            self.key[n] = len(self.sems)
            self.sems.append(es.enter_context(nc.semaphore("e_" + n)))
        self.dkeys = []
        for i in range(self.NDMA):
            self.dkeys.append(len(self.sems))
            self.sems.append(es.enter_context(nc.semaphore("d%d" % i)))
        self.val = [0] * len(self.sems)
        self.known = {n: {} for n in self.eng}
        self.dnext = 0
        self.uid = 0
        self.ninst = 0
        self.limit = 10 ** 9
        self.rec = None
        self.bundle = None
        self.fresh_swdge = False
        self._signaled = set()
        self.mhalf = None

    def sb(self, shape, dt=F32, name=None):
        self.uid += 1
        t = self.es.enter_context(self.nc.sbuf_tensor("%s_%d" % (name or "t", self.uid), list(shape), dt))
        return V(Res(), t[tuple(slice(None) for _ in shape)])

    def sb_in(self, es, shape, dt=F32, name=None):
        self.uid += 1
        t = es.enter_context(self.nc.sbuf_tensor("%s_%d" % (name or "t", self.uid), list(shape), dt))
        return V(Res(), t[tuple(slice(None) for _ in shape)])

    def _waits(self, en, reads, writes):
        deps = {}

        def add(k, v):
            if deps.get(k, 0) < v:
                deps[k] = v
        for r in reads:
            if r is not None and r.w is not None:
                add(*r.w)
        for w in writes:
            if w is None:
                continue
            if w.w is not None:
                add(*w.w)
            for k, v in w.r.items():
                add(k, v)
        e = self.eng[en]
        kn = self.known[en]
        for k, v in deps.items():
            if en == "pe" and k == self.key["pe"]:
                continue
            if kn.get(k, 0) < v:
                e.wait_ge(self.sems[k], v)
                kn[k] = v

    def _mark(self, ev, reads, writes):
        for r in reads:
            if r is not None and r.r.get(ev[0], 0) < ev[1]:
                r.r[ev[0]] = ev[1]
        for w in writes:
            if w is not None:
                w.w = ev
                w.r = {}

    def op(self, en, fn, reads, writes):
        if self.rec is not None:
            (self.bundle if self.bundle is not None else self.rec).append((0, (en, fn, reads, writes), None))
            return
        if self.ninst >= self.limit:
            return
        writes = [x.res for x in writes if x is not None] + [x.res for x in reads if x is not None and x.res.ps]
        reads = [x.res for x in reads if x is not None and not x.res.ps]
        self._waits(en, reads, writes)
        inst = fn(self.eng[en])
        k = self.key[en]
        self.val[k] += 1
        inst.then_inc(self.sems[k], 1)
        self.ninst += 1
        self._mark((k, self.val[k]), reads, writes)

    def dma(self, q, out, in_, **kw):
        if self.rec is not None:
            (self.bundle if self.bundle is not None else self.rec).append((1, (q, out, in_), kw))
            return
        if self.ninst >= self.limit:
            return
        if kw.pop("_nc", False):
            with self.nc.allow_non_contiguous_dma(reason="tiny strided state transfer"):
                return self.dma(q, out, in_, **kw)
        reads = [in_.res]
        writes = [out.res]
        if q == "pool" and self.fresh_swdge:
            k = len(self.sems)
            self.sems.append(self.es.enter_context(self.nc.semaphore("w%d" % k)))
            self.val.append(0)
        else:
            j = self.dnext
            self.dnext = (self.dnext + 1) % self.NDMA
            k = self.dkeys[j]
        if self.known[q].get(k, 0) < self.val[k]:
            self.eng[q].wait_ge(self.sems[k], self.val[k])
            self.known[q][k] = self.val[k]
        self._waits(q, reads, writes)
        inst = self.eng[q].dma_start(out=out.ap, in_=in_.ap, **kw)
        self.val[k] += 16
        inst.then_inc(self.sems[k], 16)
        self.ninst += 1
        self._mark((k, self.val[k]), reads, writes)

    def _est(self, kind, args):
        if kind == 2:
            en0, dur, rd, wr = None, 0.0, [], []
            for k2, a2, _ in args:
                e, d, r, w = self._est(k2, a2)
                en0 = en0 or e
                dur += d
                rd += r
                wr += w
            return en0, dur, rd, wr
        if kind == 1:
            q, out, in_ = args
            n = 1
            for d in out.ap.shape:
                n *= d
            return q, 2000.0 + n * 4 / 150.0, [in_.res], [out.res]
        en, fn, reads, writes = args
        n = 64
        if writes:
            n = 1
            for d in writes[0].ap.shape[1:]:
                n *= d
        dur = {"pe": 35 + n / 2.4, "act": 200 + n / 1.2, "dve": 80 + n / 0.96, "pool": 150 + n * 2.2, "sp": 50}[en]
        return en, dur, [x.res for x in reads if x is not None], [x.res for x in writes if x is not None]

    @contextlib.contextmanager
    def atomic(self):
        if self.rec is None or self.bundle is not None:
            yield
            return
        self.bundle = []
        try:
            yield
        finally:
            b, self.bundle = self.bundle, None
            if b:
                self.rec.append((2, b, None))

    @contextlib.contextmanager
    def sub(self, name):
        if self.rec is None:
            yield
            return
        main = self.rec
        if not hasattr(main, "subs"):
            main.subs = {}
        self.rec = main.subs.setdefault(name, Stream())
        try:
            yield
        finally:
            self.rec = main

    def signal(self, tok):
        if self.rec is not None:
            self.rec.append((4, tok, None))

    def wait(self, tok):
        if self.rec is not None:
            self.rec.append((3, tok, None))

    def record(self, fn):
        old = self.rec
        self.rec = Stream()
        fn()
        out = self.rec
        self.rec = old
        return out

    def play(self, streams):
        if not hasattr(self, "_tfree"):
            self._tfree = {n: 0.0 for n in self.eng}
            self._wready = {}
            self._rdone = {}
        tfree, wready, rdone = self._tfree, self._wready, self._rdone
        streams = [s if isinstance(s, Stream) else Stream(s) for s in streams]
        for s in list(streams):
            streams += list(getattr(s, "subs", {}).values())
        sig = self._signaled
        ests = [None] * len(streams)
        while True:
            best = None
            bstart = None
            for i, s in enumerate(streams):
                while s.pos < len(s) and s[s.pos][0] in (3, 4):
                    if s[s.pos][0] == 4:
                        sig.add(s[s.pos][1])
                    elif s[s.pos][1] not in sig:
                        break
                    s.pos += 1
                if s.pos >= len(s) or s[s.pos][0] == 3:
                    continue
                if ests[i] is None:
                    kind, args, kw = s[s.pos]
                    en, dur, rd, wr = self._est(kind, args)
                    t = tfree[en]
                    for r in rd:
                        if r is None:
                            continue
                        t = max(t, wready.get(id(r), 0.0))
                        if r.ps:
                            t = max(t, rdone.get(id(r), 0.0))
                    for w in wr:
                        if w is None:
                            continue
                        t = max(t, wready.get(id(w), 0.0), rdone.get(id(w), 0.0))
                    ests[i] = (t, en, dur, rd, wr)
                if bstart is None or ests[i][0] < bstart:
                    best, bstart = i, ests[i][0]
            if best is None:
                if any(s.pos < len(s) for s in streams):
                    if any(s.pos < len(s) and (s[s.pos][0] == 4 or (s[s.pos][0] == 3 and s[s.pos][1] in sig))
                           for s in streams):
                        continue
                    raise RuntimeError("stream deadlock: every stream waits on an unsignalled token")
                break
            t, en, dur, rd, wr = ests[best]
            kind, args, kw = streams[best][streams[best].pos]
            streams[best].pos += 1
            if kind == 2:
                for k2, a2, kw2 in args:
                    if k2 == 0:
                        self.op(*a2)
                    else:
                        self.dma(*a2, **kw2)
                tfree[en] = t + dur
                tend = t + dur + 60.0
            elif kind == 0:
                self.op(*args)
                tfree[en] = t + dur
                tend = t + dur + 60.0
            else:
                self.dma(*args, **kw)
                tfree[en] = t + 60.0
                tend = t + dur
            for r in rd:
                if r is not None:
                    rdone[id(r)] = max(rdone.get(id(r), 0.0), tend)
            for w in wr:
                if w is not None:
                    wready[id(w)] = tend
            ests = [None] * len(streams)

    def barrier(self):
        for en, e in self.eng.items():
            kn = self.known[en]
            for k, v in enumerate(self.val):
                if v > 0 and kn.get(k, 0) < v:
                    e.wait_ge(self.sems[k], v)
                    kn[k] = v

    def finish(self):
        e = self.eng["sp"]
        kn = self.known["sp"]
        for k, v in enumerate(self.val):
            if v > 0 and kn.get(k, 0) < v:
                e.wait_ge(self.sems[k], v)
                kn[k] = v

    def mm(self, out, lhsT, rhs, start=True, stop=True):
        self.op("pe", lambda e: e.matmul(out.ap, lhsT=lhsT.ap, rhs=rhs.ap, start=start, stop=stop),
                [lhsT, rhs], [out])

    def tr(self, out, in_, ident):
        self.op("pe", lambda e: e.transpose(out.ap, in_.ap, ident.ap), [in_, ident], [out])

    def cp(self, en, out, in_):
        if en == "act":
            self.op(en, lambda e: e.copy(out=out.ap, in_=in_.ap), [in_], [out])
        else:
            self.op(en, lambda e: e.tensor_copy(out=out.ap, in_=in_.ap), [in_], [out])

    def tt(self, en, out, a, b, op):
        self.op(en, lambda e: e.tensor_tensor(out=out.ap, in0=a.ap, in1=b.ap, op=op), [a, b], [out])

    def ts(self, en, out, a, s1, op0, s2=None, op1=None):
        rd = [a]
        s1a = s1
        s2a = s2
        if isinstance(s1, V):
            rd.append(s1)
            s1a = s1.ap
        if isinstance(s2, V):
            rd.append(s2)
            s2a = s2.ap
        if op1 is None:
            self.op(en, lambda e: e.tensor_scalar(out=out.ap, in0=a.ap, scalar1=s1a, scalar2=None, op0=op0), rd, [out])
        else:
            self.op(en, lambda e: e.tensor_scalar(out=out.ap, in0=a.ap, scalar1=s1a, scalar2=s2a, op0=op0, op1=op1),
                    rd, [out])

    def stt(self, out, a, s, b, op0, op1):
        rd = [a, b]
        sa = s
        if isinstance(s, V):
            rd.append(s)
            sa = s.ap
        self.op("dve", lambda e: e.scalar_tensor_tensor(out=out.ap, in0=a.ap, scalar=sa, in1=b.ap, op0=op0, op1=op1),
                rd, [out])

    def act(self, out, in_, func, bias=None, scale=None, accum=None):
        rd = [in_]
        kw = {}
        if bias is not None:
            if isinstance(bias, V):
                rd.append(bias)
                kw["bias"] = bias.ap
            else:
                kw["bias"] = float(bias)
        if scale is not None:
            if isinstance(scale, V):
                rd.append(scale)
                kw["scale"] = scale.ap
            else:
                kw["scale"] = float(scale)
        wr = [out]
        if accum is not None:
            kw["accum_out"] = accum.ap
            wr.append(accum)
        self.op("act", lambda e: e.activation(out=out.ap, in_=in_.ap, func=func, **kw), rd, wr)

    def red(self, out, in_, op):
        self.op("dve", lambda e: e.tensor_reduce(out=out.ap, in_=in_.ap, axis=AX.X, op=op), [in_], [out])

    def memset(self, en, out, val):
        self.op(en, lambda e: e.memset(out.ap, val), [], [out])

    def scan(self, out, d0, d1, init, op0, op1):
        rd = [d0, d1]
        ia = init
        if isinstance(init, V):
            rd.append(init)
            ia = init.ap
        self.op("dve", lambda e: e.tensor_tensor_scan(out=out.ap, data0=d0.ap, data1=d1.ap, initial=ia, op0=op0, op1=op1),
                rd, [out])

    def recip(self, out, in_):
        self.op("dve", lambda e: e.reciprocal(out=out.ap, in_=in_.ap), [in_], [out])

    def rsqrt(self, out, in_, c, op):
        self.ts("dve", out, in_, c, op)
        if self.mhalf is None:
            self.act(out, out, AF.Sqrt)
            self.recip(out, out)
        else:
            shp = list(out.ap.shape)
            self.tt("pool", out, out, self.mhalf[:shp[0], :shp[1]], ALU.pow)


def lockstep(gens):
    gens = list(gens)
    while gens:
        nxt = []
        for g in gens:
            try:
                next(g)
                nxt.append(g)
            except StopIteration:
                pass
        gens = nxt


def dram(ap):
    return V(None, ap)


class Prog:
    def __init__(self, dbg=None):
        self.dbg = dbg or {}
        nc = bass.Bass("TRN2", target_bir_lowering=False)
        self.nc = nc
        self.es = contextlib.ExitStack()
        self.S = Sched(nc, self.es)
        self.S.limit = self.dbg.get('limit', 10 ** 9)
        self.S.fresh_swdge = bool(self.dbg.get('fresh_swdge'))
        self.declare_io()
        self.build()
        self.S.finish()
        self.es.close()

    def din(self, name, shape, dt=F32):
        return dram(self.nc.dram_tensor(name, list(shape), dt, kind="ExternalInput").ap())

    def dout(self, name, shape, dt=F32):
        return dram(self.nc.dram_tensor(name, list(shape), dt, kind="ExternalOutput").ap())

    def dscr(self, name, shape, dt=F32):
        kind = "ExternalOutput" if name in self.dbg.get("expose", ()) else "Internal"
        return dram(self.nc.dram_tensor(name, list(shape), dt, kind=kind).ap())

    def declare_io(self):
        I = {}
        I["xin"] = self.din("xin", [NTOK, D_MODEL])
        I["rS"] = self.din("rS", [DEPTH, NSEQ, 16, 64, 64])
        I["rshift"] = self.din("rshift", [DEPTH, NSEQ, 3200])
        I["ck"] = self.din("ck", [DEPTH, NSEQ, 128, 128])
        I["cv"] = self.din("cv", [DEPTH, NSEQ, 128, 128])
        I["mC"] = self.din("mC", [DEPTH, NSEQ, 8, 64, 128])
        I["mn"] = self.din("mn", [DEPTH, NSEQ, 8, 64])
        I["mm"] = self.din("mm", [DEPTH, NSEQ, 8])
        I["mconv"] = self.din("mconv", [DEPTH, NSEQ, 3, 1024])
        I["dS"] = self.din("dS", [DEPTH, NSEQ, 8, 128, 128])
        for n, s in WSHAPES.items():
            if self.dbg.get("tinyw") and n in ("w_in", "w_branch", "w_out"):
                s = [DEPTH, 128, 128]
            I[n] = self.din(n, s)
        for n, s in CONST_SHAPES.items():
            I[n] = self.din(n, s)
        self.I = I
        O = {}
        O["y"] = self.dout("y", [NTOK, D_MODEL])
        O["o_rS"] = self.dout("o_rS", [DEPTH, NSEQ, 16, 64, 64])
        O["o_rshift"] = self.dout("o_rshift", [DEPTH, NSEQ, 3200])
        O["o_ck"] = self.dout("o_ck", [DEPTH, NSEQ, 128, 128])
        O["o_cv"] = self.dout("o_cv", [DEPTH, NSEQ, 128, 128])
        O["o_mC"] = self.dout("o_mC", [DEPTH, NSEQ, 8, 64, 128])
        O["o_mn"] = self.dout("o_mn", [DEPTH, NSEQ, 8, 64])
        O["o_mm"] = self.dout("o_mm", [DEPTH, NSEQ, 8])
        O["o_mconv"] = self.dout("o_mconv", [DEPTH, NSEQ, 3, 1024])
        O["o_dS"] = self.dout("o_dS", [DEPTH, NSEQ, 8, 128, 128])
        self.O = O
        self.P = self.dscr("P", [NTOK, IN_COLS])
        self.YZ = self.dscr("YZ", [NTOK, 4 * W])
        self.MG = self.dscr("MG", [NTOK, D_MODEL], BF16)
        self.X1 = self.dscr("X1", [NTOK, D_MODEL])

    def build(self):
        S = self.S
        es = self.es
        self.PS = []
        for i in range(8):
            t = es.enter_context(self.nc.psum_tensor("ps%d" % i, [128, 512], F32))
            self.PS.append(V(Res(ps=True), t[:, :]))
        self.ident = S.sb([128, 128], F32, "ident")
        S.dma("sp", self.ident, self.I["c_ident"])
        self.identb = S.sb([128, 128], BF16, "identb")
        S.cp("dve", self.identb, self.ident)
        if not self.dbg.get("nopow"):
            mh = S.sb([128, 16], F32, "mhalf")
            S.memset("pool", mh, -0.5)
            S.mhalf = mh
        nl = self.dbg.get("layers", DEPTH)
        for l in range(nl):
            xsrc = self.I["xin"] if l == 0 else self.X1
            xdst = self.O["y"] if l == nl - 1 else self.X1
            self.b_done = False
            if self.dbg.get("p1", True):
                self.phase1(l, xsrc)
            S.barrier()
            if self.dbg.get("p2", True):
                self.phase2(l)
            S.barrier()
            if self.dbg.get("p3", True):
                self.phase3a(l)
                S.barrier()
                self.phase3b(l, xsrc, xdst)
            S.barrier()

    def phase1_setup(self, l, xsrc, es):
        S = self.S
        XT = S.sb_in(es, [128, 16, NTOK], BF16, "XT")
        with contextlib.ExitStack() as es2:
            xrow = [S.sb_in(es2, [128, D_MODEL], F32, "xrow") for _ in range(2)]
            for tt in range(NTT):
                xr = xrow[tt % 2]
                S.dma("sp", xr, xsrc[tt * 128:(tt + 1) * 128, :])
                for g in range(4):
                    ps = self.PS[(tt * 4 + g) % 8]
                    for j in range(4):
                        k = g * 4 + j
                        S.tr(ps[:, j * 128:(j + 1) * 128], xr[:, k * 128:(k + 1) * 128], self.ident)
                    S.cp("act" if g % 2 else "dve", XT[:, g * 4:(g + 1) * 4, tt * 128:(tt + 1) * 128],
                         ps.re("p (j c) -> p j c", j=4))
            S.barrier()
        WB = [S.sb_in(es, [128, 16, 512], BF16, "WB") for _ in range(3)]
        stg = [S.sb_in(es, [128, 512], F32, "stg") for _ in range(4)]
        return dict(XT=XT, WB=WB, stg=stg, n=0, c=0)

    @staticmethod
    def col_tiles(a, b):
        n = -(-(b - a) // 512)
        w = -(-(b - a) // n)
        w = -(-w // 8) * 8
        out = []
        while a < b:
            out.append((a, min(w, b - a)))
            a += w
        return out

    def phase1_cols(self, ctx, l, tiles, banks):
        S = self.S
        XT, WB, stg = ctx["XT"], ctx["WB"], ctx["stg"]
        wv = self.I["w_in"][l].re("(k p) c -> p k c", p=128)
        for (c0, cw) in tiles:
            wb = WB[ctx["c"] % 3]
            ctx["c"] += 1
            S.dma("pool", wb[:, :, :cw], wv[:, :, c0:c0 + cw])
            for tt in range(NTT):
                n = ctx["n"]
                ctx["n"] += 1
                ps = banks[n % len(banks)]
                with S.atomic():
                    for k in range(16):
                        S.mm(ps[:, :cw], XT[:, k, tt * 128:(tt + 1) * 128], wb[:, k, :cw], start=(k == 0), stop=(k == 15))
                sg = stg[n % 4]
                S.cp("act" if n % 2 else "dve", sg[:, :cw], ps[:, :cw])
                S.dma("sp", self.P[tt * 128:(tt + 1) * 128, c0:c0 + cw], sg[:, :cw])

    def phase1(self, l, xsrc):
        S = self.S
        first = self.col_tiles(OB, OC) + self.col_tiles(OZ + W, OZ + 2 * W)
        rest = self.col_tiles(0, OB) + self.col_tiles(OC, OZ + W) + self.col_tiles(OZ + 2 * W, IN_COLS)
        if self.dbg.get("ncol") is not None:
            rest = rest[:self.dbg["ncol"]]
        with contextlib.ExitStack() as es:
            ctx = self.phase1_setup(l, xsrc, es)
            self.phase1_cols(ctx, l, first, self.PS)
            S.barrier()
            if not self.dbg.get("p2", True) or "B" not in self.dbg.get("branches", "ABCD") or self.dbg.get("nooverlapB"):
                self.phase1_cols(ctx, l, rest, self.PS)
                self.b_done = False
                return
            with contextlib.ExitStack() as esB:
                sB = S.record(lambda: self.branchB(l, esB))
                sP = S.record(lambda: self.phase1_cols(ctx, l, rest, [self.PS[6], self.PS[7]]))
                S.play([sP, sB])
            self.b_done = True

    def seqs(self):
        out = [(0, LP, 64, 0, 0)]
        for j in range(NSS):
            out.append((LP + j * LS, LS, LS, j + 1, LP))
        ns = self.dbg.get("nseq", len(out))
        return out[:ns]

    def bload(self, es, src_row, n=W, rows=64):
        t = self.S.sb_in(es, [rows, n], F32, "bc")
        self.S.dma("sp", t, src_row.bc([rows, n]))
        return t

    def headnorm(self, x, T, H, dv, gt, bt, sc, on_act=False):
        S = self.S
        x3 = x[:T, :].re("p (h d) -> p h d", h=H)
        sq, ssum, ssq, mean, msq, var, rstd = sc["sq"], sc["s0"], sc["s1"], sc["s2"], sc["s3"], sc["s4"], sc["s5"]
        S.red(ssum[:T, :H], x3, ALU.add)
        S.act(sq[:T, :], x[:T, :], AF.Square)
        S.red(ssq[:T, :H], sq[:T, :].re("p (h d) -> p h d", h=H), ALU.add)
        S.ts("dve", mean[:T, :H], ssum[:T, :H], 1.0 / dv, ALU.mult)
        S.tt("dve", msq[:T, :H], mean[:T, :H], mean[:T, :H], ALU.mult)
        S.stt(var[:T, :H], ssq[:T, :H], 1.0 / dv, msq[:T, :H], ALU.mult, ALU.subtract)
        S.rsqrt(rstd[:T, :H], var[:T, :H], LN_EPS, ALU.add)
        if on_act:
            S.stt(msq[:T, :H], mean[:T, :H], -1.0, rstd[:T, :H], ALU.mult, ALU.mult)
            for h in range(H):
                S.act(x3[:, h, :], x3[:, h, :], AF.Identity, bias=msq[:T, h:h + 1], scale=rstd[:T, h:h + 1])
        else:
            S.tt("pool", x3, x3, mean[:T, :H].un(2).bc([T, H, dv]), ALU.subtract)
            S.tt("pool", x3, x3, rstd[:T, :H].un(2).bc([T, H, dv]), ALU.mult)
        S.tt("dve", x[:T, :], x[:T, :], gt[:T, :], ALU.mult)
        S.tt("pool", x[:T, :], x[:T, :], bt[:T, :], ALU.add)

    def gate_store(self, x, z, T, t0, bi):
        S = self.S
        S.act(z[:T, :], z[:T, :], AF.Silu)
        S.tt("dve", x[:T, :], x[:T, :], z[:T, :], ALU.mult)
        S.dma("sp", self.YZ[t0:t0 + T, bi * W:(bi + 1) * W], x[:T, :])

    def rope(self, out, x, cos, sin, T, nh, hd, t1, t2):
        S = self.S
        n = nh * hd
        t1 = out
        x4 = x[:T, :n].re("p (h two d) -> p h two d", h=nh, two=2)
        c4 = cos[:T, :hd].re("p (two d) -> p two d", two=2).un(1).bc([T, nh, 2, hd // 2])
        s4 = sin[:T, :hd].re("p (two d) -> p two d", two=2).un(1).bc([T, nh, 2, hd // 2])
        S.tt("pool", t1[:T, :n].re("p (h two d) -> p h two d", h=nh, two=2), x4, c4, ALU.mult)
        o4 = t2[:T, :n].re("p (h two d) -> p h two d", h=nh, two=2)
        S.tt("dve", o4[:, :, 0, :], x4[:, :, 1, :], s4[:, :, 0, :], ALU.mult)
        S.tt("dve", o4[:, :, 1, :], x4[:, :, 0, :], s4[:, :, 1, :], ALU.mult)
        S.tt("pool", out[:T, :n], t1[:T, :n], t2[:T, :n], ALU.add)

    def small(self, es, n=8, cols=16):
        return {"s%d" % i: self.S.sb_in(es, [128, cols], F32, "sm") for i in range(n)}

    def phase2(self, l):
        S = self.S
        br = self.dbg.get("branches", "ABCD")
        P_ = self.PS
        self.psmap = {}
        for name in br:
            if name in "CD" and "C" in br and "D" in br and not self.dbg.get("nointer"):
                continue
            if name == "B" and getattr(self, "b_done", False):
                continue
            with contextlib.ExitStack() as es:
                getattr(self, "branch" + name)(l, es)
            S.barrier()
        if "C" in br and "D" in br and not self.dbg.get("nointer"):
            self.psmap["D"] = [P_[0], P_[1], P_[0], P_[0], P_[1], P_[2], P_[3], None]
            self.psmap["C"] = [P_[4], P_[4], P_[4], P_[5], P_[6], P_[7], P_[5], P_[6]]
            with contextlib.ExitStack() as es:
                recs = []
                for name in "CD":
                    S.rec = Stream()
                    getattr(self, "branch" + name)(l, es)
                    recs.append(S.rec)
                S.rec = None
                S.play(recs)
            self.psmap = {}
            S.barrier()

    def branchD(self, l, es):
        S, I, O, PS = self.S, self.I, self.O, self.psmap.get("D", self.PS)
        sb = lambda shape, dt=F32, n="d": S.sb_in(es, shape, dt, n)
        dmt = sb([64, 8, 64]); S.dma("sp", dmt, I["c_dmt"])
        qdec = sb([128, 8, 64]); S.dma("sp", qdec, I["c_qdec"])
        kdec = {64: sb([64, W]), 8: sb([64, W])}
        cdec = {64: sb([128, W]), 8: sb([128, W])}
        for T in (64, 8):
            S.dma("sp", kdec[T], I["c_kdec%d" % T]); S.dma("sp", cdec[T], I["c_cdec%d" % T])
        gng = self.bload(es, I["d_ln_g"][l:l + 1, :]); gnb = self.bload(es, I["d_ln_b"][l:l + 1, :])
        Sf = sb([128, 8, 128]); Sb = sb([128, 8, 128], BF16)
        PD = sb([64, 3072]); cosT = sb([64, 128]); sinT = sb([64, 128]); zz = [sb([64, W]) for _ in range(3)]
        t1 = None; t2 = sb([64, 2048]); qkr = sb([64, 2048])
        vb = sb([64, 8, 128], BF16); ktb = sb([64, 8, 128], BF16)
        qT = sb([128, 8, 64], BF16); qdT = sb([128, 8, 64], BF16); kT = sb([128, 8, 64], BF16)
        inT = sb([64, 8, 64], BF16)
        os_ = [sb([64, W]) for _ in range(2)]
        sc = self.small(es); sc["sq"] = sb([64, W])
        items = []
        for (tok0, L, T, sidx, prow0) in self.seqs():
            nch = min(L // T, self.dbg.get('maxch', 10 ** 9))
            for ci in range(nch):
                items.append((tok0, L, T, sidx, prow0, ci, ci == 0, ci == nch - 1))

        def loads(k):
            tok0, L, T, sidx, prow0, ci, first, last = items[k]
            t0 = tok0 + ci * T
            pr = prow0 + ci * T
            S.dma("sp", PD[:T, :], self.P[t0:t0 + T, OD:OD + 3072])
            S.dma("sp", cosT[:T, :], I["c_rdc"][pr:pr + T, :])
            S.dma("sp", sinT[:T, :], I["c_rds"][pr:pr + T, :])
            S.dma("sp", zz[k % 3][:T, :], self.P[t0:t0 + T, OZ + 3 * W:OZ + 4 * W])

        loads(0)
        for k, (tok0, L, T, sidx, prow0, ci, first, last) in enumerate(items):
            t0 = tok0 + ci * T
            z = zz[k % 3]
            o = os_[k % 2]
            if k >= 2:
                S.wait(("Ddone", l, k - 2))
            if first:
                S.dma("sp", Sf, I["dS"][l, sidx].re("h d v -> d h v"))
                S.cp("act", Sb, Sf)
            self.rope(qkr, PD, cosT, sinT, T, 16, 128, t1, t2)
            S.cp("act", vb[:T].re("p h d -> p (h d)"), PD[:T, 2048:3072])
            S.tt("pool", ktb[:T].re("p h d -> p (h d)"), qkr[:T, W:2 * W], kdec[T][:T, :], ALU.mult)
            for h in range(8):
                S.tr(PS[0][:, h * T:(h + 1) * T], qkr[:T, h * 128:(h + 1) * 128], self.ident[:T, :T])
                S.tr(PS[1][:, h * T:(h + 1) * T], qkr[:T, W + h * 128:W + (h + 1) * 128], self.ident[:T, :T])
            q3 = PS[0][:, :8 * T].re("p (h t) -> p h t", h=8)
            S.cp("act", qT[:, :, :T], q3)
            S.tt("dve", qdT[:, :, :T], q3, qdec[:, :, :T], ALU.mult)
            S.cp("act", kT[:, :, :T], PS[1][:, :8 * T].re("p (h t) -> p h t", h=8))
            if k + 1 < len(items):
                loads(k + 1)
            for h in range(8):
                S.mm(PS[2][:T, h * T:(h + 1) * T], kT[:, h, :T], qT[:, h, :T])
            S.tt("dve", inT[:T, :, :T], PS[2][:T, :8 * T].re("p (h t) -> p h t", h=8), dmt[:T, :, :T], ALU.mult)
            for h in range(8):
                pso = PS[3 + h // 4][:T, (h % 4) * 128:(h % 4 + 1) * 128]
                with S.atomic():
                    S.mm(pso, inT[:T, h, :T], vb[:T, h, :], start=True, stop=False)
                    S.mm(pso, qdT[:, h, :T], Sb[:, h, :], start=False, stop=True)
            S.cp("act", o[:T, 0:512], PS[3][:T, :])
            S.cp("act", o[:T, 512:1024], PS[4][:T, :])
            for h in range(8):
                S.mm(PS[5 + h // 4][:, (h % 4) * 128:(h % 4 + 1) * 128], ktb[:T, h, :], vb[:T, h, :])
            Sf2 = Sf.re("p h d -> p (h d)")
            S.tt("pool", Sf2, Sf2, cdec[T], ALU.mult)
            S.tt("dve", Sf2[:, 0:512], Sf2[:, 0:512], PS[5], ALU.add)
            S.tt("dve", Sf2[:, 512:1024], Sf2[:, 512:1024], PS[6], ALU.add)
            S.cp("act", Sb, Sf)
            S.signal(("Dmain", l, k))
            with S.sub("post"):
                S.wait(("Dmain", l, k))
                self.headnorm(o, T, 8, 128, gng, gnb, sc, on_act=True)
                self.gate_store(o, z, T, t0, 3)
                S.signal(("Ddone", l, k))
            if last:
                S.dma("sp", O["o_dS"][l, sidx].re("h d v -> d h v"), Sf)

    def branchB(self, l, es):
        S, I, O, PS = self.S, self.I, self.O, self.PS
        sb = lambda shape, dt=F32, n="b": S.sb_in(es, shape, dt, n)
        mp = sb([128, 256]); S.dma("sp", mp, I["c_mp"])
        mp0 = sb([128, 256]); S.dma("sp", mp0, I["c_mp0"])
        ms = sb([128, 256]); S.dma("sp", ms, I["c_ms"])
        sink = self.bload(es, I["b_sinks"][l:l + 1, :], 16, 128)
        kTp = [sb([64, 2, 128], BF16) for _ in range(2)]
        vp = [sb([128, 2, 64], BF16) for _ in range(2)]
        ckf = sb([128, 128]); ckb = sb([128, 128], BF16)
        PB = sb([128, 1280]); cosT = sb([128, 64]); sinT = sb([128, 64]); zz = [sb([128, W]) for _ in range(2)]
        t1 = None; t2 = sb([128, 1152]); qkr = sb([128, 1152]); qkb = sb([128, 1152], BF16)
        qT = sb([64, 16, 128], BF16)
        s_sbs = [sb([128, 2, 256]) for _ in range(4)]; p_bfs = [sb([128, 2, 256], BF16) for _ in range(4)]
        pTs = [sb([128, 4, 128], BF16) for _ in range(4)]
        sms = [self.small(es, 6, 2) for _ in range(4)]
        rden = sb([128, 16])
        yb = sb([128, W])
        bitems = []
        for (tok0, L, T, sidx, prow0) in self.seqs():
            Lb = 128 if L % 128 == 0 else L
            for bi in range(min(L // Lb, self.dbg.get('maxch', 10 ** 9))):
                bitems.append((tok0 + bi * Lb, prow0 + bi * Lb, Lb))

        def loadsB(k):
            t0_, pr_, Lb_ = bitems[k]
            S.dma("sp", PB[:Lb_, :], self.P[t0_:t0_ + Lb_, OB:OB + 1280])
            S.dma("sp", cosT[:Lb_, :], I["c_rbc"][pr_:pr_ + Lb_, :])
            S.dma("sp", sinT[:Lb_, :], I["c_rbs"][pr_:pr_ + Lb_, :])
            S.dma("sp", zz[k % 2][:Lb_, :], self.P[t0_:t0_ + Lb_, OZ + W:OZ + 2 * W])

        loadsB(0)
        kB = 0
        for (tok0, L, T, sidx, prow0) in self.seqs():
            Lb = 128 if L % 128 == 0 else L
            nb = L // Lb
            prompt = (sidx == 0)
            if prompt:
                S.memset("pool", kTp[0], 0.0)
                S.memset("pool", vp[0], 0.0)
            else:
                S.dma("sp", ckf, I["ck"][l, sidx])
                S.cp("act", ckb, ckf)
                psb = PS[0].bits(BF16)
                for g in range(2):
                    S.tr(psb[:64, g * 128:(g + 1) * 128], ckb[:, g * 64:(g + 1) * 64], self.identb)
                S.cp("dve", kTp[0], psb[:64, 0:256].re("p (g t) -> p g t", g=2))
                S.dma("sp", ckf, I["cv"][l, sidx])
                S.cp("act", vp[0].re("p g d -> p (g d)"), ckf)
            for bi in range(min(nb, self.dbg.get('maxch', 10 ** 9))):
                t0 = tok0 + bi * Lb
                pr = prow0 + bi * Lb
                prev, cur = (bi % 2, (bi + 1) % 2)
                Wk = 128 + Lb
                mask = (mp0 if bi == 0 else mp) if prompt else ms
                z = zz[kB % 2]
                self.rope(qkr, PB, cosT, sinT, Lb, 18, 64, t1, t2)
                S.cp("act", qkb[:Lb, :], qkr[:Lb, :])
                S.cp("act", vp[cur][:Lb].re("p g d -> p (g d)"), PB[:Lb, 1152:1280])
                psq = [PS[0].bits(BF16), PS[1].bits(BF16), PS[2].bits(BF16)]
                for h in range(18):
                    S.tr(psq[h // 8][:64, (h % 8) * 128:(h % 8) * 128 + Lb], qkb[:Lb, h * 64:(h + 1) * 64],
                         self.identb[:Lb, :Lb])
                for g in range(2):
                    S.cp("act" if g else "dve", qT[:, g * 8:(g + 1) * 8, :Lb],
                         psq[g][:64, :].re("p (h t) -> p h t", h=8)[:, :, :Lb])
                S.cp("dve", kTp[cur][:, :, :Lb], psq[2][:64, 0:256].re("p (g t) -> p g t", g=2)[:, :, :Lb])
                sbanks = [PS[0], PS[1], PS[3], PS[4]]

                def pair_chain(hp, slot):
                    g = hp // 4
                    pss = sbanks[slot]
                    s_sb, p_bf, pT, sm = s_sbs[slot], p_bfs[slot], pTs[slot], sms[slot]
                    for j in range(2):
                        h = hp * 2 + j
                        S.mm(pss[:Lb, j * 256:j * 256 + 128], qT[:, h, :Lb], kTp[prev][:, g, :])
                        S.mm(pss[:Lb, j * 256 + 128:j * 256 + 128 + Lb], qT[:, h, :Lb], kTp[cur][:, g, :Lb])
                    yield
                    ps3 = pss[:Lb, :].re("p (j c) -> p j c", j=2)[:, :, :Wk]
                    S.stt(s_sb[:Lb, :, :Wk], ps3, 0.125, mask[:Lb, :Wk].un(1).bc([Lb, 2, Wk]), ALU.mult, ALU.add)
                    yield
                    mx, m, negm, rs, dd, den = sm["s0"], sm["s1"], sm["s2"], sm["s3"], sm["s4"], sm["s5"]
                    S.red(mx[:Lb, :], s_sb[:Lb, :, :Wk], ALU.max)
                    yield
                    S.tt("dve", m[:Lb, :], mx[:Lb, :], sink[:Lb, hp * 2:hp * 2 + 2], ALU.max)
                    yield
                    S.ts("dve", negm[:Lb, :], m[:Lb, :], -1.0, ALU.mult)
                    S.tt("pool", dd[:Lb, :], sink[:Lb, hp * 2:hp * 2 + 2], m[:Lb, :], ALU.subtract)
                    yield
                    for j in range(2):
                        S.act(p_bf[:Lb, j, :Wk], s_sb[:Lb, j, :Wk], AF.Exp, bias=negm[:Lb, j:j + 1], accum=rs[:Lb, j:j + 1])
                    S.act(dd[:Lb, :], dd[:Lb, :], AF.Exp)
                    yield
                    pst = sbanks[slot].bits(BF16)
                    po = 0
                    for j in range(2):
                        S.tr(pst[:128, po + (2 * j) * 128:po + (2 * j) * 128 + Lb], p_bf[:Lb, j, 0:128], self.identb[:Lb, :Lb])
                        S.tr(pst[:Lb, po + (2 * j + 1) * 128:po + (2 * j + 1) * 128 + Lb], p_bf[:Lb, j, 128:128 + Lb],
                             self.identb[:Lb, :Lb])
                    S.tt("dve", den[:Lb, :], rs[:Lb, :], dd[:Lb, :], ALU.add)
                    yield
                    S.cp("act", pT[:, :, :Lb], pst[:, po:po + 512].re("p (s t) -> p s t", s=4)[:, :, :Lb])
                    S.recip(rden[:Lb, hp * 2:hp * 2 + 2], den[:Lb, :])
                    yield
                    for j in range(2):
                        h = hp * 2 + j
                        pso = PS[5 if h >= 8 else 2][:Lb, (h % 8) * 64:(h % 8 + 1) * 64]
                        S.mm(pso, pT[:, 2 * j, :Lb], vp[prev][:, g, :], start=True, stop=False)
                        S.mm(pso, pT[:Lb, 2 * j + 1, :Lb], vp[cur][:Lb, g, :], start=False, stop=True)
                    yield

                for hp0 in range(0, 8, 4):
                    lockstep([pair_chain(hp0 + s, s) for s in range(4)])
                for hh in range(2):
                    S.tt("dve", yb[:Lb, hh * 512:(hh + 1) * 512].re("p (h d) -> p h d", h=8),
                         PS[5 if hh else 2][:Lb, :].re("p (h d) -> p h d", h=8),
                         rden[:Lb, hh * 8:(hh + 1) * 8].un(2).bc([Lb, 8, 64]), ALU.mult)
                if bi == nb - 1:
                    if prompt:
                        S.dma("sp", O["o_ck"][l, sidx], qkr[:, 1024:1152])
                        S.dma("sp", O["o_cv"][l, sidx], PB[:, 1152:1280])
                    else:
                        S.dma("sp", O["o_ck"][l, sidx][0:128 - Lb, :], I["ck"][l, sidx][Lb:128, :])
                        S.dma("sp", O["o_cv"][l, sidx][0:128 - Lb, :], I["cv"][l, sidx][Lb:128, :])
                        S.dma("sp", O["o_ck"][l, sidx][128 - Lb:128, :], qkr[:Lb, 1024:1152])
                        S.dma("sp", O["o_cv"][l, sidx][128 - Lb:128, :], PB[:Lb, 1152:1280])
                kB += 1
                if kB < len(bitems):
                    loadsB(kB)
                self.gate_store(yb, z, Lb, t0, 1)

    def branchC(self, l, es):
        S, I, O, PS = self.S, self.I, self.O, self.psmap.get("C", self.PS)
        sb = lambda shape, dt=F32, n="c": S.sb_in(es, shape, dt, n)
        tri = sb([64, 64]); S.dma("sp", tri, I["c_tri"])
        madd = sb([64, 64]); S.dma("sp", madd, I["c_madd"])
        cw = [self.bload(es, I["c_conv_w"][l, j:j + 1, :]) for j in range(4)]
        cb = self.bload(es, I["c_conv_b"][l:l + 1, :])
        gng = self.bload(es, I["c_ln_g"][l:l + 1, :]); gnb = self.bload(es, I["c_ln_b"][l:l + 1, :])
        ibias = sb([8, 1]); S.dma("sp", ibias, I["c_i_bias"][l].re("(h o) -> h o", o=1))
        fbias = sb([8, 1]); S.dma("sp", fbias, I["c_f_bias"][l].re("(h o) -> h o", o=1))
        negfb = sb([8, 1]); S.ts("dve", negfb, fbias, -1.0, ALU.mult)
        ones8 = sb([8, 64]); S.memset("pool", ones8, 1.0)
        zeros8 = sb([8, 64]); S.memset("pool", zeros8, 0.0)
        onesb = sb([64, 1], BF16); S.memset("pool", onesb, 1.0)
        Cf = sb([64, 8, 128]); Cb = sb([64, 8, 128], BF16)
        nf = sb([64, 8]); nb_ = sb([64, 8], BF16)
        mfm = sb([8, 1])
        U = [sb([64, W]) for _ in range(4)]
        Vt = sb([64, W]); gates = sb([64, 16]); zz = [sb([64, W]) for _ in range(3)]
        conv = sb([64, W]); tmp = sb([64, W])
        FR = sb([8, 8, 64]); tf1 = sb([8, 64]); tf2 = sb([8, 64]); negMl = sb([8, 1]); dg = sb([8, 8])
        BD = sb([8, 8, 64]); tm = sb([64, 4, 8])
        wpre = sb([64, 8, 64]); wts = sb([64, 8, 64])
        vb = sb([64, 8, 128], BF16); qT = sb([64, 8, 64], BF16); kT = sb([64, 8, 64], BF16)
        AT = sb([64, 8, 64], BF16); ktb = sb([64, 8, 64], BF16)
        n1s = [sb([64, W]) for _ in range(2)]; dens = [sb([128, 16]) for _ in range(2)]; dq = sb([64, 16]); sR = sb([64, 8])
        sc = self.small(es); sc["sq"] = sb([64, W])
        id8 = self.ident[:8, :8]
        citems = []
        for (tok0, L, T, sidx, prow0) in self.seqs():
            for ci in range(min(L // T, self.dbg.get('maxch', 10 ** 9))):
                citems.append((tok0, T, sidx, ci))

        def loadsC(k):
            tok0_, T_, sidx_, ci_ = citems[k]
            t0_ = tok0_ + ci_ * T_
            for j in range(4):
                sh = 3 - j
                if ci_ == 0:
                    if sh > 0:
                        S.dma("sp", U[j][0:sh, :], I["mconv"][l, sidx_][j:3, :])
                    S.dma("sp", U[j][sh:T_, :], self.P[t0_:t0_ + T_ - sh, OC:OC + W])
                else:
                    S.dma("sp", U[j][:T_, :], self.P[t0_ - sh:t0_ - sh + T_, OC:OC + W])
            S.dma("sp", Vt[:T_, :], self.P[t0_:t0_ + T_, OC + W:OC + 2 * W])
            S.dma("sp", gates[:T_, :], self.P[t0_:t0_ + T_, OC + 2 * W:OC + 2 * W + 16])
            S.dma("sp", zz[k % 3][:T_, :], self.P[t0_:t0_ + T_, OZ + 2 * W:OZ + 3 * W])

        loadsC(0)
        kC = 0
        for (tok0, L, T, sidx, prow0) in self.seqs():
            S.dma("sp", Cf, I["mC"][l, sidx].re("h d v -> d h v"))
            S.cp("act", Cb, Cf)
            S.dma("sp", nf, I["mn"][l, sidx].re("h d -> d h"), _nc=True)
            S.cp("act", nb_, nf)
            S.dma("sp", mfm, I["mm"][l, sidx].re("(h o) -> h o", o=1))
            for ci in range(min(L // T, self.dbg.get('maxch', 10 ** 9))):
                t0 = tok0 + ci * T
                z = zz[kC % 3]
                n1 = n1s[kC % 2]
                kcur = kC
                if kC >= 2:
                    S.wait(("Cdone", l, kC - 2))
                S.tt("pool", conv[:T, :], U[0][:T, :], cw[0][:T, :], ALU.mult)
                for j in range(1, 4):
                    S.tt("dve" if j % 2 else "pool", tmp[:T, :], U[j][:T, :], cw[j][:T, :], ALU.mult)
                    S.tt("pool", conv[:T, :], conv[:T, :], tmp[:T, :], ALU.add)
                S.tt("dve", conv[:T, :], conv[:T, :], cb[:T, :], ALU.add)
                S.act(conv[:T, :], conv[:T, :], AF.Silu)
                S.cp("act", vb[:T].re("p h d -> p (h d)"), Vt[:T, :])
                S.tr(PS[0][:8, 0:T], gates[:T, 0:8], self.ident[:T, :T])
                S.tr(PS[0][:8, T:2 * T], gates[:T, 8:16], self.ident[:T, :T])
                li, lf, bb, gg, MM, scf, emt, wj = [FR[:, i, :T] for i in range(8)]
                S.ts("dve", li, PS[0][:8, 0:T], ibias, ALU.add)
                S.act(tf1[:, :T], PS[0][:8, T:2 * T], AF.Exp, bias=negfb, scale=-1.0)
                S.act(tf2[:, :T], tf1[:, :T], AF.Ln, bias=1.0)
                S.ts("pool", lf, tf2[:, :T], -1.0, ALU.mult)
                S.scan(bb, lf, zeros8[:, :T], 0.0, ALU.add, ALU.add)
                S.tt("dve", gg, li, bb, ALU.subtract)
                S.scan(MM, gg, gg, mfm, ALU.max, ALU.max)
                S.act(scf, MM, AF.Exp, bias=mfm, scale=-1.0)
                S.tt("dve", tf1[:, :T], bb, MM, ALU.add)
                S.act(emt, tf1[:, :T], AF.Exp, scale=-1.0)
                S.ts("dve", negMl, FR[:, 4, T - 1:T], -1.0, ALU.mult)
                S.act(wj, gg, AF.Exp, bias=negMl)
                for i, src in enumerate((gg, scf, emt, wj)):
                    S.tr(PS[1][:T, i * 8:(i + 1) * 8], src, id8)
                S.cp("dve", tm[:T].re("p a h -> p (a h)"), PS[1][:T, 0:32])
                g_tm, sc_tm, emt_tm, wj_tm = [tm[:T, i, :] for i in range(4)]
                S.tt("pool", BD[:, :, :T], MM.un(1).bc([8, 8, T]), id8.un(2).bc([8, 8, T]), ALU.mult)
                S.mm(PS[2][:T, :8 * T], ones8[:, :T], BD[:, :, :T])
                S.stt(wpre[:T, :, :T], PS[2][:T, :8 * T].re("p (h t) -> p h t", h=8), -1.0,
                      madd[:T, :T].un(1).bc([T, 8, T]), ALU.mult, ALU.add)
                for h in range(8):
                    S.act(wts[:T, h, :T], wpre[:T, h, :T], AF.Exp, bias=g_tm[:, h:h + 1])
                for h in range(8):
                    S.tr(PS[3][:64, h * T:(h + 1) * T], conv[:T, h * 64:(h + 1) * 64], self.ident[:T, :T])
                    S.tr(PS[4][:64, h * T:(h + 1) * T], conv[:T, 512 + h * 64:512 + (h + 1) * 64], self.ident[:T, :T])
                S.cp("act", qT[:, :, :T], PS[3][:64, :8 * T].re("p (h t) -> p h t", h=8))
                S.cp("dve", kT[:, :, :T], PS[4][:64, :8 * T].re("p (h t) -> p h t", h=8))
                kC += 1
                if kC < len(citems):
                    loadsC(kC)
                for h in range(8):
                    S.mm(PS[5][:T, h * T:(h + 1) * T], kT[:, h, :T], qT[:, h, :T])
                S.stt(AT[:T, :, :T], PS[5][:T, :8 * T].re("p (h t) -> p h t", h=8), 0.125, wts[:T, :, :T], ALU.mult, ALU.mult)
                for h in range(8):
                    S.mm(PS[6 + h // 4][:T, (h % 4) * 128:(h % 4 + 1) * 128], AT[:T, h, :T], vb[:T, h, :])
                    S.mm(PS[0][:T, h:h + 1], AT[:T, h, :T], onesb[:T, :])
                    S.mm(PS[0][:T, 8 + h:9 + h], qT[:, h, :T], nb_[:, h:h + 1])
                S.cp("act", n1[:T, 0:512], PS[6][:T, :])
                S.cp("act", n1[:T, 512:1024], PS[7][:T, :])
                S.cp("dve", dq[:T, :], PS[0][:T, 0:16])
                for h in range(8):
                    S.mm(PS[6 + h // 4][:T, (h % 4) * 128:(h % 4 + 1) * 128], qT[:, h, :T], Cb[:, h, :])
                den, aden = dens[kcur % 2], sc["s7"]
                S.tt("dve", den[:T, :8], sc_tm, dq[:T, 8:16], ALU.mult)
                S.tt("dve", den[:T, :8], den[:T, :8], dq[:T, 0:8], ALU.add)
                S.ts("dve", aden[:T, :8], den[:T, :8], -1.0, ALU.mult)
                S.tt("dve", aden[:T, :8], aden[:T, :8], den[:T, :8], ALU.max)
                S.tt("dve", aden[:T, :8], aden[:T, :8], emt_tm, ALU.max)
                S.recip(den[:T, :8], aden[:T, :8])
                for hh in range(2):
                    S.tt("dve", tmp[:T, hh * 512:(hh + 1) * 512].re("p (h d) -> p h d", h=4),
                         PS[6 + hh][:T, :].re("p (h d) -> p h d", h=4),
                         sc_tm[:, hh * 4:(hh + 1) * 4].un(2).bc([T, 4, 128]), ALU.mult)
                S.tt("pool", n1[:T, :], n1[:T, :], tmp[:T, :], ALU.add)
                n13 = n1[:T, :].re("p (h d) -> p h d", h=8)
                S.tt("pool", n13, n13, den[:T, :8].un(2).bc([T, 8, 128]), ALU.mult)
                S.stt(ktb[:T], conv[:T, 512:1024].re("p (h d) -> p h d", h=8), 0.125,
                      wj_tm.un(2).bc([T, 8, 64]), ALU.mult, ALU.mult)
                for h in range(8):
                    S.mm(PS[3 + h // 4][:64, (h % 4) * 128:(h % 4 + 1) * 128], ktb[:T, h, :], vb[:T, h, :])
                    S.mm(PS[5][:64, h:h + 1], ktb[:T, h, :], onesb[:T, :])
                S.ts("dve", dg, id8, FR[:, 5, T - 1:T], ALU.mult)
                S.mm(PS[5][:64, 16:24], ones8[:, :64], dg)
                S.cp("dve", sR, PS[5][:64, 16:24])
                S.tt("pool", Cf, Cf, sR.un(2).bc([64, 8, 128]), ALU.mult)
                Cf2 = Cf.re("p h d -> p (h d)")
                S.tt("dve", Cf2[:, 0:512], Cf2[:, 0:512], PS[3][:64, :], ALU.add)
                S.tt("dve", Cf2[:, 512:1024], Cf2[:, 512:1024], PS[4][:64, :], ALU.add)
                S.cp("act", Cb, Cf)
                S.tt("dve", nf, nf, sR, ALU.mult)
                S.tt("dve", nf, nf, PS[5][:64, 0:8], ALU.add)
                S.cp("act", nb_, nf)
                S.tt("dve", mfm, FR[:, 2, T - 1:T], FR[:, 4, T - 1:T], ALU.add)
                S.signal(("Cmain", l, kcur))
                with S.sub("post"):
                    S.wait(("Cmain", l, kcur))
                    self.headnorm(n1, T, 8, 128, gng, gnb, sc, on_act=True)
                    self.gate_store(n1, z, T, t0, 2)
                    S.signal(("Cdone", l, kcur))
            S.dma("sp", O["o_mC"][l, sidx].re("h d v -> d h v"), Cf)
            S.dma("sp", O["o_mn"][l, sidx].re("h d -> d h"), nf, _nc=True)
            S.dma("sp", O["o_mm"][l, sidx].re("(h o) -> h o", o=1), mfm)
            S.dma("sp", O["o_mconv"][l, sidx], self.P[tok0 + L - 3:tok0 + L, OC:OC + W])

    def branchA(self, l, es):
        S, I, O, PS = self.S, self.I, self.O, self.PS
        sb = lambda shape, dt=F32, n="a": S.sb_in(es, shape, dt, n)
        tri = sb([64, 64]); S.dma("sp", tri, I["c_tri"])
        m5 = {64: sb([64, 320]), 8: sb([64, 320])}
        S.dma("sp", m5[64], I["c_m5_64"]); S.dma("sp", m5[8], I["c_m5_8"])
        mu = self.bload(es, I["a_mu"][l:l + 1, :], 3200)
        w0 = self.bload(es, I["a_w0"][l:l + 1, :]); a0 = self.bload(es, I["a_a0"][l:l + 1, :])
        kk_ = self.bload(es, I["a_k_k"][l:l + 1, :]); ka = self.bload(es, I["a_k_a"][l:l + 1, :])
        rk_ = self.bload(es, I["a_r_k"][l:l + 1, :])
        gng = self.bload(es, I["a_ln_g"][l:l + 1, :]); gnb = self.bload(es, I["a_ln_b"][l:l + 1, :])
        WA = sb([128, W])
        S.dma("sp", WA[0:64, :], I["a_w_up"][l]); S.dma("sp", WA[64:128, :], I["a_a_up"][l])
        onesf = sb([64, 1]); S.memset("pool", onesf, 1.0)
        Hf = sb([64, 16, 64]); Hb = sb([64, 16, 64], BF16)
        Sin = sb([64, 16, 64])
        PA = sb([64, 3200]); PV = sb([64, 3200])
        L2 = sb([64, 128]); LT = sb([128, 64])
        sw = sb([64, W]); aa = sb([64, W]); kk = sb([64, W]); km = sb([64, W]); tA = sb([64, W]); tB = sb([64, W])
        eP = sb([64, W]); eN = sb([64, W]); ePm = sb([64, W])
        psm = self.small(es, 3)
        sets = []
        for _ in range(2):
            sets.append(dict(Xb=[sb([64, W], BF16) for _ in range(4)],
                             vb=sb([64, W], BF16), FM=sb([64, 16, 4, 64], BF16), PT=sb([64, 16])))
        zb3 = [dict(z=sb([64, W]), bonus=sb([64, W])) for _ in range(3)]
        Gm = [sb([64, 5, 64], BF16) for _ in range(8)]
        Xs = [[sb([64, 64], BF16)] for _ in range(8)]
        XL = [sb([64, 192], BF16) for _ in range(8)]
        ys = [sb([64, W]) for _ in range(2)]
        sc = self.small(es); sc["sq"] = sb([64, W])
        PQ = [PS[6], PS[7]]
        YB, HB = PS[0], PS[1]
        work = [PS[2], PS[3], PS[4], PS[5]]

        items = []
        for (tok0, L, T, sidx, prow0) in self.seqs():
            nch = min(L // T, self.dbg.get('maxch', 10 ** 9))
            for ci in range(nch):
                items.append((tok0, L, T, sidx, ci, ci == 0, ci == nch - 1))

        def prep(item, B, ZB):
            tok0, L, T, sidx, ci, first, last = item
            z, bonus, Xb, vb, FM, PT = ZB["z"], ZB["bonus"], B["Xb"], B["vb"], B["FM"], B["PT"]
            t0 = tok0 + ci * T
            S.dma("sp", PA[:T, :], self.P[t0:t0 + T, 0:3200])
            if ci == 0:
                S.dma("sp", PV[0:1, :], I["rshift"][l, sidx:sidx + 1, :])
                S.dma("sp", PV[1:T, :], self.P[t0:t0 + T - 1, 0:3200])
            else:
                S.dma("sp", PV[:T, :], self.P[t0 - 1:t0 - 1 + T, 0:3200])
            S.dma("sp", z[:T, :], self.P[t0:t0 + T, OZ:OZ + W])
            def shift(en, c0, c1):
                S.tt(en, PV[:T, c0:c1], PV[:T, c0:c1], PA[:T, c0:c1], ALU.subtract)
                S.tt(en, PV[:T, c0:c1], PV[:T, c0:c1], mu[:T, c0:c1], ALU.mult)
                S.tt(en, PV[:T, c0:c1], PV[:T, c0:c1], PA[:T, c0:c1], ALU.add)
            shift("dve", 3 * W, 3200)
            r, k, v = PV[:T, 0:W], PV[:T, W:2 * W], PV[:T, 2 * W:3 * W]
            S.act(L2[:T, 0:64], PV[:T, 3072:3136], AF.Tanh)
            S.cp("dve", L2[:T, 64:128], PV[:T, 3136:3200])
            S.tr(PQ[0][:, 0:T], L2[:T, :], self.ident[:T, :T])
            S.cp("act", LT[:, :T], PQ[0][:, 0:T])
            shift("pool", 0, 3 * W)
            for hh in range(2):
                S.mm(PQ[hh][:T, :], LT[0:64, :T], WA[0:64, hh * 512:(hh + 1) * 512])
            for hh in range(2):
                cs = slice(hh * 512, (hh + 1) * 512)
                S.tt("dve", sw[:T, cs], PQ[hh][:T, :], w0[:T, cs], ALU.add)
            for hh in range(2):
                S.mm(PQ[hh][:T, :], LT[64:128, :T], WA[64:128, hh * 512:(hh + 1) * 512])
            for hh in range(2):
                cs = slice(hh * 512, (hh + 1) * 512)
                S.tt("dve", aa[:T, cs], PQ[hh][:T, :], a0[:T, cs], ALU.add)
            S.act(sw[:T, :], sw[:T, :], AF.Sigmoid)
            S.act(aa[:T, :], aa[:T, :], AF.Sigmoid)
            S.tt("pool", kk[:T, :], k, kk_[:T, :], ALU.mult)
            S.act(tA[:T, :], kk[:T, :], AF.Square)
            ss, rn, bs = psm["s0"], psm["s1"], psm["s2"]
            S.red(ss[:T, :16], tA[:T, :].re("p (h d) -> p h d", h=16), ALU.add)
            S.rsqrt(rn[:T, :16], ss[:T, :16], 1e-24, ALU.max)
            kk3 = kk[:T, :].re("p (h d) -> p h d", h=16)
            S.tt("pool", kk3, kk3, rn[:T, :16].un(2).bc([T, 16, 64]), ALU.mult)
            S.stt(tA[:T, :], aa[:T, :], -1.0, ka[:T, :], ALU.add, ALU.mult)
            S.stt(km[:T, :], tA[:T, :], 1.0, k, ALU.add, ALU.mult)
            S.tt("dve", tB[:T, :], r, km[:T, :], ALU.mult)
            S.tt("dve", tB[:T, :], tB[:T, :], rk_[:T, :], ALU.mult)
            S.red(bs[:T, :16], tB[:T, :].re("p (h d) -> p h d", h=16), ALU.add)
            S.tt("pool", bonus[:T, :].re("p (h d) -> p h d", h=16), v.re("p (h d) -> p h d", h=16),
                 bs[:T, :16].un(2).bc([T, 16, 64]), ALU.mult)
            for hh in range(2):
                S.mm(PQ[hh][:T, :], tri[:T, :T], sw[:T, hh * 512:(hh + 1) * 512])
            for hh in range(2):
                cs = slice(hh * 512, (hh + 1) * 512)
                S.act(eP[:T, cs], PQ[hh][:T, :], AF.Exp, scale=C0)
                S.act(eN[:T, cs], PQ[hh][:T, :], AF.Exp, scale=-C0)
                S.tt("dve", ePm[:T, cs], PQ[hh][:T, :], sw[:T, cs], ALU.subtract)
            S.act(ePm[:T, :], ePm[:T, :], AF.Exp, scale=C0)
            S.tt("pool", tB[:T, :], kk[:T, :], aa[:T, :], ALU.mult)
            S.tt("pool", Xb[0][:T, :], tB[:T, :], eN[:T, :], ALU.mult)
            S.tt("dve", Xb[1][:T, :], km[:T, :], eN[:T, :], ALU.mult)
            S.stt(Xb[2][:T, :], kk[:T, :], -1.0, ePm[:T, :], ALU.mult, ALU.mult)
            S.tt("pool", Xb[3][:T, :], r, eP[:T, :], ALU.mult)
            S.cp("act", vb[:T, :], v)
            for h in range(16):
                S.mm(PQ[0][:64, h:h + 1], sw[:T, h * 64:(h + 1) * 64], onesf[:T, :])
            S.act(PT, PQ[0][:64, 0:16], AF.Exp, scale=C0)
            for g in range(4):
                psb = PQ[(g + 1) % 2].bits(BF16)
                for hq in range(4):
                    h = g * 4 + hq
                    for q in range(4):
                        S.tr(psb[:64, (hq * 4 + q) * 64:(hq * 4 + q) * 64 + T], Xb[q][:T, h * 64:(h + 1) * 64],
                             self.identb[:T, :T])
                S.cp("act" if g % 2 else "dve", FM[:, g * 4:(g + 1) * 4, :, :T],
                     psb[:64, :].re("p (h q t) -> p h q t", h=4, q=4)[:, :, :, :T])

        def back(item, B, y):
            tok0, L, T, sidx, ci, first, last = item
            Xb, vb, FM, PT = B["Xb"], B["vb"], B["FM"], B["PT"]
            t0 = tok0 + ci * T
            nlev = {64: 6, 8: 3}[T]
            if first:
                S.dma("sp", Sin, I["rS"][l, sidx].re("h v k -> v h k"))
                for h in range(16):
                    S.tr(PS[h // 8][:64, (h % 8) * 64:(h % 8 + 1) * 64], Sin[:, h, :], self.ident[:64, :64])
                for hh in range(2):
                    S.cp("act", Hf[:, hh * 8:(hh + 1) * 8, :].re("p h v -> p (h v)"), PS[hh][:64, :])
                S.cp("dve", Hb, Hf)

            def head_chain(h, slot):
                cs = slice(h * 64, (h + 1) * 64)
                gm = Gm[slot]
                reg = work[slot // 2]
                xo = (slot % 2) * 256
                S.mm(reg[:T, xo:xo + 2 * T], FM[:, h, 0, :T], FM[:, h, 2:4, :T])
                S.mm(reg[:T, xo + 2 * T:xo + 4 * T], FM[:, h, 1, :T], FM[:, h, 2:4, :T])
                yield
                S.tt("dve", gm[:T, 0:4, :T], reg[:T, xo:xo + 4 * T].re("p (q t) -> p q t", q=4),
                     m5[T][:T, :4 * T].re("p (q t) -> p q t", q=4), ALU.mult)
                yield
                labT, rabT, lakT, rakT = [gm[:T, q, :T] for q in range(4)]
                S.mm(reg[:T, xo:xo + T], FM[:, h, 2, :T], FM[:, h, 0, :T])
                S.mm(reg[:T, xo + 64:xo + 128], FM[:, h, 2, :T], Hb[:, h, :], start=True, stop=False)
                S.mm(reg[:T, xo + 64:xo + 128], lakT, vb[:T, cs], start=False, stop=True)
                yield
                S.tt("dve", gm[:T, 4, :T], reg[:T, xo:xo + T], m5[T][:T, 4 * T:5 * T], ALU.mult)
                X = Xs[slot][0]
                S.cp("act", X[:T, :], reg[:T, xo + 64:xo + 128])
                yield
                Lt, Ln = labT, gm[:T, 4, :T]
                for lev in range(nlev):
                    with S.atomic():
                        S.mm(reg[:T, xo:xo + 64], self.identb[:T, :T], X[:T, :], start=True, stop=False)
                        S.mm(reg[:T, xo:xo + 64], Lt, X[:T, :], start=False, stop=True)
                    if lev < nlev - 1:
                        S.mm(reg[:T, xo + 64:xo + 64 + T], Ln, Lt)
                        S.mm(reg[:T, xo + 64 + T:xo + 64 + 2 * T], Lt, Ln)
                    yield
                    xl = XL[slot]
                    wdt = 64 + 2 * T if lev < nlev - 1 else 64
                    S.cp("act", xl[:T, :wdt], reg[:T, xo:xo + wdt])
                    X = xl[:, 0:64]
                    if lev < nlev - 1:
                        Lt, Ln = xl[:T, 64:64 + T], xl[:T, 64 + T:64 + 2 * T]
                    yield
                psy = YB[:T, (h % 8) * 64:(h % 8 + 1) * 64]
                S.mm(psy, FM[:, h, 3, :T], Hb[:, h, :], start=True, stop=False)
                S.mm(psy, rabT, X[:T, :], start=False, stop=False)
                S.mm(psy, rakT, vb[:T, cs], start=False, stop=True)
                psh = HB[:64, (h % 8) * 64:(h % 8 + 1) * 64]
                S.mm(psh, Xb[0][:T, cs], X[:T, :], start=True, stop=False)
                S.mm(psh, Xb[1][:T, cs], vb[:T, cs], start=False, stop=True)
                yield

            for hh in range(2):
                lockstep([head_chain(hh * 8 + s, s) for s in range(8)])
                S.cp("act", y[:T, hh * 512:(hh + 1) * 512], YB[:T, :])
                Hh = Hf[:, hh * 8:(hh + 1) * 8, :]
                S.tt("dve", Hh.re("p h v -> p (h v)"), Hh.re("p h v -> p (h v)"), HB[:64, :], ALU.add)
                S.tt("pool", Hh, Hh, PT[:, hh * 8:(hh + 1) * 8].un(2).bc([64, 8, 64]), ALU.mult)
                if not (hh == 1 and last):
                    S.cp("act", Hb[:, hh * 8:(hh + 1) * 8, :], Hh)
            if last:
                for h in range(16):
                    S.tr(PS[h // 8][:64, (h % 8) * 64:(h % 8 + 1) * 64], Hf[:, h, :], self.ident[:64, :64])
                for hh in range(2):
                    S.cp("act", Sin[:, hh * 8:(hh + 1) * 8, :].re("p h v -> p (h v)"), PS[hh][:64, :])
                S.dma("sp", O["o_rS"][l, sidx].re("h v k -> v h k"), Sin)
                S.dma("sp", O["o_rshift"][l, sidx:sidx + 1, :], self.P[tok0 + L - 1:tok0 + L, 0:3200])

        def post(item, ZB, y):
            tok0, L, T, sidx, ci, first, last = item
            self.headnorm(y, T, 16, 64, gng, gnb, sc)
            S.tt("pool", y[:T, :], y[:T, :], ZB["bonus"][:T, :], ALU.add)
            self.gate_store(y, ZB["z"], T, tok0 + ci * T, 0)

        prep(items[0], sets[0], zb3[0])
        for k, item in enumerate(items):
            recs = []
            if k + 1 < len(items):
                recs.append(S.record(lambda: prep(items[k + 1], sets[(k + 1) % 2], zb3[(k + 1) % 3])))
            recs.append(S.record(lambda: back(item, sets[k % 2], ys[k % 2])))
            if k >= 1:
                recs.append(S.record(lambda: post(items[k - 1], zb3[(k - 1) % 3], ys[(k - 1) % 2])))
            S.play(recs)
        post(items[-1], zb3[(len(items) - 1) % 3], ys[(len(items) - 1) % 2])


    def phase3a(self, l):
        S = self.S
        with contextlib.ExitStack() as es:
            WBR = S.sb_in(es, [128, 32, D_MODEL], BF16, "WBR")
            for i in range(4):
                S.dma("pool", WBR[:, i * 8:(i + 1) * 8, :], self.I["w_branch"][l, i].re("(k p) c -> p k c", p=128))
            yz = S.sb_in(es, [128, 4 * W], F32, "yz")
            yzT = [S.sb_in(es, [128, 32, 128], BF16, "yzT") for _ in range(2)]
            Gb = [S.sb_in(es, [128, D_MODEL], F32, "G") for _ in range(2)]
            mg = [S.sb_in(es, [128, D_MODEL], F32, "mg") for _ in range(1)]
            mgb = [S.sb_in(es, [128, D_MODEL], BF16, "mgb") for _ in range(1)]
            tmp = [S.sb_in(es, [128, 512], F32, "tmp3") for _ in range(2)]

            def rows_(tt):
                return slice(tt * 128, (tt + 1) * 128)

            def load_yz(tt):
                S.dma("sp", yz, self.YZ[rows_(tt), :])

            def load_G(q):
                tt, i = divmod(q, 4)
                S.dma("sp", Gb[q % 2], self.P[rows_(tt), OG + i * D_MODEL:OG + (i + 1) * D_MODEL])

            load_yz(0)
            load_G(0)
            n = 0
            for tt in range(NTT):
                yT = yzT[tt % 2]
                for g in range(8):
                    ps = self.PS[g % 8]
                    for j in range(4):
                        k = g * 4 + j
                        S.tr(ps[:, j * 128:(j + 1) * 128], yz[:, k * 128:(k + 1) * 128], self.ident)
                    S.cp("act" if g % 2 else "dve", yT[:, g * 4:(g + 1) * 4, :], ps.re("p (j c) -> p j c", j=4))
                if tt + 1 < NTT:
                    load_yz(tt + 1)
                m_ = mg[0]
                for i in range(4):
                    q = tt * 4 + i
                    G = Gb[q % 2]
                    if q + 1 < NTT * 4:
                        load_G(q + 1)
                    S.act(G, G, AF.Sigmoid)
                    for j in range(4):
                        cs = slice(j * 512, (j + 1) * 512)
                        ps = self.PS[n % 8]
                        n += 1
                        for k in range(8):
                            S.mm(ps, yT[:, i * 8 + k, :], WBR[:, i * 8 + k, cs], start=(k == 0), stop=(k == 7))
                        if i == 0:
                            S.tt("dve", m_[:, cs], ps, G[:, cs], ALU.mult)
                        else:
                            t_ = tmp[n % 2]
                            S.tt("dve", t_, ps, G[:, cs], ALU.mult)
                            S.tt("pool", m_[:, cs], m_[:, cs], t_, ALU.add)
                mb = mgb[0]
                S.cp("act", mb, m_)
                S.dma("sp", self.MG[rows_(tt), :], mb)

    def phase3b(self, l, xsrc, xdst):
        S = self.S
        with contextlib.ExitStack() as es:
            WO = S.sb_in(es, [128, 16, D_MODEL], BF16, "WO")
            S.dma("pool", WO, self.I["w_out"][l].re("(k p) c -> p k c", p=128))
            lng = S.sb_in(es, [128, D_MODEL], F32, "lng")
            lnb = S.sb_in(es, [128, D_MODEL], F32, "lnb")
            S.dma("sp", lng, self.I["ln_g"][l:l + 1, :].bc([128, D_MODEL]))
            S.dma("sp", lnb, self.I["ln_b"][l:l + 1, :].bc([128, D_MODEL]))
            mb = [S.sb_in(es, [128, D_MODEL], BF16, "mb") for _ in range(2)]
            mT = [S.sb_in(es, [128, 16, 128], BF16, "mT") for _ in range(2)]
            xr = [S.sb_in(es, [128, D_MODEL], F32, "xr") for _ in range(2)]
            pre = [S.sb_in(es, [128, D_MODEL], F32, "pre") for _ in range(2)]
            st = S.sb_in(es, [128, 4, 6], F32, "bst")
            mv = S.sb_in(es, [128, 2], F32, "bmv")
            rstd = S.sb_in(es, [128, 1], F32, "rstd")
            n = 0
            for tt in range(NTT):
                rows = slice(tt * 128, (tt + 1) * 128)
                b_ = mb[tt % 2]
                t_ = mT[tt % 2]
                x_ = xr[tt % 2]
                p_ = pre[tt % 2]
                S.dma("sp", b_, self.MG[rows, :])
                S.dma("sp", x_, xsrc[rows, :])
                for g in range(2):
                    ps = self.PS[n % 8].bits(BF16)
                    n += 1
                    for j in range(8):
                        k = g * 8 + j
                        S.tr(ps[:, j * 128:(j + 1) * 128], b_[:, k * 128:(k + 1) * 128], self.identb)
                    S.cp("act" if g % 2 else "dve", t_[:, g * 8:(g + 1) * 8, :], ps.re("p (j c) -> p j c", j=8))
                for j in range(4):
                    cs = slice(j * 512, (j + 1) * 512)
                    ps = self.PS[n % 8]
                    n += 1
                    for k in range(16):
                        S.mm(ps, t_[:, k, :], WO[:, k, cs], start=(k == 0), stop=(k == 15))
                    S.stt(p_[:, cs], x_[:, cs], ALPHA, ps, ALU.mult, ALU.add)
                    S.op("dve", lambda e, j=j, p_=p_, cs=cs: e.bn_stats(out=st.ap[:, j, :], in_=p_.ap[:, cs]), [p_], [st])
                S.op("dve", lambda e: e.bn_aggr(out=mv.ap, in_=st.ap), [st], [mv])
                S.rsqrt(rstd, mv[:, 1:2], LN_EPS, ALU.add)
                S.ts("dve", p_, p_, mv[:, 0:1], ALU.subtract, rstd, ALU.mult)
                S.tt("pool", p_, p_, lng, ALU.mult)
                S.tt("pool", p_, p_, lnb, ALU.add)
                S.dma("sp", xdst[rows, :], p_)


WSHAPES = {
    "w_in": [DEPTH, D_MODEL, IN_COLS], "a_mu": [DEPTH, 3200], "a_w0": [DEPTH, W], "a_w_up": [DEPTH, 64, W],
    "a_a0": [DEPTH, W], "a_a_up": [DEPTH, 64, W], "a_k_k": [DEPTH, W], "a_k_a": [DEPTH, W], "a_r_k": [DEPTH, W],
    "a_ln_g": [DEPTH, W], "a_ln_b": [DEPTH, W], "b_sinks": [DEPTH, 16], "c_conv_w": [DEPTH, 4, W],
    "c_conv_b": [DEPTH, W], "c_i_bias": [DEPTH, 8], "c_f_bias": [DEPTH, 8], "c_ln_g": [DEPTH, W],
    "c_ln_b": [DEPTH, W], "d_ln_g": [DEPTH, W], "d_ln_b": [DEPTH, W], "w_branch": [DEPTH, 4, W, D_MODEL],
    "w_out": [DEPTH, D_MODEL, D_MODEL], "ln_g": [DEPTH, D_MODEL], "ln_b": [DEPTH, D_MODEL],
}
CONST_SHAPES = {
    "c_ident": [128, 128], "c_tri": [64, 64], "c_madd": [64, 64],
    "c_rbc": [NPOSROW, 64], "c_rbs": [NPOSROW, 64], "c_rdc": [NPOSROW, 128], "c_rds": [NPOSROW, 128],
    "c_dmt": [64, 8, 64], "c_qdec": [128, 8, 64], "c_kdec64": [64, 1024], "c_kdec8": [64, 1024],
    "c_cdec64": [128, 1024], "c_cdec8": [128, 1024], "c_mp": [128, 256], "c_mp0": [128, 256], "c_ms": [128, 256],
    "c_m5_64": [64, 320], "c_m5_8": [64, 320],
}


def make_consts():
    c = {}
    f32 = np.float32
    c["c_ident"] = np.eye(128, dtype=f32)
    s = np.arange(64)[:, None]
    t = np.arange(64)[None, :]
    c["c_tri"] = (s <= t).astype(f32)
    c["c_madd"] = np.where(s <= t, 0.0, -1e30).astype(f32)
    pos = np.concatenate([np.arange(LP), 8192 + np.arange(LS)]).astype(f32)

    def rope_tab(d):
        inv = (f32(10000.0) ** (-np.arange(0, d, 2, dtype=f32) / f32(d))).astype(f32)
        ang = (pos[:, None] * inv[None, :]).astype(f32)
        cs = np.cos(ang).astype(f32)
        sn = np.sin(ang).astype(f32)
        ct = np.stack([cs, cs], 1)
        st = np.stack([-sn, sn], 1)
        return ct.reshape(NPOSROW, -1), st.reshape(NPOSROW, -1)
    c["c_rbc"], c["c_rbs"] = rope_tab(64)
    c["c_rdc"], c["c_rds"] = rope_tab(128)
    ksc = float(f32(128.0) ** f32(-0.5))
    lg = np.log1p(-np.exp2(-5.0 - np.arange(8, dtype=np.float64)))
    rel = (t - s).astype(np.float64)
    dmt = np.zeros((64, 8, 64), f32)
    for h in range(8):
        dmt[:, h, :] = np.where(rel >= 0, np.exp(np.maximum(rel, 0.0) * lg[h]), 0.0) * ksc
    c["c_dmt"] = dmt
    idx = np.arange(64, dtype=np.float64)
    qd = np.exp((idx + 1.0)[None, :] * lg[:, None])
    c["c_qdec"] = np.broadcast_to(qd[None], (128, 8, 64)).astype(f32).copy()
    for Lc in (64, 8):
        kd = np.zeros((64, 8, 128), f32)
        for h in range(8):
            kd[:Lc, h, :] = (np.exp((Lc - 1.0 - idx[:Lc]) * lg[h]) * ksc)[:, None]
        c["c_kdec%d" % Lc] = kd.reshape(64, 1024)
        cd = np.zeros((128, 8, 128), f32)
        for h in range(8):
            cd[:, h, :] = np.exp(Lc * lg[h])
        c["c_cdec%d" % Lc] = cd.reshape(128, 1024)
    a = np.arange(128)[:, None]
    cc = np.arange(256)[None, :]
    ok = (cc >= a) & (cc <= 128 + a)
    c["c_mp"] = np.where(ok, 0.0, -1e30).astype(f32)
    c["c_mp0"] = np.where(ok & (cc >= 128), 0.0, -1e30).astype(f32)
    c["c_ms"] = np.where(ok & (cc < 136) & (a < 8), 0.0, -1e30).astype(f32)
    for T in (64, 8):
        ss = np.arange(64)[:, None]
        tt_ = np.arange(T)[None, :]
        su = (ss < tt_).astype(f32)
        iu = (ss <= tt_).astype(f32)
        sl = (ss > tt_).astype(f32)
        m5 = np.zeros((64, 320), f32)
        m5[:, :5 * T] = np.concatenate([su, iu, su, iu, sl], 1)
        m5[T:, :] = 0.0
        c["c_m5_%d" % T] = m5
    return c


_PROG = {}


def get_prog(dbg=None):
    key = repr(sorted((dbg or {}).items()))
    if key not in _PROG:
        _PROG[key] = Prog(dbg)
    return _PROG[key]


def make_in_maps(inp):
    f = lambda a: np.ascontiguousarray(np.asarray(a, dtype=np.float32))
    consts = make_consts()
    shared = {n: f(inp[n]).reshape(WSHAPES[n]) for n in WSHAPES}
    st_names = [("rS", "state_rwkv_S"), ("rshift", "state_rwkv_shift"), ("ck", "cache_swa_k"), ("cv", "cache_swa_v"),
                ("mC", "state_mlstm_C"), ("mn", "state_mlstm_n"), ("mm", "state_mlstm_m"),
                ("mconv", "state_mlstm_conv"), ("dS", "state_ret_S")]
    xp = f(inp["x_prompt"])
    xs = f(inp["x_sample"])
    maps = []
    for c in range(NCORES):
        m = dict(shared)
        m.update(consts)
        m["xin"] = np.concatenate([xp[c % 4], xs[c * NSS:(c + 1) * NSS].reshape(NSS * LS, D_MODEL)], 0)
        for kn, full in st_names:
            a = f(inp[full])[:, c * NSS:(c + 1) * NSS]
            z = np.zeros((DEPTH, 1) + a.shape[2:], np.float32)
            a = np.concatenate([z, a], 1)
            if kn in ("ck", "cv"):
                a = a.reshape(DEPTH, NSEQ, 128, 128)
            m[kn] = np.ascontiguousarray(a)
        maps.append(m)
    return maps


def assemble(results):
    y = [r["y"] for r in results]
    y_prompt = np.stack([y[c][:LP] for c in range(4)], 0)
    y_sample = np.concatenate([y[c][LP:].reshape(NSS, LS, D_MODEL) for c in range(NCORES)], 0)
    outs = [y_prompt, y_sample]
    shp = {"o_rS": (16, 64, 64), "o_rshift": (3200,), "o_ck": (128, 2, 64), "o_cv": (128, 2, 64),
           "o_mC": (8, 64, 128), "o_mn": (8, 64), "o_mm": (8,), "o_mconv": (3, 1024), "o_dS": (8, 128, 128)}
    for n in ["o_rS", "o_rshift", "o_ck", "o_cv", "o_mC", "o_mn", "o_mm", "o_mconv", "o_dS"]:
        p = np.stack([results[c][n][:, 0] for c in range(4)], 1).reshape((DEPTH, 4) + shp[n])
        s = np.concatenate([results[c][n][:, 1:] for c in range(NCORES)], 1).reshape((DEPTH, NCORES * NSS) + shp[n])
        outs += [np.ascontiguousarray(p, dtype=np.float32), np.ascontiguousarray(s, dtype=np.float32)]
    return tuple(outs)


def kernel(**inputs):
    prog = get_prog()
    maps = make_in_maps(inputs)
    res = run_bass_kernel_spmd(prog.nc, maps, core_ids=list(range(NCORES)))
    return assemble(res.results)
```

```python
import contextlib
import numpy as np
import ml_dtypes
import concourse.bass as bass
import concourse.mybir as mybir
from concourse.bass_utils import run_bass_kernel_spmd

F32 = mybir.dt.float32
BF16 = mybir.dt.bfloat16
AF = mybir.ActivationFunctionType
ALU = mybir.AluOpType
AX = mybir.AxisListType

D_MODEL = 2048
DEPTH = 2
NCORES = 8
LP = 2048
NSS = 16
LS = 8
NTOK = LP + NSS * LS
NTT = NTOK // 128
NSEQ = 1 + NSS
W = 1024
IN_COLS = 21904
OA, OB, OC, OD, OZ, OG = 0, 3200, 4480, 6544, 9616, 13712
ALPHA = (2 * DEPTH) ** 0.25
LN_EPS = 1e-5
C0 = -float(np.exp(-0.5))
NPOSROW = LP + LS


class Res:
    __slots__ = ("w", "r", "ps")

    def __init__(self, ps=False):
        self.w = None
        self.r = {}
        self.ps = ps


class V:
    __slots__ = ("res", "ap")

    def __init__(self, res, ap):
        self.res = res
        self.ap = ap

    def __getitem__(self, idx):
        return V(self.res, self.ap[idx])

    def un(self, axis):
        return V(self.res, self.ap.unsqueeze(axis))

    def bc(self, shape):
        return V(self.res, self.ap.broadcast_to(list(shape)))

    def re(self, pat, **kw):
        return V(self.res, self.ap.rearrange(pat, **kw))

    def bits(self, dt):
        return V(self.res, self.ap.bitcast(dt))


class Stream(list):
    pos = 0


class Sched:
    NDMA = 40

    def __init__(self, nc, es):
        self.nc = nc
        self.es = es
        self.eng = {"pe": nc.tensor, "act": nc.scalar, "dve": nc.vector, "pool": nc.gpsimd, "sp": nc.sync}
        self.sems = []
        self.key = {}
        for n in self.eng:
            self.key[n] = len(self.sems)
            self.sems.append(es.enter_context(nc.semaphore("e_" + n)))
        self.dkeys = []
        for i in range(self.NDMA):
            self.dkeys.append(len(self.sems))
            self.sems.append(es.enter_context(nc.semaphore("d%d" % i)))
        self.val = [0] * len(self.sems)
        self.known = {n: {} for n in self.eng}
        self.dnext = 0
        self.uid = 0
        self.ninst = 0
        self.limit = 10 ** 9
        self.rec = None
        self.bundle = None
        self.fresh_swdge = False
        self._signaled = set()
        self.mhalf = None

    def sb(self, shape, dt=F32, name=None):
        self.uid += 1
        t = self.es.enter_context(self.nc.sbuf_tensor("%s_%d" % (name or "t", self.uid), list(shape), dt))
        return V(Res(), t[tuple(slice(None) for _ in shape)])

    def sb_in(self, es, shape, dt=F32, name=None):
        self.uid += 1
        t = es.enter_context(self.nc.sbuf_tensor("%s_%d" % (name or "t", self.uid), list(shape), dt))
        return V(Res(), t[tuple(slice(None) for _ in shape)])

    def _waits(self, en, reads, writes):
        deps = {}

        def add(k, v):
            if deps.get(k, 0) < v:
                deps[k] = v
        for r in reads:
            if r is not None and r.w is not None:
                add(*r.w)
        for w in writes:
            if w is None:
                continue
            if w.w is not None:
                add(*w.w)
            for k, v in w.r.items():
                add(k, v)
        e = self.eng[en]
        kn = self.known[en]
        for k, v in deps.items():
            if en == "pe" and k == self.key["pe"]:
                continue
            if kn.get(k, 0) < v:
                e.wait_ge(self.sems[k], v)
                kn[k] = v

    def _mark(self, ev, reads, writes):
        for r in reads:
            if r is not None and r.r.get(ev[0], 0) < ev[1]:
                r.r[ev[0]] = ev[1]
        for w in writes:
            if w is not None:
                w.w = ev
                w.r = {}

    def op(self, en, fn, reads, writes):
        if self.rec is not None:
            (self.bundle if self.bundle is not None else self.rec).append((0, (en, fn, reads, writes), None))
            return
        if self.ninst >= self.limit:
            return
        writes = [x.res for x in writes if x is not None] + [x.res for x in reads if x is not None and x.res.ps]
        reads = [x.res for x in reads if x is not None and not x.res.ps]
        self._waits(en, reads, writes)
        inst = fn(self.eng[en])
        k = self.key[en]
        self.val[k] += 1
        inst.then_inc(self.sems[k], 1)
        self.ninst += 1
        self._mark((k, self.val[k]), reads, writes)

    def dma(self, q, out, in_, **kw):
        if self.rec is not None:
            (self.bundle if self.bundle is not None else self.rec).append((1, (q, out, in_), kw))
            return
        if self.ninst >= self.limit:
            return
        if kw.pop("_nc", False):
            with self.nc.allow_non_contiguous_dma(reason="tiny strided state transfer"):
                return self.dma(q, out, in_, **kw)
        reads = [in_.res]
        writes = [out.res]
        if q == "pool" and self.fresh_swdge:
            k = len(self.sems)
            self.sems.append(self.es.enter_context(self.nc.semaphore("w%d" % k)))
            self.val.append(0)
        else:
            j = self.dnext
            self.dnext = (self.dnext + 1) % self.NDMA
            k = self.dkeys[j]
        if self.known[q].get(k, 0) < self.val[k]:
            self.eng[q].wait_ge(self.sems[k], self.val[k])
            self.known[q][k] = self.val[k]
        self._waits(q, reads, writes)
        inst = self.eng[q].dma_start(out=out.ap, in_=in_.ap, **kw)
        self.val[k] += 16
        inst.then_inc(self.sems[k], 16)
        self.ninst += 1
        self._mark((k, self.val[k]), reads, writes)

    def _est(self, kind, args):
        if kind == 2:
            en0, dur, rd, wr = None, 0.0, [], []
            for k2, a2, _ in args:
                e, d, r, w = self._est(k2, a2)
                en0 = en0 or e
                dur += d
                rd += r
                wr += w
            return en0, dur, rd, wr
        if kind == 1:
            q, out, in_ = args
            n = 1
            for d in out.ap.shape:
                n *= d
            return q, 2000.0 + n * 4 / 150.0, [in_.res], [out.res]
        en, fn, reads, writes = args
        n = 64
        if writes:
            n = 1
            for d in writes[0].ap.shape[1:]:
                n *= d
        dur = {"pe": 35 + n / 2.4, "act": 200 + n / 1.2, "dve": 80 + n / 0.96, "pool": 150 + n * 2.2, "sp": 50}[en]
        return en, dur, [x.res for x in reads if x is not None], [x.res for x in writes if x is not None]

    @contextlib.contextmanager
    def atomic(self):
        if self.rec is None or self.bundle is not None:
            yield
            return
        self.bundle = []
        try:
            yield
        finally:
            b, self.bundle = self.bundle, None
            if b:
                self.rec.append((2, b, None))

    @contextlib.contextmanager
    def sub(self, name):
        if self.rec is None:
            yield
            return
        main = self.rec
        if not hasattr(main, "subs"):
            main.subs = {}
        self.rec = main.subs.setdefault(name, Stream())
        try:
            yield
        finally:
            self.rec = main

    def signal(self, tok):
        if self.rec is not None:
            self.rec.append((4, tok, None))

    def wait(self, tok):
        if self.rec is not None:
            self.rec.append((3, tok, None))

    def record(self, fn):
        old = self.rec
        self.rec = Stream()
        fn()
        out = self.rec
        self.rec = old
        return out

    def play(self, streams):
        if not hasattr(self, "_tfree"):
            self._tfree = {n: 0.0 for n in self.eng}
            self._wready = {}
            self._rdone = {}
        tfree, wready, rdone = self._tfree, self._wready, self._rdone
        streams = [s if isinstance(s, Stream) else Stream(s) for s in streams]
        for s in list(streams):
            streams += list(getattr(s, "subs", {}).values())
        sig = self._signaled
        ests = [None] * len(streams)
        while True:
            best = None
            bstart = None
            for i, s in enumerate(streams):
                while s.pos < len(s) and s[s.pos][0] in (3, 4):
                    if s[s.pos][0] == 4:
                        sig.add(s[s.pos][1])
                    elif s[s.pos][1] not in sig:
                        break
                    s.pos += 1
                if s.pos >= len(s) or s[s.pos][0] == 3:
                    continue
                if ests[i] is None:
                    kind, args, kw = s[s.pos]
                    en, dur, rd, wr = self._est(kind, args)
                    t = tfree[en]
                    for r in rd:
                        if r is None:
                            continue
                        t = max(t, wready.get(id(r), 0.0))
                        if r.ps:
                            t = max(t, rdone.get(id(r), 0.0))
                    for w in wr:
                        if w is None:
                            continue
                        t = max(t, wready.get(id(w), 0.0), rdone.get(id(w), 0.0))
                    ests[i] = (t, en, dur, rd, wr)
                if bstart is None or ests[i][0] < bstart:
                    best, bstart = i, ests[i][0]
            if best is None:
                if any(s.pos < len(s) for s in streams):
                    if any(s.pos < len(s) and (s[s.pos][0] == 4 or (s[s.pos][0] == 3 and s[s.pos][1] in sig))
                           for s in streams):
                        continue
                    raise RuntimeError("stream deadlock: every stream waits on an unsignalled token")
                break
            t, en, dur, rd, wr = ests[best]
            kind, args, kw = streams[best][streams[best].pos]
            streams[best].pos += 1
            if kind == 2:
                for k2, a2, kw2 in args:
                    if k2 == 0:
                        self.op(*a2)
                    else:
                        self.dma(*a2, **kw2)
                tfree[en] = t + dur
                tend = t + dur + 60.0
            elif kind == 0:
                self.op(*args)
                tfree[en] = t + dur
                tend = t + dur + 60.0
            else:
                self.dma(*args, **kw)
                tfree[en] = t + 60.0
                tend = t + dur
            for r in rd:
                if r is not None:
                    rdone[id(r)] = max(rdone.get(id(r), 0.0), tend)
            for w in wr:
                if w is not None:
                    wready[id(w)] = tend
            ests = [None] * len(streams)

    def barrier(self):
        for en, e in self.eng.items():
            kn = self.known[en]
            for k, v in enumerate(self.val):
                if v > 0 and kn.get(k, 0) < v:
                    e.wait_ge(self.sems[k], v)
                    kn[k] = v

    def finish(self):
        e = self.eng["sp"]
        kn = self.known["sp"]
        for k, v in enumerate(self.val):
            if v > 0 and kn.get(k, 0) < v:
                e.wait_ge(self.sems[k], v)
                kn[k] = v

    def mm(self, out, lhsT, rhs, start=True, stop=True):
        self.op("pe", lambda e: e.matmul(out.ap, lhsT=lhsT.ap, rhs=rhs.ap, start=start, stop=stop),
                [lhsT, rhs], [out])

    def tr(self, out, in_, ident):
        self.op("pe", lambda e: e.transpose(out.ap, in_.ap, ident.ap), [in_, ident], [out])

    def cp(self, en, out, in_):
        if en == "act":
            self.op(en, lambda e: e.copy(out=out.ap, in_=in_.ap), [in_], [out])
        else:
            self.op(en, lambda e: e.tensor_copy(out=out.ap, in_=in_.ap), [in_], [out])

    def tt(self, en, out, a, b, op):
        self.op(en, lambda e: e.tensor_tensor(out=out.ap, in0=a.ap, in1=b.ap, op=op), [a, b], [out])

    def ts(self, en, out, a, s1, op0, s2=None, op1=None):
        rd = [a]
        s1a = s1
        s2a = s2
        if isinstance(s1, V):
            rd.append(s1)
            s1a = s1.ap
        if isinstance(s2, V):
            rd.append(s2)
            s2a = s2.ap
        if op1 is None:
            self.op(en, lambda e: e.tensor_scalar(out=out.ap, in0=a.ap, scalar1=s1a, scalar2=None, op0=op0), rd, [out])
        else:
            self.op(en, lambda e: e.tensor_scalar(out=out.ap, in0=a.ap, scalar1=s1a, scalar2=s2a, op0=op0, op1=op1),
                    rd, [out])

    def stt(self, out, a, s, b, op0, op1):
        rd = [a, b]
        sa = s
        if isinstance(s, V):
            rd.append(s)
            sa = s.ap
        self.op("dve", lambda e: e.scalar_tensor_tensor(out=out.ap, in0=a.ap, scalar=sa, in1=b.ap, op0=op0, op1=op1),
                rd, [out])

    def act(self, out, in_, func, bias=None, scale=None, accum=None):
        rd = [in_]
        kw = {}
        if bias is not None:
            if isinstance(bias, V):
                rd.append(bias)
                kw["bias"] = bias.ap
            else:
                kw["bias"] = float(bias)
        if scale is not None:
            if isinstance(scale, V):
                rd.append(scale)
                kw["scale"] = scale.ap
            else:
                kw["scale"] = float(scale)
        wr = [out]
        if accum is not None:
            kw["accum_out"] = accum.ap
            wr.append(accum)
        self.op("act", lambda e: e.activation(out=out.ap, in_=in_.ap, func=func, **kw), rd, wr)

    def red(self, out, in_, op):
        self.op("dve", lambda e: e.tensor_reduce(out=out.ap, in_=in_.ap, axis=AX.X, op=op), [in_], [out])

    def memset(self, en, out, val):
        self.op(en, lambda e: e.memset(out.ap, val), [], [out])

    def scan(self, out, d0, d1, init, op0, op1):
        rd = [d0, d1]
        ia = init
        if isinstance(init, V):
            rd.append(init)
            ia = init.ap
        self.op("dve", lambda e: e.tensor_tensor_scan(out=out.ap, data0=d0.ap, data1=d1.ap, initial=ia, op0=op0, op1=op1),
                rd, [out])

    def recip(self, out, in_):
        self.op("dve", lambda e: e.reciprocal(out=out.ap, in_=in_.ap), [in_], [out])

    def rsqrt(self, out, in_, c, op):
        self.ts("dve", out, in_, c, op)
        if self.mhalf is None:
            self.act(out, out, AF.Sqrt)
            self.recip(out, out)
        else:
            shp = list(out.ap.shape)
            self.tt("pool", out, out, self.mhalf[:shp[0], :shp[1]], ALU.pow)


def lockstep(gens):
    gens = list(gens)
    while gens:
        nxt = []
        for g in gens:
            try:
                next(g)
                nxt.append(g)
            except StopIteration:
                pass
        gens = nxt


def dram(ap):
    return V(None, ap)


class Prog:
    def __init__(self, dbg=None):
        self.dbg = dbg or {}
        nc = bass.Bass("TRN2", target_bir_lowering=False)
        self.nc = nc
        self.es = contextlib.ExitStack()
        self.S = Sched(nc, self.es)
        self.S.limit = self.dbg.get('limit', 10 ** 9)
        self.S.fresh_swdge = bool(self.dbg.get('fresh_swdge'))
        self.declare_io()
        self.build()
        self.S.finish()
        self.es.close()

    def din(self, name, shape, dt=F32):
        return dram(self.nc.dram_tensor(name, list(shape), dt, kind="ExternalInput").ap())

    def dout(self, name, shape, dt=F32):
        return dram(self.nc.dram_tensor(name, list(shape), dt, kind="ExternalOutput").ap())

    def dscr(self, name, shape, dt=F32):
        kind = "ExternalOutput" if name in self.dbg.get("expose", ()) else "Internal"
        return dram(self.nc.dram_tensor(name, list(shape), dt, kind=kind).ap())

    def declare_io(self):
        I = {}
        I["xin"] = self.din("xin", [NTOK, D_MODEL])
        I["rS"] = self.din("rS", [DEPTH, NSEQ, 16, 64, 64])
        I["rshift"] = self.din("rshift", [DEPTH, NSEQ, 3200])
        I["ck"] = self.din("ck", [DEPTH, NSEQ, 128, 128])
        I["cv"] = self.din("cv", [DEPTH, NSEQ, 128, 128])
        I["mC"] = self.din("mC", [DEPTH, NSEQ, 8, 64, 128])
        I["mn"] = self.din("mn", [DEPTH, NSEQ, 8, 64])
        I["mm"] = self.din("mm", [DEPTH, NSEQ, 8])
        I["mconv"] = self.din("mconv", [DEPTH, NSEQ, 3, 1024])
        I["dS"] = self.din("dS", [DEPTH, NSEQ, 8, 128, 128])
        for n, s in WSHAPES.items():
            if self.dbg.get("tinyw") and n in ("w_in", "w_branch", "w_out"):
                s = [DEPTH, 128, 128]
            I[n] = self.din(n, s)
        for n, s in CONST_SHAPES.items():
            I[n] = self.din(n, s)
        self.I = I
        O = {}
        O["y"] = self.dout("y", [NTOK, D_MODEL])
        O["o_rS"] = self.dout("o_rS", [DEPTH, NSEQ, 16, 64, 64])
        O["o_rshift"] = self.dout("o_rshift", [DEPTH, NSEQ, 3200])
        O["o_ck"] = self.dout("o_ck", [DEPTH, NSEQ, 128, 128])
        O["o_cv"] = self.dout("o_cv", [DEPTH, NSEQ, 128, 128])
        O["o_mC"] = self.dout("o_mC", [DEPTH, NSEQ, 8, 64, 128])
        O["o_mn"] = self.dout("o_mn", [DEPTH, NSEQ, 8, 64])
        O["o_mm"] = self.dout("o_mm", [DEPTH, NSEQ, 8])
        O["o_mconv"] = self.dout("o_mconv", [DEPTH, NSEQ, 3, 1024])
        O["o_dS"] = self.dout("o_dS", [DEPTH, NSEQ, 8, 128, 128])
        self.O = O
        self.P = self.dscr("P", [NTOK, IN_COLS])
        self.YZ = self.dscr("YZ", [NTOK, 4 * W])
        self.MG = self.dscr("MG", [NTOK, D_MODEL], BF16)
        self.X1 = self.dscr("X1", [NTOK, D_MODEL])

    def build(self):
        S = self.S
        es = self.es
        self.PS = []
        for i in range(8):
            t = es.enter_context(self.nc.psum_tensor("ps%d" % i, [128, 512], F32))
            self.PS.append(V(Res(ps=True), t[:, :]))
        self.ident = S.sb([128, 128], F32, "ident")
        S.dma("sp", self.ident, self.I["c_ident"])
        self.identb = S.sb([128, 128], BF16, "identb")
        S.cp("dve", self.identb, self.ident)
        if not self.dbg.get("nopow"):
            mh = S.sb([128, 16], F32, "mhalf")
            S.memset("pool", mh, -0.5)
            S.mhalf = mh
        nl = self.dbg.get("layers", DEPTH)
        for l in range(nl):
            xsrc = self.I["xin"] if l == 0 else self.X1
            xdst = self.O["y"] if l == nl - 1 else self.X1
            self.b_done = False
            if self.dbg.get("p1", True):
                self.phase1(l, xsrc)
            S.barrier()
            if self.dbg.get("p2", True):
                self.phase2(l)
            S.barrier()
            if self.dbg.get("p3", True):
                self.phase3a(l)
                S.barrier()
                self.phase3b(l, xsrc, xdst)
            S.barrier()

    def phase1_setup(self, l, xsrc, es):
        S = self.S
        XT = S.sb_in(es, [128, 16, NTOK], BF16, "XT")
        with contextlib.ExitStack() as es2:
            xrow = [S.sb_in(es2, [128, D_MODEL], F32, "xrow") for _ in range(2)]
            for tt in range(NTT):
                xr = xrow[tt % 2]
                S.dma("sp", xr, xsrc[tt * 128:(tt + 1) * 128, :])
                for g in range(4):
                    ps = self.PS[(tt * 4 + g) % 8]
                    for j in range(4):
                        k = g * 4 + j
                        S.tr(ps[:, j * 128:(j + 1) * 128], xr[:, k * 128:(k + 1) * 128], self.ident)
                    S.cp("act" if g % 2 else "dve", XT[:, g * 4:(g + 1) * 4, tt * 128:(tt + 1) * 128],
                         ps.re("p (j c) -> p j c", j=4))
            S.barrier()
        WB = [S.sb_in(es, [128, 16, 512], BF16, "WB") for _ in range(3)]
        stg = [S.sb_in(es, [128, 512], F32, "stg") for _ in range(4)]
        return dict(XT=XT, WB=WB, stg=stg, n=0, c=0)

    @staticmethod
    def col_tiles(a, b):
        n = -(-(b - a) // 512)
        w = -(-(b - a) // n)
        w = -(-w // 8) * 8
        out = []
        while a < b:
            out.append((a, min(w, b - a)))
            a += w
        return out

    def phase1_cols(self, ctx, l, tiles, banks):
        S = self.S
        XT, WB, stg = ctx["XT"], ctx["WB"], ctx["stg"]
        wv = self.I["w_in"][l].re("(k p) c -> p k c", p=128)
        for (c0, cw) in tiles:
            wb = WB[ctx["c"] % 3]
            ctx["c"] += 1
            S.dma("pool", wb[:, :, :cw], wv[:, :, c0:c0 + cw])
            for tt in range(NTT):
                n = ctx["n"]
                ctx["n"] += 1
                ps = banks[n % len(banks)]
                with S.atomic():
                    for k in range(16):
                        S.mm(ps[:, :cw], XT[:, k, tt * 128:(tt + 1) * 128], wb[:, k, :cw], start=(k == 0), stop=(k == 15))
                sg = stg[n % 4]
                S.cp("act" if n % 2 else "dve", sg[:, :cw], ps[:, :cw])
                S.dma("sp", self.P[tt * 128:(tt + 1) * 128, c0:c0 + cw], sg[:, :cw])

    def phase1(self, l, xsrc):
        S = self.S
        first = self.col_tiles(OB, OC) + self.col_tiles(OZ + W, OZ + 2 * W)
        rest = self.col_tiles(0, OB) + self.col_tiles(OC, OZ + W) + self.col_tiles(OZ + 2 * W, IN_COLS)
        if self.dbg.get("ncol") is not None:
            rest = rest[:self.dbg["ncol"]]
        with contextlib.ExitStack() as es:
            ctx = self.phase1_setup(l, xsrc, es)
            self.phase1_cols(ctx, l, first, self.PS)
            S.barrier()
            if not self.dbg.get("p2", True) or "B" not in self.dbg.get("branches", "ABCD") or self.dbg.get("nooverlapB"):
                self.phase1_cols(ctx, l, rest, self.PS)
                self.b_done = False
                return
            with contextlib.ExitStack() as esB:
                sB = S.record(lambda: self.branchB(l, esB))
                sP = S.record(lambda: self.phase1_cols(ctx, l, rest, [self.PS[6], self.PS[7]]))
                S.play([sP, sB])
            self.b_done = True

    def seqs(self):
        out = [(0, LP, 64, 0, 0)]
        for j in range(NSS):
            out.append((LP + j * LS, LS, LS, j + 1, LP))
        ns = self.dbg.get("nseq", len(out))
        return out[:ns]

    def bload(self, es, src_row, n=W, rows=64):
        t = self.S.sb_in(es, [rows, n], F32, "bc")
        self.S.dma("sp", t, src_row.bc([rows, n]))
        return t

    def headnorm(self, x, T, H, dv, gt, bt, sc, on_act=False):
        S = self.S
        x3 = x[:T, :].re("p (h d) -> p h d", h=H)
        sq, ssum, ssq, mean, msq, var, rstd = sc["sq"], sc["s0"], sc["s1"], sc["s2"], sc["s3"], sc["s4"], sc["s5"]
        S.red(ssum[:T, :H], x3, ALU.add)
        S.act(sq[:T, :], x[:T, :], AF.Square)
        S.red(ssq[:T, :H], sq[:T, :].re("p (h d) -> p h d", h=H), ALU.add)
        S.ts("dve", mean[:T, :H], ssum[:T, :H], 1.0 / dv, ALU.mult)
        S.tt("dve", msq[:T, :H], mean[:T, :H], mean[:T, :H], ALU.mult)
        S.stt(var[:T, :H], ssq[:T, :H], 1.0 / dv, msq[:T, :H], ALU.mult, ALU.subtract)
        S.rsqrt(rstd[:T, :H], var[:T, :H], LN_EPS, ALU.add)
        if on_act:
            S.stt(msq[:T, :H], mean[:T, :H], -1.0, rstd[:T, :H], ALU.mult, ALU.mult)
            for h in range(H):
                S.act(x3[:, h, :], x3[:, h, :], AF.Identity, bias=msq[:T, h:h + 1], scale=rstd[:T, h:h + 1])
        else:
            S.tt("pool", x3, x3, mean[:T, :H].un(2).bc([T, H, dv]), ALU.subtract)
            S.tt("pool", x3, x3, rstd[:T, :H].un(2).bc([T, H, dv]), ALU.mult)
        S.tt("dve", x[:T, :], x[:T, :], gt[:T, :], ALU.mult)
        S.tt("pool", x[:T, :], x[:T, :], bt[:T, :], ALU.add)

    def gate_store(self, x, z, T, t0, bi):
        S = self.S
        S.act(z[:T, :], z[:T, :], AF.Silu)
        S.tt("dve", x[:T, :], x[:T, :], z[:T, :], ALU.mult)
        S.dma("sp", self.YZ[t0:t0 + T, bi * W:(bi + 1) * W], x[:T, :])

    def rope(self, out, x, cos, sin, T, nh, hd, t1, t2):
        S = self.S
        n = nh * hd
        t1 = out
        x4 = x[:T, :n].re("p (h two d) -> p h two d", h=nh, two=2)
        c4 = cos[:T, :hd].re("p (two d) -> p two d", two=2).un(1).bc([T, nh, 2, hd // 2])
        s4 = sin[:T, :hd].re("p (two d) -> p two d", two=2).un(1).bc([T, nh, 2, hd // 2])
        S.tt("pool", t1[:T, :n].re("p (h two d) -> p h two d", h=nh, two=2), x4, c4, ALU.mult)
        o4 = t2[:T, :n].re("p (h two d) -> p h two d", h=nh, two=2)
        S.tt("dve", o4[:, :, 0, :], x4[:, :, 1, :], s4[:, :, 0, :], ALU.mult)
        S.tt("dve", o4[:, :, 1, :], x4[:, :, 0, :], s4[:, :, 1, :], ALU.mult)
        S.tt("pool", out[:T, :n], t1[:T, :n], t2[:T, :n], ALU.add)

    def small(self, es, n=8, cols=16):
        return {"s%d" % i: self.S.sb_in(es, [128, cols], F32, "sm") for i in range(n)}

    def phase2(self, l):
        S = self.S
        br = self.dbg.get("branches", "ABCD")
        P_ = self.PS
        self.psmap = {}
        for name in br:
            if name in "CD" and "C" in br and "D" in br and not self.dbg.get("nointer"):
                continue
            if name == "B" and getattr(self, "b_done", False):
                continue
            with contextlib.ExitStack() as es:
                getattr(self, "branch" + name)(l, es)
            S.barrier()
        if "C" in br and "D" in br and not self.dbg.get("nointer"):
            self.psmap["D"] = [P_[0], P_[1], P_[0], P_[0], P_[1], P_[2], P_[3], None]
            self.psmap["C"] = [P_[4], P_[4], P_[4], P_[5], P_[6], P_[7], P_[5], P_[6]]
            with contextlib.ExitStack() as es:
                recs = []
                for name in "CD":
                    S.rec = Stream()
                    getattr(self, "branch" + name)(l, es)
                    recs.append(S.rec)
                S.rec = None
                S.play(recs)
            self.psmap = {}
            S.barrier()

    def branchD(self, l, es):
        S, I, O, PS = self.S, self.I, self.O, self.psmap.get("D", self.PS)
        sb = lambda shape, dt=F32, n="d": S.sb_in(es, shape, dt, n)
        dmt = sb([64, 8, 64]); S.dma("sp", dmt, I["c_dmt"])
        qdec = sb([128, 8, 64]); S.dma("sp", qdec, I["c_qdec"])
        kdec = {64: sb([64, W]), 8: sb([64, W])}
        cdec = {64: sb([128, W]), 8: sb([128, W])}
        for T in (64, 8):
            S.dma("sp", kdec[T], I["c_kdec%d" % T]); S.dma("sp", cdec[T], I["c_cdec%d" % T])
        gng = self.bload(es, I["d_ln_g"][l:l + 1, :]); gnb = self.bload(es, I["d_ln_b"][l:l + 1, :])
        Sf = sb([128, 8, 128]); Sb = sb([128, 8, 128], BF16)
        PD = sb([64, 3072]); cosT = sb([64, 128]); sinT = sb([64, 128]); zz = [sb([64, W]) for _ in range(3)]
        t1 = None; t2 = sb([64, 2048]); qkr = sb([64, 2048])
        vb = sb([64, 8, 128], BF16); ktb = sb([64, 8, 128], BF16)
        qT = sb([128, 8, 64], BF16); qdT = sb([128, 8, 64], BF16); kT = sb([128, 8, 64], BF16)
        inT = sb([64, 8, 64], BF16)
        os_ = [sb([64, W]) for _ in range(2)]
        sc = self.small(es); sc["sq"] = sb([64, W])
        items = []
        for (tok0, L, T, sidx, prow0) in self.seqs():
            nch = min(L // T, self.dbg.get('maxch', 10 ** 9))
            for ci in range(nch):
                items.append((tok0, L, T, sidx, prow0, ci, ci == 0, ci == nch - 1))

        def loads(k):
            tok0, L, T, sidx, prow0, ci, first, last = items[k]
            t0 = tok0 + ci * T
            pr = prow0 + ci * T
            S.dma("sp", PD[:T, :], self.P[t0:t0 + T, OD:OD + 3072])
            S.dma("sp", cosT[:T, :], I["c_rdc"][pr:pr + T, :])
            S.dma("sp", sinT[:T, :], I["c_rds"][pr:pr + T, :])
            S.dma("sp", zz[k % 3][:T, :], self.P[t0:t0 + T, OZ + 3 * W:OZ + 4 * W])

        loads(0)
        for k, (tok0, L, T, sidx, prow0, ci, first, last) in enumerate(items):
            t0 = tok0 + ci * T
            z = zz[k % 3]
            o = os_[k % 2]
            if k >= 2:
                S.wait(("Ddone", l, k - 2))
            if first:
                S.dma("sp", Sf, I["dS"][l, sidx].re("h d v -> d h v"))
                S.cp("act", Sb, Sf)
            self.rope(qkr, PD, cosT, sinT, T, 16, 128, t1, t2)
            S.cp("act", vb[:T].re("p h d -> p (h d)"), PD[:T, 2048:3072])
            S.tt("pool", ktb[:T].re("p h d -> p (h d)"), qkr[:T, W:2 * W], kdec[T][:T, :], ALU.mult)
            for h in range(8):
                S.tr(PS[0][:, h * T:(h + 1) * T], qkr[:T, h * 128:(h + 1) * 128], self.ident[:T, :T])
                S.tr(PS[1][:, h * T:(h + 1) * T], qkr[:T, W + h * 128:W + (h + 1) * 128], self.ident[:T, :T])
            q3 = PS[0][:, :8 * T].re("p (h t) -> p h t", h=8)
            S.cp("act", qT[:, :, :T], q3)
            S.tt("dve", qdT[:, :, :T], q3, qdec[:, :, :T], ALU.mult)
            S.cp("act", kT[:, :, :T], PS[1][:, :8 * T].re("p (h t) -> p h t", h=8))
            if k + 1 < len(items):
                loads(k + 1)
            for h in range(8):
                S.mm(PS[2][:T, h * T:(h + 1) * T], kT[:, h, :T], qT[:, h, :T])
            S.tt("dve", inT[:T, :, :T], PS[2][:T, :8 * T].re("p (h t) -> p h t", h=8), dmt[:T, :, :T], ALU.mult)
            for h in range(8):
                pso = PS[3 + h // 4][:T, (h % 4) * 128:(h % 4 + 1) * 128]
                with S.atomic():
                    S.mm(pso, inT[:T, h, :T], vb[:T, h, :], start=True, stop=False)
                    S.mm(pso, qdT[:, h, :T], Sb[:, h, :], start=False, stop=True)
            S.cp("act", o[:T, 0:512], PS[3][:T, :])
            S.cp("act", o[:T, 512:1024], PS[4][:T, :])
            for h in range(8):
                S.mm(PS[5 + h // 4][:, (h % 4) * 128:(h % 4 + 1) * 128], ktb[:T, h, :], vb[:T, h, :])
            Sf2 = Sf.re("p h d -> p (h d)")
            S.tt("pool", Sf2, Sf2, cdec[T], ALU.mult)
            S.tt("dve", Sf2[:, 0:512], Sf2[:, 0:512], PS[5], ALU.add)
            S.tt("dve", Sf2[:, 512:1024], Sf2[:, 512:1024], PS[6], ALU.add)
            S.cp("act", Sb, Sf)
            S.signal(("Dmain", l, k))
            with S.sub("post"):
                S.wait(("Dmain", l, k))
                self.headnorm(o, T, 8, 128, gng, gnb, sc, on_act=True)
                self.gate_store(o, z, T, t0, 3)
                S.signal(("Ddone", l, k))
            if last:
                S.dma("sp", O["o_dS"][l, sidx].re("h d v -> d h v"), Sf)

    def branchB(self, l, es):
        S, I, O, PS = self.S, self.I, self.O, self.PS
        sb = lambda shape, dt=F32, n="b": S.sb_in(es, shape, dt, n)
        mp = sb([128, 256]); S.dma("sp", mp, I["c_mp"])
        mp0 = sb([128, 256]); S.dma("sp", mp0, I["c_mp0"])
        ms = sb([128, 256]); S.dma("sp", ms, I["c_ms"])
        sink = self.bload(es, I["b_sinks"][l:l + 1, :], 16, 128)
        kTp = [sb([64, 2, 128], BF16) for _ in range(2)]
        vp = [sb([128, 2, 64], BF16) for _ in range(2)]
        ckf = sb([128, 128]); ckb = sb([128, 128], BF16)
        PB = sb([128, 1280]); cosT = sb([128, 64]); sinT = sb([128, 64]); zz = [sb([128, W]) for _ in range(2)]
        t1 = None; t2 = sb([128, 1152]); qkr = sb([128, 1152]); qkb = sb([128, 1152], BF16)
        qT = sb([64, 16, 128], BF16)
        s_sbs = [sb([128, 2, 256]) for _ in range(4)]; p_bfs = [sb([128, 2, 256], BF16) for _ in range(4)]
        pTs = [sb([128, 4, 128], BF16) for _ in range(4)]
        sms = [self.small(es, 6, 2) for _ in range(4)]
        rden = sb([128, 16])
        yb = sb([128, W])
        bitems = []
        for (tok0, L, T, sidx, prow0) in self.seqs():
            Lb = 128 if L % 128 == 0 else L
            for bi in range(min(L // Lb, self.dbg.get('maxch', 10 ** 9))):
                bitems.append((tok0 + bi * Lb, prow0 + bi * Lb, Lb))

        def loadsB(k):
            t0_, pr_, Lb_ = bitems[k]
            S.dma("sp", PB[:Lb_, :], self.P[t0_:t0_ + Lb_, OB:OB + 1280])
            S.dma("sp", cosT[:Lb_, :], I["c_rbc"][pr_:pr_ + Lb_, :])
            S.dma("sp", sinT[:Lb_, :], I["c_rbs"][pr_:pr_ + Lb_, :])
            S.dma("sp", zz[k % 2][:Lb_, :], self.P[t0_:t0_ + Lb_, OZ + W:OZ + 2 * W])

        loadsB(0)
        kB = 0
        for (tok0, L, T, sidx, prow0) in self.seqs():
            Lb = 128 if L % 128 == 0 else L
            nb = L // Lb
            prompt = (sidx == 0)
            if prompt:
                S.memset("pool", kTp[0], 0.0)
                S.memset("pool", vp[0], 0.0)
            else:
                S.dma("sp", ckf, I["ck"][l, sidx])
                S.cp("act", ckb, ckf)
                psb = PS[0].bits(BF16)
                for g in range(2):
                    S.tr(psb[:64, g * 128:(g + 1) * 128], ckb[:, g * 64:(g + 1) * 64], self.identb)
                S.cp("dve", kTp[0], psb[:64, 0:256].re("p (g t) -> p g t", g=2))
                S.dma("sp", ckf, I["cv"][l, sidx])
                S.cp("act", vp[0].re("p g d -> p (g d)"), ckf)
            for bi in range(min(nb, self.dbg.get('maxch', 10 ** 9))):
                t0 = tok0 + bi * Lb
                pr = prow0 + bi * Lb
                prev, cur = (bi % 2, (bi + 1) % 2)
                Wk = 128 + Lb
                mask = (mp0 if bi == 0 else mp) if prompt else ms
                z = zz[kB % 2]
                self.rope(qkr, PB, cosT, sinT, Lb, 18, 64, t1, t2)
                S.cp("act", qkb[:Lb, :], qkr[:Lb, :])
                S.cp("act", vp[cur][:Lb].re("p g d -> p (g d)"), PB[:Lb, 1152:1280])
                psq = [PS[0].bits(BF16), PS[1].bits(BF16), PS[2].bits(BF16)]
                for h in range(18):
                    S.tr(psq[h // 8][:64, (h % 8) * 128:(h % 8) * 128 + Lb], qkb[:Lb, h * 64:(h + 1) * 64],
                         self.identb[:Lb, :Lb])
                for g in range(2):
                    S.cp("act" if g else "dve", qT[:, g * 8:(g + 1) * 8, :Lb],
                         psq[g][:64, :].re("p (h t) -> p h t", h=8)[:, :, :Lb])
                S.cp("dve", kTp[cur][:, :, :Lb], psq[2][:64, 0:256].re("p (g t) -> p g t", g=2)[:, :, :Lb])
                sbanks = [PS[0], PS[1], PS[3], PS[4]]

                def pair_chain(hp, slot):
                    g = hp // 4
                    pss = sbanks[slot]
                    s_sb, p_bf, pT, sm = s_sbs[slot], p_bfs[slot], pTs[slot], sms[slot]
                    for j in range(2):
                        h = hp * 2 + j
                        S.mm(pss[:Lb, j * 256:j * 256 + 128], qT[:, h, :Lb], kTp[prev][:, g, :])
                        S.mm(pss[:Lb, j * 256 + 128:j * 256 + 128 + Lb], qT[:, h, :Lb], kTp[cur][:, g, :Lb])
                    yield
                    ps3 = pss[:Lb, :].re("p (j c) -> p j c", j=2)[:, :, :Wk]
                    S.stt(s_sb[:Lb, :, :Wk], ps3, 0.125, mask[:Lb, :Wk].un(1).bc([Lb, 2, Wk]), ALU.mult, ALU.add)
                    yield
                    mx, m, negm, rs, dd, den = sm["s0"], sm["s1"], sm["s2"], sm["s3"], sm["s4"], sm["s5"]
                    S.red(mx[:Lb, :], s_sb[:Lb, :, :Wk], ALU.max)
                    yield
                    S.tt("dve", m[:Lb, :], mx[:Lb, :], sink[:Lb, hp * 2:hp * 2 + 2], ALU.max)
                    yield
                    S.ts("dve", negm[:Lb, :], m[:Lb, :], -1.0, ALU.mult)
                    S.tt("pool", dd[:Lb, :], sink[:Lb, hp * 2:hp * 2 + 2], m[:Lb, :], ALU.subtract)
                    yield
                    for j in range(2):
                        S.act(p_bf[:Lb, j, :Wk], s_sb[:Lb, j, :Wk], AF.Exp, bias=negm[:Lb, j:j + 1], accum=rs[:Lb, j:j + 1])
                    S.act(dd[:Lb, :], dd[:Lb, :], AF.Exp)
                    yield
                    pst = sbanks[slot].bits(BF16)
                    po = 0
                    for j in range(2):
                        S.tr(pst[:128, po + (2 * j) * 128:po + (2 * j) * 128 + Lb], p_bf[:Lb, j, 0:128], self.identb[:Lb, :Lb])
                        S.tr(pst[:Lb, po + (2 * j + 1) * 128:po + (2 * j + 1) * 128 + Lb], p_bf[:Lb, j, 128:128 + Lb],
                             self.identb[:Lb, :Lb])
                    S.tt("dve", den[:Lb, :], rs[:Lb, :], dd[:Lb, :], ALU.add)
                    yield
                    S.cp("act", pT[:, :, :Lb], pst[:, po:po + 512].re("p (s t) -> p s t", s=4)[:, :, :Lb])
                    S.recip(rden[:Lb, hp * 2:hp * 2 + 2], den[:Lb, :])
                    yield
                    for j in range(2):
                        h = hp * 2 + j
                        pso = PS[5 if h >= 8 else 2][:Lb, (h % 8) * 64:(h % 8 + 1) * 64]
                        S.mm(pso, pT[:, 2 * j, :Lb], vp[prev][:, g, :], start=True, stop=False)
                        S.mm(pso, pT[:Lb, 2 * j + 1, :Lb], vp[cur][:Lb, g, :], start=False, stop=True)
                    yield

                for hp0 in range(0, 8, 4):
                    lockstep([pair_chain(hp0 + s, s) for s in range(4)])
                for hh in range(2):
                    S.tt("dve", yb[:Lb, hh * 512:(hh + 1) * 512].re("p (h d) -> p h d", h=8),
                         PS[5 if hh else 2][:Lb, :].re("p (h d) -> p h d", h=8),
                         rden[:Lb, hh * 8:(hh + 1) * 8].un(2).bc([Lb, 8, 64]), ALU.mult)
                if bi == nb - 1:
                    if prompt:
                        S.dma("sp", O["o_ck"][l, sidx], qkr[:, 1024:1152])
                        S.dma("sp", O["o_cv"][l, sidx], PB[:, 1152:1280])
                    else:
                        S.dma("sp", O["o_ck"][l, sidx][0:128 - Lb, :], I["ck"][l, sidx][Lb:128, :])
                        S.dma("sp", O["o_cv"][l, sidx][0:128 - Lb, :], I["cv"][l, sidx][Lb:128, :])
                        S.dma("sp", O["o_ck"][l, sidx][128 - Lb:128, :], qkr[:Lb, 1024:1152])
                        S.dma("sp", O["o_cv"][l, sidx][128 - Lb:128, :], PB[:Lb, 1152:1280])
                kB += 1
                if kB < len(bitems):
                    loadsB(kB)
                self.gate_store(yb, z, Lb, t0, 1)

    def branchC(self, l, es):
        S, I, O, PS = self.S, self.I, self.O, self.psmap.get("C", self.PS)
        sb = lambda shape, dt=F32, n="c": S.sb_in(es, shape, dt, n)
        tri = sb([64, 64]); S.dma("sp", tri, I["c_tri"])
        madd = sb([64, 64]); S.dma("sp", madd, I["c_madd"])
        cw = [self.bload(es, I["c_conv_w"][l, j:j + 1, :]) for j in range(4)]
        cb = self.bload(es, I["c_conv_b"][l:l + 1, :])
        gng = self.bload(es, I["c_ln_g"][l:l + 1, :]); gnb = self.bload(es, I["c_ln_b"][l:l + 1, :])
        ibias = sb([8, 1]); S.dma("sp", ibias, I["c_i_bias"][l].re("(h o) -> h o", o=1))
        fbias = sb([8, 1]); S.dma("sp", fbias, I["c_f_bias"][l].re("(h o) -> h o", o=1))
        negfb = sb([8, 1]); S.ts("dve", negfb, fbias, -1.0, ALU.mult)
        ones8 = sb([8, 64]); S.memset("pool", ones8, 1.0)
        zeros8 = sb([8, 64]); S.memset("pool", zeros8, 0.0)
        onesb = sb([64, 1], BF16); S.memset("pool", onesb, 1.0)
        Cf = sb([64, 8, 128]); Cb = sb([64, 8, 128], BF16)
        nf = sb([64, 8]); nb_ = sb([64, 8], BF16)
        mfm = sb([8, 1])
        U = [sb([64, W]) for _ in range(4)]
        Vt = sb([64, W]); gates = sb([64, 16]); zz = [sb([64, W]) for _ in range(3)]
        conv = sb([64, W]); tmp = sb([64, W])
        FR = sb([8, 8, 64]); tf1 = sb([8, 64]); tf2 = sb([8, 64]); negMl = sb([8, 1]); dg = sb([8, 8])
        BD = sb([8, 8, 64]); tm = sb([64, 4, 8])
        wpre = sb([64, 8, 64]); wts = sb([64, 8, 64])
        vb = sb([64, 8, 128], BF16); qT = sb([64, 8, 64], BF16); kT = sb([64, 8, 64], BF16)
        AT = sb([64, 8, 64], BF16); ktb = sb([64, 8, 64], BF16)
        n1s = [sb([64, W]) for _ in range(2)]; dens = [sb([128, 16]) for _ in range(2)]; dq = sb([64, 16]); sR = sb([64, 8])
        sc = self.small(es); sc["sq"] = sb([64, W])
        id8 = self.ident[:8, :8]
        citems = []
        for (tok0, L, T, sidx, prow0) in self.seqs():
            for ci in range(min(L // T, self.dbg.get('maxch', 10 ** 9))):
                citems.append((tok0, T, sidx, ci))

        def loadsC(k):
            tok0_, T_, sidx_, ci_ = citems[k]
            t0_ = tok0_ + ci_ * T_
            for j in range(4):
                sh = 3 - j
                if ci_ == 0:
                    if sh > 0:
                        S.dma("sp", U[j][0:sh, :], I["mconv"][l, sidx_][j:3, :])
                    S.dma("sp", U[j][sh:T_, :], self.P[t0_:t0_ + T_ - sh, OC:OC + W])
                else:
                    S.dma("sp", U[j][:T_, :], self.P[t0_ - sh:t0_ - sh + T_, OC:OC + W])
            S.dma("sp", Vt[:T_, :], self.P[t0_:t0_ + T_, OC + W:OC + 2 * W])
            S.dma("sp", gates[:T_, :], self.P[t0_:t0_ + T_, OC + 2 * W:OC + 2 * W + 16])
            S.dma("sp", zz[k % 3][:T_, :], self.P[t0_:t0_ + T_, OZ + 2 * W:OZ + 3 * W])

        loadsC(0)
        kC = 0
        for (tok0, L, T, sidx, prow0) in self.seqs():
            S.dma("sp", Cf, I["mC"][l, sidx].re("h d v -> d h v"))
            S.cp("act", Cb, Cf)
            S.dma("sp", nf, I["mn"][l, sidx].re("h d -> d h"), _nc=True)
            S.cp("act", nb_, nf)
            S.dma("sp", mfm, I["mm"][l, sidx].re("(h o) -> h o", o=1))
            for ci in range(min(L // T, self.dbg.get('maxch', 10 ** 9))):
                t0 = tok0 + ci * T
                z = zz[kC % 3]
                n1 = n1s[kC % 2]
                kcur = kC
                if kC >= 2:
                    S.wait(("Cdone", l, kC - 2))
                S.tt("pool", conv[:T, :], U[0][:T, :], cw[0][:T, :], ALU.mult)
                for j in range(1, 4):
                    S.tt("dve" if j % 2 else "pool", tmp[:T, :], U[j][:T, :], cw[j][:T, :], ALU.mult)
                    S.tt("pool", conv[:T, :], conv[:T, :], tmp[:T, :], ALU.add)
                S.tt("dve", conv[:T, :], conv[:T, :], cb[:T, :], ALU.add)
                S.act(conv[:T, :], conv[:T, :], AF.Silu)
                S.cp("act", vb[:T].re("p h d -> p (h d)"), Vt[:T, :])
                S.tr(PS[0][:8, 0:T], gates[:T, 0:8], self.ident[:T, :T])
                S.tr(PS[0][:8, T:2 * T], gates[:T, 8:16], self.ident[:T, :T])
                li, lf, bb, gg, MM, scf, emt, wj = [FR[:, i, :T] for i in range(8)]
                S.ts("dve", li, PS[0][:8, 0:T], ibias, ALU.add)
                S.act(tf1[:, :T], PS[0][:8, T:2 * T], AF.Exp, bias=negfb, scale=-1.0)
                S.act(tf2[:, :T], tf1[:, :T], AF.Ln, bias=1.0)
                S.ts("pool", lf, tf2[:, :T], -1.0, ALU.mult)
                S.scan(bb, lf, zeros8[:, :T], 0.0, ALU.add, ALU.add)
                S.tt("dve", gg, li, bb, ALU.subtract)
                S.scan(MM, gg, gg, mfm, ALU.max, ALU.max)
                S.act(scf, MM, AF.Exp, bias=mfm, scale=-1.0)
                S.tt("dve", tf1[:, :T], bb, MM, ALU.add)
                S.act(emt, tf1[:, :T], AF.Exp, scale=-1.0)
                S.ts("dve", negMl, FR[:, 4, T - 1:T], -1.0, ALU.mult)
                S.act(wj, gg, AF.Exp, bias=negMl)
                for i, src in enumerate((gg, scf, emt, wj)):
                    S.tr(PS[1][:T, i * 8:(i + 1) * 8], src, id8)
                S.cp("dve", tm[:T].re("p a h -> p (a h)"), PS[1][:T, 0:32])
                g_tm, sc_tm, emt_tm, wj_tm = [tm[:T, i, :] for i in range(4)]
                S.tt("pool", BD[:, :, :T], MM.un(1).bc([8, 8, T]), id8.un(2).bc([8, 8, T]), ALU.mult)
                S.mm(PS[2][:T, :8 * T], ones8[:, :T], BD[:, :, :T])
                S.stt(wpre[:T, :, :T], PS[2][:T, :8 * T].re("p (h t) -> p h t", h=8), -1.0,
                      madd[:T, :T].un(1).bc([T, 8, T]), ALU.mult, ALU.add)
                for h in range(8):
                    S.act(wts[:T, h, :T], wpre[:T, h, :T], AF.Exp, bias=g_tm[:, h:h + 1])
                for h in range(8):
                    S.tr(PS[3][:64, h * T:(h + 1) * T], conv[:T, h * 64:(h + 1) * 64], self.ident[:T, :T])
                    S.tr(PS[4][:64, h * T:(h + 1) * T], conv[:T, 512 + h * 64:512 + (h + 1) * 64], self.ident[:T, :T])
                S.cp("act", qT[:, :, :T], PS[3][:64, :8 * T].re("p (h t) -> p h t", h=8))
                S.cp("dve", kT[:, :, :T], PS[4][:64, :8 * T].re("p (h t) -> p h t", h=8))
                kC += 1
                if kC < len(citems):
                    loadsC(kC)
                for h in range(8):
                    S.mm(PS[5][:T, h * T:(h + 1) * T], kT[:, h, :T], qT[:, h, :T])
                S.stt(AT[:T, :, :T], PS[5][:T, :8 * T].re("p (h t) -> p h t", h=8), 0.125, wts[:T, :, :T], ALU.mult, ALU.mult)
                for h in range(8):
                    S.mm(PS[6 + h // 4][:T, (h % 4) * 128:(h % 4 + 1) * 128], AT[:T, h, :T], vb[:T, h, :])
                    S.mm(PS[0][:T, h:h + 1], AT[:T, h, :T], onesb[:T, :])
                    S.mm(PS[0][:T, 8 + h:9 + h], qT[:, h, :T], nb_[:, h:h + 1])
                S.cp("act", n1[:T, 0:512], PS[6][:T, :])
                S.cp("act", n1[:T, 512:1024], PS[7][:T, :])
                S.cp("dve", dq[:T, :], PS[0][:T, 0:16])
                for h in range(8):
                    S.mm(PS[6 + h // 4][:T, (h % 4) * 128:(h % 4 + 1) * 128], qT[:, h, :T], Cb[:, h, :])
                den, aden = dens[kcur % 2], sc["s7"]
                S.tt("dve", den[:T, :8], sc_tm, dq[:T, 8:16], ALU.mult)
                S.tt("dve", den[:T, :8], den[:T, :8], dq[:T, 0:8], ALU.add)
                S.ts("dve", aden[:T, :8], den[:T, :8], -1.0, ALU.mult)
                S.tt("dve", aden[:T, :8], aden[:T, :8], den[:T, :8], ALU.max)
                S.tt("dve", aden[:T, :8], aden[:T, :8], emt_tm, ALU.max)
                S.recip(den[:T, :8], aden[:T, :8])
                for hh in range(2):
                    S.tt("dve", tmp[:T, hh * 512:(hh + 1) * 512].re("p (h d) -> p h d", h=4),
                         PS[6 + hh][:T, :].re("p (h d) -> p h d", h=4),
                         sc_tm[:, hh * 4:(hh + 1) * 4].un(2).bc([T, 4, 128]), ALU.mult)
                S.tt("pool", n1[:T, :], n1[:T, :], tmp[:T, :], ALU.add)
                n13 = n1[:T, :].re("p (h d) -> p h d", h=8)
                S.tt("pool", n13, n13, den[:T, :8].un(2).bc([T, 8, 128]), ALU.mult)
                S.stt(ktb[:T], conv[:T, 512:1024].re("p (h d) -> p h d", h=8), 0.125,
                      wj_tm.un(2).bc([T, 8, 64]), ALU.mult, ALU.mult)
                for h in range(8):
                    S.mm(PS[3 + h // 4][:64, (h % 4) * 128:(h % 4 + 1) * 128], ktb[:T, h, :], vb[:T, h, :])
                    S.mm(PS[5][:64, h:h + 1], ktb[:T, h, :], onesb[:T, :])
                S.ts("dve", dg, id8, FR[:, 5, T - 1:T], ALU.mult)
                S.mm(PS[5][:64, 16:24], ones8[:, :64], dg)
                S.cp("dve", sR, PS[5][:64, 16:24])
                S.tt("pool", Cf, Cf, sR.un(2).bc([64, 8, 128]), ALU.mult)
                Cf2 = Cf.re("p h d -> p (h d)")
                S.tt("dve", Cf2[:, 0:512], Cf2[:, 0:512], PS[3][:64, :], ALU.add)
                S.tt("dve", Cf2[:, 512:1024], Cf2[:, 512:1024], PS[4][:64, :], ALU.add)
                S.cp("act", Cb, Cf)
                S.tt("dve", nf, nf, sR, ALU.mult)
                S.tt("dve", nf, nf, PS[5][:64, 0:8], ALU.add)
                S.cp("act", nb_, nf)
                S.tt("dve", mfm, FR[:, 2, T - 1:T], FR[:, 4, T - 1:T], ALU.add)
                S.signal(("Cmain", l, kcur))
                with S.sub("post"):
                    S.wait(("Cmain", l, kcur))
                    self.headnorm(n1, T, 8, 128, gng, gnb, sc, on_act=True)
                    self.gate_store(n1, z, T, t0, 2)
                    S.signal(("Cdone", l, kcur))
            S.dma("sp", O["o_mC"][l, sidx].re("h d v -> d h v"), Cf)
            S.dma("sp", O["o_mn"][l, sidx].re("h d -> d h"), nf, _nc=True)
            S.dma("sp", O["o_mm"][l, sidx].re("(h o) -> h o", o=1), mfm)
            S.dma("sp", O["o_mconv"][l, sidx], self.P[tok0 + L - 3:tok0 + L, OC:OC + W])

    def branchA(self, l, es):
        S, I, O, PS = self.S, self.I, self.O, self.PS
        sb = lambda shape, dt=F32, n="a": S.sb_in(es, shape, dt, n)
        tri = sb([64, 64]); S.dma("sp", tri, I["c_tri"])
        m5 = {64: sb([64, 320]), 8: sb([64, 320])}
        S.dma("sp", m5[64], I["c_m5_64"]); S.dma("sp", m5[8], I["c_m5_8"])
        mu = self.bload(es, I["a_mu"][l:l + 1, :], 3200)
        w0 = self.bload(es, I["a_w0"][l:l + 1, :]); a0 = self.bload(es, I["a_a0"][l:l + 1, :])
        kk_ = self.bload(es, I["a_k_k"][l:l + 1, :]); ka = self.bload(es, I["a_k_a"][l:l + 1, :])
        rk_ = self.bload(es, I["a_r_k"][l:l + 1, :])
        gng = self.bload(es, I["a_ln_g"][l:l + 1, :]); gnb = self.bload(es, I["a_ln_b"][l:l + 1, :])
        WA = sb([128, W])
        S.dma("sp", WA[0:64, :], I["a_w_up"][l]); S.dma("sp", WA[64:128, :], I["a_a_up"][l])
        onesf = sb([64, 1]); S.memset("pool", onesf, 1.0)
        Hf = sb([64, 16, 64]); Hb = sb([64, 16, 64], BF16)
        Sin = sb([64, 16, 64])
        PA = sb([64, 3200]); PV = sb([64, 3200])
        L2 = sb([64, 128]); LT = sb([128, 64])
        sw = sb([64, W]); aa = sb([64, W]); kk = sb([64, W]); km = sb([64, W]); tA = sb([64, W]); tB = sb([64, W])
        eP = sb([64, W]); eN = sb([64, W]); ePm = sb([64, W])
        psm = self.small(es, 3)
        sets = []
        for _ in range(2):
            sets.append(dict(Xb=[sb([64, W], BF16) for _ in range(4)],
                             vb=sb([64, W], BF16), FM=sb([64, 16, 4, 64], BF16), PT=sb([64, 16])))
        zb3 = [dict(z=sb([64, W]), bonus=sb([64, W])) for _ in range(3)]
        Gm = [sb([64, 5, 64], BF16) for _ in range(8)]
        Xs = [[sb([64, 64], BF16)] for _ in range(8)]
        XL = [sb([64, 192], BF16) for _ in range(8)]
        ys = [sb([64, W]) for _ in range(2)]
        sc = self.small(es); sc["sq"] = sb([64, W])
        PQ = [PS[6], PS[7]]
        YB, HB = PS[0], PS[1]
        work = [PS[2], PS[3], PS[4], PS[5]]

        items = []
        for (tok0, L, T, sidx, prow0) in self.seqs():
            nch = min(L // T, self.dbg.get('maxch', 10 ** 9))
            for ci in range(nch):
                items.append((tok0, L, T, sidx, ci, ci == 0, ci == nch - 1))

        def prep(item, B, ZB):
            tok0, L, T, sidx, ci, first, last = item
            z, bonus, Xb, vb, FM, PT = ZB["z"], ZB["bonus"], B["Xb"], B["vb"], B["FM"], B["PT"]
            t0 = tok0 + ci * T
            S.dma("sp", PA[:T, :], self.P[t0:t0 + T, 0:3200])
            if ci == 0:
                S.dma("sp", PV[0:1, :], I["rshift"][l, sidx:sidx + 1, :])
                S.dma("sp", PV[1:T, :], self.P[t0:t0 + T - 1, 0:3200])
            else:
                S.dma("sp", PV[:T, :], self.P[t0 - 1:t0 - 1 + T, 0:3200])
            S.dma("sp", z[:T, :], self.P[t0:t0 + T, OZ:OZ + W])
            def shift(en, c0, c1):
                S.tt(en, PV[:T, c0:c1], PV[:T, c0:c1], PA[:T, c0:c1], ALU.subtract)
                S.tt(en, PV[:T, c0:c1], PV[:T, c0:c1], mu[:T, c0:c1], ALU.mult)
                S.tt(en, PV[:T, c0:c1], PV[:T, c0:c1], PA[:T, c0:c1], ALU.add)
            shift("dve", 3 * W, 3200)
            r, k, v = PV[:T, 0:W], PV[:T, W:2 * W], PV[:T, 2 * W:3 * W]
            S.act(L2[:T, 0:64], PV[:T, 3072:3136], AF.Tanh)
            S.cp("dve", L2[:T, 64:128], PV[:T, 3136:3200])
            S.tr(PQ[0][:, 0:T], L2[:T, :], self.ident[:T, :T])
            S.cp("act", LT[:, :T], PQ[0][:, 0:T])
            shift("pool", 0, 3 * W)
            for hh in range(2):
                S.mm(PQ[hh][:T, :], LT[0:64, :T], WA[0:64, hh * 512:(hh + 1) * 512])
            for hh in range(2):
                cs = slice(hh * 512, (hh + 1) * 512)
                S.tt("dve", sw[:T, cs], PQ[hh][:T, :], w0[:T, cs], ALU.add)
            for hh in range(2):
                S.mm(PQ[hh][:T, :], LT[64:128, :T], WA[64:128, hh * 512:(hh + 1) * 512])
            for hh in range(2):
                cs = slice(hh * 512, (hh + 1) * 512)
                S.tt("dve", aa[:T, cs], PQ[hh][:T, :], a0[:T, cs], ALU.add)
            S.act(sw[:T, :], sw[:T, :], AF.Sigmoid)
            S.act(aa[:T, :], aa[:T, :], AF.Sigmoid)
            S.tt("pool", kk[:T, :], k, kk_[:T, :], ALU.mult)
            S.act(tA[:T, :], kk[:T, :], AF.Square)
            ss, rn, bs = psm["s0"], psm["s1"], psm["s2"]
            S.red(ss[:T, :16], tA[:T, :].re("p (h d) -> p h d", h=16), ALU.add)
            S.rsqrt(rn[:T, :16], ss[:T, :16], 1e-24, ALU.max)
            kk3 = kk[:T, :].re("p (h d) -> p h d", h=16)
            S.tt("pool", kk3, kk3, rn[:T, :16].un(2).bc([T, 16, 64]), ALU.mult)
            S.stt(tA[:T, :], aa[:T, :], -1.0, ka[:T, :], ALU.add, ALU.mult)
            S.stt(km[:T, :], tA[:T, :], 1.0, k, ALU.add, ALU.mult)
            S.tt("dve", tB[:T, :], r, km[:T, :], ALU.mult)
            S.tt("dve", tB[:T, :], tB[:T, :], rk_[:T, :], ALU.mult)
            S.red(bs[:T, :16], tB[:T, :].re("p (h d) -> p h d", h=16), ALU.add)
            S.tt("pool", bonus[:T, :].re("p (h d) -> p h d", h=16), v.re("p (h d) -> p h d", h=16),
                 bs[:T, :16].un(2).bc([T, 16, 64]), ALU.mult)
            for hh in range(2):
                S.mm(PQ[hh][:T, :], tri[:T, :T], sw[:T, hh * 512:(hh + 1) * 512])
            for hh in range(2):
                cs = slice(hh * 512, (hh + 1) * 512)
                S.act(eP[:T, cs], PQ[hh][:T, :], AF.Exp, scale=C0)
                S.act(eN[:T, cs], PQ[hh][:T, :], AF.Exp, scale=-C0)
                S.tt("dve", ePm[:T, cs], PQ[hh][:T, :], sw[:T, cs], ALU.subtract)
            S.act(ePm[:T, :], ePm[:T, :], AF.Exp, scale=C0)
            S.tt("pool", tB[:T, :], kk[:T, :], aa[:T, :], ALU.mult)
            S.tt("pool", Xb[0][:T, :], tB[:T, :], eN[:T, :], ALU.mult)
            S.tt("dve", Xb[1][:T, :], km[:T, :], eN[:T, :], ALU.mult)
            S.stt(Xb[2][:T, :], kk[:T, :], -1.0, ePm[:T, :], ALU.mult, ALU.mult)
            S.tt("pool", Xb[3][:T, :], r, eP[:T, :], ALU.mult)
            S.cp("act", vb[:T, :], v)
            for h in range(16):
                S.mm(PQ[0][:64, h:h + 1], sw[:T, h * 64:(h + 1) * 64], onesf[:T, :])
            S.act(PT, PQ[0][:64, 0:16], AF.Exp, scale=C0)
            for g in range(4):
                psb = PQ[(g + 1) % 2].bits(BF16)
                for hq in range(4):
                    h = g * 4 + hq
                    for q in range(4):
                        S.tr(psb[:64, (hq * 4 + q) * 64:(hq * 4 + q) * 64 + T], Xb[q][:T, h * 64:(h + 1) * 64],
                             self.identb[:T, :T])
                S.cp("act" if g % 2 else "dve", FM[:, g * 4:(g + 1) * 4, :, :T],
                     psb[:64, :].re("p (h q t) -> p h q t", h=4, q=4)[:, :, :, :T])

        def back(item, B, y):
            tok0, L, T, sidx, ci, first, last = item
            Xb, vb, FM, PT = B["Xb"], B["vb"], B["FM"], B["PT"]
            t0 = tok0 + ci * T
            nlev = {64: 6, 8: 3}[T]
            if first:
                S.dma("sp", Sin, I["rS"][l, sidx].re("h v k -> v h k"))
                for h in range(16):
                    S.tr(PS[h // 8][:64, (h % 8) * 64:(h % 8 + 1) * 64], Sin[:, h, :], self.ident[:64, :64])
                for hh in range(2):
                    S.cp("act", Hf[:, hh * 8:(hh + 1) * 8, :].re("p h v -> p (h v)"), PS[hh][:64, :])
                S.cp("dve", Hb, Hf)

            def head_chain(h, slot):
                cs = slice(h * 64, (h + 1) * 64)
                gm = Gm[slot]
                reg = work[slot // 2]
                xo = (slot % 2) * 256
                S.mm(reg[:T, xo:xo + 2 * T], FM[:, h, 0, :T], FM[:, h, 2:4, :T])
                S.mm(reg[:T, xo + 2 * T:xo + 4 * T], FM[:, h, 1, :T], FM[:, h, 2:4, :T])
                yield
                S.tt("dve", gm[:T, 0:4, :T], reg[:T, xo:xo + 4 * T].re("p (q t) -> p q t", q=4),
                     m5[T][:T, :4 * T].re("p (q t) -> p q t", q=4), ALU.mult)
                yield
                labT, rabT, lakT, rakT = [gm[:T, q, :T] for q in range(4)]
                S.mm(reg[:T, xo:xo + T], FM[:, h, 2, :T], FM[:, h, 0, :T])
                S.mm(reg[:T, xo + 64:xo + 128], FM[:, h, 2, :T], Hb[:, h, :], start=True, stop=False)
                S.mm(reg[:T, xo + 64:xo + 128], lakT, vb[:T, cs], start=False, stop=True)
                yield
                S.tt("dve", gm[:T, 4, :T], reg[:T, xo:xo + T], m5[T][:T, 4 * T:5 * T], ALU.mult)
                X = Xs[slot][0]
                S.cp("act", X[:T, :], reg[:T, xo + 64:xo + 128])
                yield
                Lt, Ln = labT, gm[:T, 4, :T]
                for lev in range(nlev):
                    with S.atomic():
                        S.mm(reg[:T, xo:xo + 64], self.identb[:T, :T], X[:T, :], start=True, stop=False)
                        S.mm(reg[:T, xo:xo + 64], Lt, X[:T, :], start=False, stop=True)
                    if lev < nlev - 1:
                        S.mm(reg[:T, xo + 64:xo + 64 + T], Ln, Lt)
                        S.mm(reg[:T, xo + 64 + T:xo + 64 + 2 * T], Lt, Ln)
                    yield
                    xl = XL[slot]
                    wdt = 64 + 2 * T if lev < nlev - 1 else 64
                    S.cp("act", xl[:T, :wdt], reg[:T, xo:xo + wdt])
                    X = xl[:, 0:64]
                    if lev < nlev - 1:
                        Lt, Ln = xl[:T, 64:64 + T], xl[:T, 64 + T:64 + 2 * T]
                    yield
                psy = YB[:T, (h % 8) * 64:(h % 8 + 1) * 64]
                S.mm(psy, FM[:, h, 3, :T], Hb[:, h, :], start=True, stop=False)
                S.mm(psy, rabT, X[:T, :], start=False, stop=False)
                S.mm(psy, rakT, vb[:T, cs], start=False, stop=True)
                psh = HB[:64, (h % 8) * 64:(h % 8 + 1) * 64]
                S.mm(psh, Xb[0][:T, cs], X[:T, :], start=True, stop=False)
                S.mm(psh, Xb[1][:T, cs], vb[:T, cs], start=False, stop=True)
                yield

            for hh in range(2):
                lockstep([head_chain(hh * 8 + s, s) for s in range(8)])
                S.cp("act", y[:T, hh * 512:(hh + 1) * 512], YB[:T, :])
                Hh = Hf[:, hh * 8:(hh + 1) * 8, :]
                S.tt("dve", Hh.re("p h v -> p (h v)"), Hh.re("p h v -> p (h v)"), HB[:64, :], ALU.add)
                S.tt("pool", Hh, Hh, PT[:, hh * 8:(hh + 1) * 8].un(2).bc([64, 8, 64]), ALU.mult)
                if not (hh == 1 and last):
                    S.cp("act", Hb[:, hh * 8:(hh + 1) * 8, :], Hh)
            if last:
                for h in range(16):
                    S.tr(PS[h // 8][:64, (h % 8) * 64:(h % 8 + 1) * 64], Hf[:, h, :], self.ident[:64, :64])
                for hh in range(2):
                    S.cp("act", Sin[:, hh * 8:(hh + 1) * 8, :].re("p h v -> p (h v)"), PS[hh][:64, :])
                S.dma("sp", O["o_rS"][l, sidx].re("h v k -> v h k"), Sin)
                S.dma("sp", O["o_rshift"][l, sidx:sidx + 1, :], self.P[tok0 + L - 1:tok0 + L, 0:3200])

        def post(item, ZB, y):
            tok0, L, T, sidx, ci, first, last = item
            self.headnorm(y, T, 16, 64, gng, gnb, sc)
            S.tt("pool", y[:T, :], y[:T, :], ZB["bonus"][:T, :], ALU.add)
            self.gate_store(y, ZB["z"], T, tok0 + ci * T, 0)

        prep(items[0], sets[0], zb3[0])
        for k, item in enumerate(items):
            recs = []
            if k + 1 < len(items):
                recs.append(S.record(lambda: prep(items[k + 1], sets[(k + 1) % 2], zb3[(k + 1) % 3])))
            recs.append(S.record(lambda: back(item, sets[k % 2], ys[k % 2])))
            if k >= 1:
                recs.append(S.record(lambda: post(items[k - 1], zb3[(k - 1) % 3], ys[(k - 1) % 2])))
            S.play(recs)
        post(items[-1], zb3[(len(items) - 1) % 3], ys[(len(items) - 1) % 2])


    def phase3a(self, l):
        S = self.S
        with contextlib.ExitStack() as es:
            WBR = S.sb_in(es, [128, 32, D_MODEL], BF16, "WBR")
            for i in range(4):
                S.dma("pool", WBR[:, i * 8:(i + 1) * 8, :], self.I["w_branch"][l, i].re("(k p) c -> p k c", p=128))
            yz = S.sb_in(es, [128, 4 * W], F32, "yz")
            yzT = [S.sb_in(es, [128, 32, 128], BF16, "yzT") for _ in range(2)]
            Gb = [S.sb_in(es, [128, D_MODEL], F32, "G") for _ in range(2)]
            mg = [S.sb_in(es, [128, D_MODEL], F32, "mg") for _ in range(1)]
            mgb = [S.sb_in(es, [128, D_MODEL], BF16, "mgb") for _ in range(1)]
            tmp = [S.sb_in(es, [128, 512], F32, "tmp3") for _ in range(2)]

            def rows_(tt):
                return slice(tt * 128, (tt + 1) * 128)

            def load_yz(tt):
                S.dma("sp", yz, self.YZ[rows_(tt), :])

            def load_G(q):
                tt, i = divmod(q, 4)
                S.dma("sp", Gb[q % 2], self.P[rows_(tt), OG + i * D_MODEL:OG + (i + 1) * D_MODEL])

            load_yz(0)
            load_G(0)
            n = 0
            for tt in range(NTT):
                yT = yzT[tt % 2]
                for g in range(8):
                    ps = self.PS[g % 8]
                    for j in range(4):
                        k = g * 4 + j
                        S.tr(ps[:, j * 128:(j + 1) * 128], yz[:, k * 128:(k + 1) * 128], self.ident)
                    S.cp("act" if g % 2 else "dve", yT[:, g * 4:(g + 1) * 4, :], ps.re("p (j c) -> p j c", j=4))
                if tt + 1 < NTT:
                    load_yz(tt + 1)
                m_ = mg[0]
                for i in range(4):
                    q = tt * 4 + i
                    G = Gb[q % 2]
                    if q + 1 < NTT * 4:
                        load_G(q + 1)
                    S.act(G, G, AF.Sigmoid)
                    for j in range(4):
                        cs = slice(j * 512, (j + 1) * 512)
                        ps = self.PS[n % 8]
                        n += 1
                        for k in range(8):
                            S.mm(ps, yT[:, i * 8 + k, :], WBR[:, i * 8 + k, cs], start=(k == 0), stop=(k == 7))
                        if i == 0:
                            S.tt("dve", m_[:, cs], ps, G[:, cs], ALU.mult)
                        else:
                            t_ = tmp[n % 2]
                            S.tt("dve", t_, ps, G[:, cs], ALU.mult)
                            S.tt("pool", m_[:, cs], m_[:, cs], t_, ALU.add)
                mb = mgb[0]
                S.cp("act", mb, m_)
                S.dma("sp", self.MG[rows_(tt), :], mb)

    def phase3b(self, l, xsrc, xdst):
        S = self.S
        with contextlib.ExitStack() as es:
            WO = S.sb_in(es, [128, 16, D_MODEL], BF16, "WO")
            S.dma("pool", WO, self.I["w_out"][l].re("(k p) c -> p k c", p=128))
            lng = S.sb_in(es, [128, D_MODEL], F32, "lng")
            lnb = S.sb_in(es, [128, D_MODEL], F32, "lnb")
            S.dma("sp", lng, self.I["ln_g"][l:l + 1, :].bc([128, D_MODEL]))
            S.dma("sp", lnb, self.I["ln_b"][l:l + 1, :].bc([128, D_MODEL]))
            mb = [S.sb_in(es, [128, D_MODEL], BF16, "mb") for _ in range(2)]
            mT = [S.sb_in(es, [128, 16, 128], BF16, "mT") for _ in range(2)]
            xr = [S.sb_in(es, [128, D_MODEL], F32, "xr") for _ in range(2)]
            pre = [S.sb_in(es, [128, D_MODEL], F32, "pre") for _ in range(2)]
            st = S.sb_in(es, [128, 4, 6], F32, "bst")
            mv = S.sb_in(es, [128, 2], F32, "bmv")
            rstd = S.sb_in(es, [128, 1], F32, "rstd")
            def loads3b(tt):
                r_ = slice(tt * 128, (tt + 1) * 128)
                S.dma("sp", mb[tt % 2], self.MG[r_, :])
                S.dma("sp", xr[tt % 2], xsrc[r_, :])

            loads3b(0)
            n = 0
            for tt in range(NTT):
                rows = slice(tt * 128, (tt + 1) * 128)
                b_ = mb[tt % 2]
                t_ = mT[tt % 2]
                x_ = xr[tt % 2]
                p_ = pre[tt % 2]
                if tt + 1 < NTT:
                    loads3b(tt + 1)
                for g in range(2):
                    ps = self.PS[n % 8].bits(BF16)
                    n += 1
                    for j in range(8):
                        k = g * 8 + j
                        S.tr(ps[:, j * 128:(j + 1) * 128], b_[:, k * 128:(k + 1) * 128], self.identb)
                    S.cp("act" if g % 2 else "dve", t_[:, g * 8:(g + 1) * 8, :], ps.re("p (j c) -> p j c", j=8))
                for j in range(4):
                    cs = slice(j * 512, (j + 1) * 512)
                    ps = self.PS[n % 8]
                    n += 1
                    for k in range(16):
                        S.mm(ps, t_[:, k, :], WO[:, k, cs], start=(k == 0), stop=(k == 15))
                    S.stt(p_[:, cs], x_[:, cs], ALPHA, ps, ALU.mult, ALU.add)
                    S.op("dve", lambda e, j=j, p_=p_, cs=cs: e.bn_stats(out=st.ap[:, j, :], in_=p_.ap[:, cs]), [p_], [st])
                S.op("dve", lambda e: e.bn_aggr(out=mv.ap, in_=st.ap), [st], [mv])
                S.rsqrt(rstd, mv[:, 1:2], LN_EPS, ALU.add)
                S.ts("dve", p_, p_, mv[:, 0:1], ALU.subtract, rstd, ALU.mult)
                S.tt("pool", p_, p_, lng, ALU.mult)
                S.tt("pool", p_, p_, lnb, ALU.add)
                S.dma("sp", xdst[rows, :], p_)


WSHAPES = {
    "w_in": [DEPTH, D_MODEL, IN_COLS], "a_mu": [DEPTH, 3200], "a_w0": [DEPTH, W], "a_w_up": [DEPTH, 64, W],
    "a_a0": [DEPTH, W], "a_a_up": [DEPTH, 64, W], "a_k_k": [DEPTH, W], "a_k_a": [DEPTH, W], "a_r_k": [DEPTH, W],
    "a_ln_g": [DEPTH, W], "a_ln_b": [DEPTH, W], "b_sinks": [DEPTH, 16], "c_conv_w": [DEPTH, 4, W],
    "c_conv_b": [DEPTH, W], "c_i_bias": [DEPTH, 8], "c_f_bias": [DEPTH, 8], "c_ln_g": [DEPTH, W],
    "c_ln_b": [DEPTH, W], "d_ln_g": [DEPTH, W], "d_ln_b": [DEPTH, W], "w_branch": [DEPTH, 4, W, D_MODEL],
    "w_out": [DEPTH, D_MODEL, D_MODEL], "ln_g": [DEPTH, D_MODEL], "ln_b": [DEPTH, D_MODEL],
}
CONST_SHAPES = {
    "c_ident": [128, 128], "c_tri": [64, 64], "c_madd": [64, 64],
    "c_rbc": [NPOSROW, 64], "c_rbs": [NPOSROW, 64], "c_rdc": [NPOSROW, 128], "c_rds": [NPOSROW, 128],
    "c_dmt": [64, 8, 64], "c_qdec": [128, 8, 64], "c_kdec64": [64, 1024], "c_kdec8": [64, 1024],
    "c_cdec64": [128, 1024], "c_cdec8": [128, 1024], "c_mp": [128, 256], "c_mp0": [128, 256], "c_ms": [128, 256],
    "c_m5_64": [64, 320], "c_m5_8": [64, 320],
}


def make_consts():
    c = {}
    f32 = np.float32
    c["c_ident"] = np.eye(128, dtype=f32)
    s = np.arange(64)[:, None]
    t = np.arange(64)[None, :]
    c["c_tri"] = (s <= t).astype(f32)
    c["c_madd"] = np.where(s <= t, 0.0, -1e30).astype(f32)
    pos = np.concatenate([np.arange(LP), 8192 + np.arange(LS)]).astype(f32)

    def rope_tab(d):
        inv = (f32(10000.0) ** (-np.arange(0, d, 2, dtype=f32) / f32(d))).astype(f32)
        ang = (pos[:, None] * inv[None, :]).astype(f32)
        cs = np.cos(ang).astype(f32)
        sn = np.sin(ang).astype(f32)
        ct = np.stack([cs, cs], 1)
        st = np.stack([-sn, sn], 1)
        return ct.reshape(NPOSROW, -1), st.reshape(NPOSROW, -1)
    c["c_rbc"], c["c_rbs"] = rope_tab(64)
    c["c_rdc"], c["c_rds"] = rope_tab(128)
    ksc = float(f32(128.0) ** f32(-0.5))
    lg = np.log1p(-np.exp2(-5.0 - np.arange(8, dtype=np.float64)))
    rel = (t - s).astype(np.float64)
    dmt = np.zeros((64, 8, 64), f32)
    for h in range(8):
        dmt[:, h, :] = np.where(rel >= 0, np.exp(np.maximum(rel, 0.0) * lg[h]), 0.0) * ksc
    c["c_dmt"] = dmt
    idx = np.arange(64, dtype=np.float64)
    qd = np.exp((idx + 1.0)[None, :] * lg[:, None])
    c["c_qdec"] = np.broadcast_to(qd[None], (128, 8, 64)).astype(f32).copy()
    for Lc in (64, 8):
        kd = np.zeros((64, 8, 128), f32)
        for h in range(8):
            kd[:Lc, h, :] = (np.exp((Lc - 1.0 - idx[:Lc]) * lg[h]) * ksc)[:, None]
        c["c_kdec%d" % Lc] = kd.reshape(64, 1024)
        cd = np.zeros((128, 8, 128), f32)
        for h in range(8):
            cd[:, h, :] = np.exp(Lc * lg[h])
        c["c_cdec%d" % Lc] = cd.reshape(128, 1024)
    a = np.arange(128)[:, None]
    cc = np.arange(256)[None, :]
    ok = (cc >= a) & (cc <= 128 + a)
    c["c_mp"] = np.where(ok, 0.0, -1e30).astype(f32)
    c["c_mp0"] = np.where(ok & (cc >= 128), 0.0, -1e30).astype(f32)
    c["c_ms"] = np.where(ok & (cc < 136) & (a < 8), 0.0, -1e30).astype(f32)
    for T in (64, 8):
        ss = np.arange(64)[:, None]
        tt_ = np.arange(T)[None, :]
        su = (ss < tt_).astype(f32)
        iu = (ss <= tt_).astype(f32)
        sl = (ss > tt_).astype(f32)
        m5 = np.zeros((64, 320), f32)
        m5[:, :5 * T] = np.concatenate([su, iu, su, iu, sl], 1)
        m5[T:, :] = 0.0
        c["c_m5_%d" % T] = m5
    return c


_PROG = {}


def get_prog(dbg=None):
    key = repr(sorted((dbg or {}).items()))
    if key not in _PROG:
        _PROG[key] = Prog(dbg)
    return _PROG[key]


def make_in_maps(inp):
    f = lambda a: np.ascontiguousarray(np.asarray(a, dtype=np.float32))
    consts = make_consts()
    shared = {n: f(inp[n]).reshape(WSHAPES[n]) for n in WSHAPES}
    st_names = [("rS", "state_rwkv_S"), ("rshift", "state_rwkv_shift"), ("ck", "cache_swa_k"), ("cv", "cache_swa_v"),
                ("mC", "state_mlstm_C"), ("mn", "state_mlstm_n"), ("mm", "state_mlstm_m"),
                ("mconv", "state_mlstm_conv"), ("dS", "state_ret_S")]
    xp = f(inp["x_prompt"])
    xs = f(inp["x_sample"])
    maps = []
    for c in range(NCORES):
        m = dict(shared)
        m.update(consts)
        m["xin"] = np.concatenate([xp[c % 4], xs[c * NSS:(c + 1) * NSS].reshape(NSS * LS, D_MODEL)], 0)
        for kn, full in st_names:
            a = f(inp[full])[:, c * NSS:(c + 1) * NSS]
            z = np.zeros((DEPTH, 1) + a.shape[2:], np.float32)
            a = np.concatenate([z, a], 1)
            if kn in ("ck", "cv"):
                a = a.reshape(DEPTH, NSEQ, 128, 128)
            m[kn] = np.ascontiguousarray(a)
        maps.append(m)
    return maps


def assemble(results):
    y = [r["y"] for r in results]
    y_prompt = np.stack([y[c][:LP] for c in range(4)], 0)
    y_sample = np.concatenate([y[c][LP:].reshape(NSS, LS, D_MODEL) for c in range(NCORES)], 0)
    outs = [y_prompt, y_sample]
    shp = {"o_rS": (16, 64, 64), "o_rshift": (3200,), "o_ck": (128, 2, 64), "o_cv": (128, 2, 64),
           "o_mC": (8, 64, 128), "o_mn": (8, 64), "o_mm": (8,), "o_mconv": (3, 1024), "o_dS": (8, 128, 128)}
    for n in ["o_rS", "o_rshift", "o_ck", "o_cv", "o_mC", "o_mn", "o_mm", "o_mconv", "o_dS"]:
        p = np.stack([results[c][n][:, 0] for c in range(4)], 1).reshape((DEPTH, 4) + shp[n])
        s = np.concatenate([results[c][n][:, 1:] for c in range(NCORES)], 1).reshape((DEPTH, NCORES * NSS) + shp[n])
        outs += [np.ascontiguousarray(p, dtype=np.float32), np.ascontiguousarray(s, dtype=np.float32)]
    return tuple(outs)


def kernel(**inputs):
    prog = get_prog()
    maps = make_in_maps(inputs)
    res = run_bass_kernel_spmd(prog.nc, maps, core_ids=list(range(NCORES)))
    return assemble(res.results)
```

```python
import contextlib
import numpy as np
import ml_dtypes
import concourse.bass as bass
import concourse.mybir as mybir
from concourse.bass_utils import run_bass_kernel_spmd

F32 = mybir.dt.float32
BF16 = mybir.dt.bfloat16
AF = mybir.ActivationFunctionType
ALU = mybir.AluOpType
AX = mybir.AxisListType

D_MODEL = 2048
DEPTH = 2
NCORES = 8
LP = 2048
NSS = 16
LS = 8
NTOK = LP + NSS * LS
NTT = NTOK // 128
NSEQ = 1 + NSS
W = 1024
IN_COLS = 21904
OA, OB, OC, OD, OZ, OG = 0, 3200, 4480, 6544, 9616, 13712
ALPHA = (2 * DEPTH) ** 0.25
LN_EPS = 1e-5
C0 = -float(np.exp(-0.5))
NPOSROW = LP + LS


class Res:
    __slots__ = ("w", "r", "ps")

    def __init__(self, ps=False):
        self.w = None
        self.r = {}
        self.ps = ps


class V:
    __slots__ = ("res", "ap")

    def __init__(self, res, ap):
        self.res = res
        self.ap = ap

    def __getitem__(self, idx):
        return V(self.res, self.ap[idx])

    def un(self, axis):
        return V(self.res, self.ap.unsqueeze(axis))

    def bc(self, shape):
        return V(self.res, self.ap.broadcast_to(list(shape)))

    def re(self, pat, **kw):
        return V(self.res, self.ap.rearrange(pat, **kw))

    def bits(self, dt):
        return V(self.res, self.ap.bitcast(dt))


class Stream(list):
    pos = 0


class Sched:
    NDMA = 40

    def __init__(self, nc, es):
        self.nc = nc
        self.es = es
        self.eng = {"pe": nc.tensor, "act": nc.scalar, "dve": nc.vector, "pool": nc.gpsimd, "sp": nc.sync}
        self.sems = []
        self.key = {}
        for n in self.eng:
            self.key[n] = len(self.sems)
            self.sems.append(es.enter_context(nc.semaphore("e_" + n)))
        self.dkeys = []
        for i in range(self.NDMA):
            self.dkeys.append(len(self.sems))
            self.sems.append(es.enter_context(nc.semaphore("d%d" % i)))
        self.val = [0] * len(self.sems)
        self.known = {n: {} for n in self.eng}
        self.dnext = 0
        self.uid = 0
        self.ninst = 0
        self.limit = 10 ** 9
        self.rec = None
        self.bundle = None
        self.fresh_swdge = False
        self._signaled = set()
        self.mhalf = None

    def sb(self, shape, dt=F32, name=None):
        self.uid += 1
        t = self.es.enter_context(self.nc.sbuf_tensor("%s_%d" % (name or "t", self.uid), list(shape), dt))
        return V(Res(), t[tuple(slice(None) for _ in shape)])

    def sb_in(self, es, shape, dt=F32, name=None):
        self.uid += 1
        t = es.enter_context(self.nc.sbuf_tensor("%s_%d" % (name or "t", self.uid), list(shape), dt))
        return V(Res(), t[tuple(slice(None) for _ in shape)])

    def _waits(self, en, reads, writes):
        deps = {}

        def add(k, v):
            if deps.get(k, 0) < v:
                deps[k] = v
        for r in reads:
            if r is not None and r.w is not None:
                add(*r.w)
        for w in writes:
            if w is None:
                continue
            if w.w is not None:
                add(*w.w)
            for k, v in w.r.items():
                add(k, v)
        e = self.eng[en]
        kn = self.known[en]
        for k, v in deps.items():
            if en == "pe" and k == self.key["pe"]:
                continue
            if kn.get(k, 0) < v:
                e.wait_ge(self.sems[k], v)
                kn[k] = v

    def _mark(self, ev, reads, writes):
        for r in reads:
            if r is not None and r.r.get(ev[0], 0) < ev[1]:
                r.r[ev[0]] = ev[1]
        for w in writes:
            if w is not None:
                w.w = ev
                w.r = {}

    def op(self, en, fn, reads, writes):
        if self.rec is not None:
            (self.bundle if self.bundle is not None else self.rec).append((0, (en, fn, reads, writes), None))
            return
        if self.ninst >= self.limit:
            return
        writes = [x.res for x in writes if x is not None] + [x.res for x in reads if x is not None and x.res.ps]
        reads = [x.res for x in reads if x is not None and not x.res.ps]
        self._waits(en, reads, writes)
        inst = fn(self.eng[en])
        k = self.key[en]
        self.val[k] += 1
        inst.then_inc(self.sems[k], 1)
        self.ninst += 1
        self._mark((k, self.val[k]), reads, writes)

    def dma(self, q, out, in_, **kw):
        if self.rec is not None:
            (self.bundle if self.bundle is not None else self.rec).append((1, (q, out, in_), kw))
            return
        if self.ninst >= self.limit:
            return
        if kw.pop("_nc", False):
            with self.nc.allow_non_contiguous_dma(reason="tiny strided state transfer"):
                return self.dma(q, out, in_, **kw)
        reads = [in_.res]
        writes = [out.res]
        if q == "pool" and self.fresh_swdge:
            k = len(self.sems)
            self.sems.append(self.es.enter_context(self.nc.semaphore("w%d" % k)))
            self.val.append(0)
        else:
            j = self.dnext
            self.dnext = (self.dnext + 1) % self.NDMA
            k = self.dkeys[j]
        if self.known[q].get(k, 0) < self.val[k]:
            self.eng[q].wait_ge(self.sems[k], self.val[k])
            self.known[q][k] = self.val[k]
        self._waits(q, reads, writes)
        inst = self.eng[q].dma_start(out=out.ap, in_=in_.ap, **kw)
        self.val[k] += 16
        inst.then_inc(self.sems[k], 16)
        self.ninst += 1
        self._mark((k, self.val[k]), reads, writes)

    def _est(self, kind, args):
        if kind == 2:
            en0, dur, rd, wr = None, 0.0, [], []
            for k2, a2, _ in args:
                e, d, r, w = self._est(k2, a2)
                en0 = en0 or e
                dur += d
                rd += r
                wr += w
            return en0, dur, rd, wr
        if kind == 1:
            q, out, in_ = args
            n = 1
            for d in out.ap.shape:
                n *= d
            return q, 2000.0 + n * 4 / 150.0, [in_.res], [out.res]
        en, fn, reads, writes = args
        n = 64
        if writes:
            n = 1
            for d in writes[0].ap.shape[1:]:
                n *= d
        dur = {"pe": 35 + n / 2.4, "act": 200 + n / 1.2, "dve": 80 + n / 0.96, "pool": 150 + n * 2.2, "sp": 50}[en]
        return en, dur, [x.res for x in reads if x is not None], [x.res for x in writes if x is not None]

    @contextlib.contextmanager
    def atomic(self):
        if self.rec is None or self.bundle is not None:
            yield
            return
        self.bundle = []
        try:
            yield
        finally:
            b, self.bundle = self.bundle, None
            if b:
                self.rec.append((2, b, None))

    @contextlib.contextmanager
    def sub(self, name):
        if self.rec is None:
            yield
            return
        main = self.rec
        if not hasattr(main, "subs"):
            main.subs = {}
        self.rec = main.subs.setdefault(name, Stream())
        try:
            yield
        finally:
            self.rec = main

    def signal(self, tok):
        if self.rec is not None:
            self.rec.append((4, tok, None))

    def wait(self, tok):
        if self.rec is not None:
            self.rec.append((3, tok, None))

    def record(self, fn):
        old = self.rec
        self.rec = Stream()
        fn()
        out = self.rec
        self.rec = old
        return out

    def play(self, streams):
        if not hasattr(self, "_tfree"):
            self._tfree = {n: 0.0 for n in self.eng}
            self._wready = {}
            self._rdone = {}
        tfree, wready, rdone = self._tfree, self._wready, self._rdone
        streams = [s if isinstance(s, Stream) else Stream(s) for s in streams]
        for s in list(streams):
            streams += list(getattr(s, "subs", {}).values())
        sig = self._signaled
        ests = [None] * len(streams)
        while True:
            best = None
            bstart = None
            for i, s in enumerate(streams):
                while s.pos < len(s) and s[s.pos][0] in (3, 4):
                    if s[s.pos][0] == 4:
                        sig.add(s[s.pos][1])
                    elif s[s.pos][1] not in sig:
                        break
                    s.pos += 1
                if s.pos >= len(s) or s[s.pos][0] == 3:
                    continue
                if ests[i] is None:
                    kind, args, kw = s[s.pos]
                    en, dur, rd, wr = self._est(kind, args)
                    t = tfree[en]
                    for r in rd:
                        if r is None:
                            continue
                        t = max(t, wready.get(id(r), 0.0))
                        if r.ps:
                            t = max(t, rdone.get(id(r), 0.0))
                    for w in wr:
                        if w is None:
                            continue
                        t = max(t, wready.get(id(w), 0.0), rdone.get(id(w), 0.0))
                    ests[i] = (t, en, dur, rd, wr)
                if bstart is None or ests[i][0] < bstart:
                    best, bstart = i, ests[i][0]
            if best is None:
                if any(s.pos < len(s) for s in streams):
                    if any(s.pos < len(s) and (s[s.pos][0] == 4 or (s[s.pos][0] == 3 and s[s.pos][1] in sig))
                           for s in streams):
                        continue
                    raise RuntimeError("stream deadlock: every stream waits on an unsignalled token")
                break
            t, en, dur, rd, wr = ests[best]
            kind, args, kw = streams[best][streams[best].pos]
            streams[best].pos += 1
            if kind == 2:
                for k2, a2, kw2 in args:
                    if k2 == 0:
                        self.op(*a2)
                    else:
                        self.dma(*a2, **kw2)
                tfree[en] = t + dur
                tend = t + dur + 60.0
            elif kind == 0:
                self.op(*args)
                tfree[en] = t + dur
                tend = t + dur + 60.0
            else:
                self.dma(*args, **kw)
                tfree[en] = t + 60.0
                tend = t + dur
            for r in rd:
                if r is not None:
                    rdone[id(r)] = max(rdone.get(id(r), 0.0), tend)
            for w in wr:
                if w is not None:
                    wready[id(w)] = tend
            ests = [None] * len(streams)

    def barrier(self):
        for en, e in self.eng.items():
            kn = self.known[en]
            for k, v in enumerate(self.val):
                if v > 0 and kn.get(k, 0) < v:
                    e.wait_ge(self.sems[k], v)
                    kn[k] = v

    def finish(self):
        e = self.eng["sp"]
        kn = self.known["sp"]
        for k, v in enumerate(self.val):
            if v > 0 and kn.get(k, 0) < v:
                e.wait_ge(self.sems[k], v)
                kn[k] = v

    def mm(self, out, lhsT, rhs, start=True, stop=True):
        self.op("pe", lambda e: e.matmul(out.ap, lhsT=lhsT.ap, rhs=rhs.ap, start=start, stop=stop),
                [lhsT, rhs], [out])

    def tr(self, out, in_, ident):
        self.op("pe", lambda e: e.transpose(out.ap, in_.ap, ident.ap), [in_, ident], [out])

    def cp(self, en, out, in_):
        if en == "act":
            self.op(en, lambda e: e.copy(out=out.ap, in_=in_.ap), [in_], [out])
        else:
            self.op(en, lambda e: e.tensor_copy(out=out.ap, in_=in_.ap), [in_], [out])

    def tt(self, en, out, a, b, op):
        self.op(en, lambda e: e.tensor_tensor(out=out.ap, in0=a.ap, in1=b.ap, op=op), [a, b], [out])

    def ts(self, en, out, a, s1, op0, s2=None, op1=None):
        rd = [a]
        s1a = s1
        s2a = s2
        if isinstance(s1, V):
            rd.append(s1)
            s1a = s1.ap
        if isinstance(s2, V):
            rd.append(s2)
            s2a = s2.ap
        if op1 is None:
            self.op(en, lambda e: e.tensor_scalar(out=out.ap, in0=a.ap, scalar1=s1a, scalar2=None, op0=op0), rd, [out])
        else:
            self.op(en, lambda e: e.tensor_scalar(out=out.ap, in0=a.ap, scalar1=s1a, scalar2=s2a, op0=op0, op1=op1),
                    rd, [out])

    def stt(self, out, a, s, b, op0, op1):
        rd = [a, b]
        sa = s
        if isinstance(s, V):
            rd.append(s)
            sa = s.ap
        self.op("dve", lambda e: e.scalar_tensor_tensor(out=out.ap, in0=a.ap, scalar=sa, in1=b.ap, op0=op0, op1=op1),
                rd, [out])

    def act(self, out, in_, func, bias=None, scale=None, accum=None):
        rd = [in_]
        kw = {}
        if bias is not None:
            if isinstance(bias, V):
                rd.append(bias)
                kw["bias"] = bias.ap
            else:
                kw["bias"] = float(bias)
        if scale is not None:
            if isinstance(scale, V):
                rd.append(scale)
                kw["scale"] = scale.ap
            else:
                kw["scale"] = float(scale)
        wr = [out]
        if accum is not None:
            kw["accum_out"] = accum.ap
            wr.append(accum)
        self.op("act", lambda e: e.activation(out=out.ap, in_=in_.ap, func=func, **kw), rd, wr)

    def red(self, out, in_, op):
        self.op("dve", lambda e: e.tensor_reduce(out=out.ap, in_=in_.ap, axis=AX.X, op=op), [in_], [out])

    def memset(self, en, out, val):
        self.op(en, lambda e: e.memset(out.ap, val), [], [out])

    def scan(self, out, d0, d1, init, op0, op1):
        rd = [d0, d1]
        ia = init
        if isinstance(init, V):
            rd.append(init)
            ia = init.ap
        self.op("dve", lambda e: e.tensor_tensor_scan(out=out.ap, data0=d0.ap, data1=d1.ap, initial=ia, op0=op0, op1=op1),
                rd, [out])

    def recip(self, out, in_):
        self.op("dve", lambda e: e.reciprocal(out=out.ap, in_=in_.ap), [in_], [out])

    def rsqrt(self, out, in_, c, op):
        self.ts("dve", out, in_, c, op)
        if self.mhalf is None:
            self.act(out, out, AF.Sqrt)
            self.recip(out, out)
        else:
            shp = list(out.ap.shape)
            self.tt("pool", out, out, self.mhalf[:shp[0], :shp[1]], ALU.pow)


def lockstep(gens):
    gens = list(gens)
    while gens:
        nxt = []
        for g in gens:
            try:
                next(g)
                nxt.append(g)
            except StopIteration:
                pass
        gens = nxt


def dram(ap):
    return V(None, ap)


class Prog:
    def __init__(self, dbg=None):
        self.dbg = dbg or {}
        nc = bass.Bass("TRN2", target_bir_lowering=False)
        self.nc = nc
        self.es = contextlib.ExitStack()
        self.S = Sched(nc, self.es)
        self.S.limit = self.dbg.get('limit', 10 ** 9)
        self.S.fresh_swdge = bool(self.dbg.get('fresh_swdge'))
        self.declare_io()
        self.build()
        self.S.finish()
        self.es.close()

    def din(self, name, shape, dt=F32):
        return dram(self.nc.dram_tensor(name, list(shape), dt, kind="ExternalInput").ap())

    def dout(self, name, shape, dt=F32):
        return dram(self.nc.dram_tensor(name, list(shape), dt, kind="ExternalOutput").ap())

    def dscr(self, name, shape, dt=F32):
        kind = "ExternalOutput" if name in self.dbg.get("expose", ()) else "Internal"
        return dram(self.nc.dram_tensor(name, list(shape), dt, kind=kind).ap())

    def declare_io(self):
        I = {}
        I["xin"] = self.din("xin", [NTOK, D_MODEL])
        I["rS"] = self.din("rS", [DEPTH, NSEQ, 16, 64, 64])
        I["rshift"] = self.din("rshift", [DEPTH, NSEQ, 3200])
        I["ck"] = self.din("ck", [DEPTH, NSEQ, 128, 128])
        I["cv"] = self.din("cv", [DEPTH, NSEQ, 128, 128])
        I["mC"] = self.din("mC", [DEPTH, NSEQ, 8, 64, 128])
        I["mn"] = self.din("mn", [DEPTH, NSEQ, 8, 64])
        I["mm"] = self.din("mm", [DEPTH, NSEQ, 8])
        I["mconv"] = self.din("mconv", [DEPTH, NSEQ, 3, 1024])
        I["dS"] = self.din("dS", [DEPTH, NSEQ, 8, 128, 128])
        for n, s in WSHAPES.items():
            if self.dbg.get("tinyw") and n in ("w_in", "w_branch", "w_out"):
                s = [DEPTH, 128, 128]
            I[n] = self.din(n, s)
        for n, s in CONST_SHAPES.items():
            I[n] = self.din(n, s)
        self.I = I
        O = {}
        O["y"] = self.dout("y", [NTOK, D_MODEL])
        O["o_rS"] = self.dout("o_rS", [DEPTH, NSEQ, 16, 64, 64])
        O["o_rshift"] = self.dout("o_rshift", [DEPTH, NSEQ, 3200])
        O["o_ck"] = self.dout("o_ck", [DEPTH, NSEQ, 128, 128])
        O["o_cv"] = self.dout("o_cv", [DEPTH, NSEQ, 128, 128])
        O["o_mC"] = self.dout("o_mC", [DEPTH, NSEQ, 8, 64, 128])
        O["o_mn"] = self.dout("o_mn", [DEPTH, NSEQ, 8, 64])
        O["o_mm"] = self.dout("o_mm", [DEPTH, NSEQ, 8])
        O["o_mconv"] = self.dout("o_mconv", [DEPTH, NSEQ, 3, 1024])
        O["o_dS"] = self.dout("o_dS", [DEPTH, NSEQ, 8, 128, 128])
        self.O = O
        self.P = self.dscr("P", [NTOK, IN_COLS])
        self.YZ = self.dscr("YZ", [NTOK, 4 * W])
        self.MG = self.dscr("MG", [NTOK, D_MODEL], BF16)
        self.X1 = self.dscr("X1", [NTOK, D_MODEL])

    def build(self):
        S = self.S
        es = self.es
        self.PS = []
        for i in range(8):
            t = es.enter_context(self.nc.psum_tensor("ps%d" % i, [128, 512], F32))
            self.PS.append(V(Res(ps=True), t[:, :]))
        self.ident = S.sb([128, 128], F32, "ident")
        S.dma("sp", self.ident, self.I["c_ident"])
        self.identb = S.sb([128, 128], BF16, "identb")
        S.cp("dve", self.identb, self.ident)
        if not self.dbg.get("nopow"):
            mh = S.sb([128, 16], F32, "mhalf")
            S.memset("pool", mh, -0.5)
            S.mhalf = mh
        nl = self.dbg.get("layers", DEPTH)
        for l in range(nl):
            xsrc = self.I["xin"] if l == 0 else self.X1
            xdst = self.O["y"] if l == nl - 1 else self.X1
            self.b_done = False
            if self.dbg.get("p1", True):
                self.phase1(l, xsrc)
            S.barrier()
            if self.dbg.get("p2", True):
                self.phase2(l)
            S.barrier()
            if self.dbg.get("p3", True):
                self.phase3a(l)
                S.barrier()
                self.phase3b(l, xsrc, xdst)
            S.barrier()

    def phase1_setup(self, l, xsrc, es):
        S = self.S
        XT = S.sb_in(es, [128, 16, NTOK], BF16, "XT")
        with contextlib.ExitStack() as es2:
            xrow = [S.sb_in(es2, [128, D_MODEL], F32, "xrow") for _ in range(2)]
            for tt in range(NTT):
                xr = xrow[tt % 2]
                S.dma("sp", xr, xsrc[tt * 128:(tt + 1) * 128, :])
                for g in range(4):
                    ps = self.PS[(tt * 4 + g) % 8]
                    for j in range(4):
                        k = g * 4 + j
                        S.tr(ps[:, j * 128:(j + 1) * 128], xr[:, k * 128:(k + 1) * 128], self.ident)
                    S.cp("act" if g % 2 else "dve", XT[:, g * 4:(g + 1) * 4, tt * 128:(tt + 1) * 128],
                         ps.re("p (j c) -> p j c", j=4))
            S.barrier()
        WB = [S.sb_in(es, [128, 16, 512], BF16, "WB") for _ in range(3)]
        stg = [S.sb_in(es, [128, 512], F32, "stg") for _ in range(4)]
        return dict(XT=XT, WB=WB, stg=stg, n=0, c=0)

    @staticmethod
    def col_tiles(a, b):
        n = -(-(b - a) // 512)
        w = -(-(b - a) // n)
        w = -(-w // 8) * 8
        out = []
        while a < b:
            out.append((a, min(w, b - a)))
            a += w
        return out

    def phase1_cols(self, ctx, l, tiles, banks):
        S = self.S
        XT, WB, stg = ctx["XT"], ctx["WB"], ctx["stg"]
        wv = self.I["w_in"][l].re("(k p) c -> p k c", p=128)
        for (c0, cw) in tiles:
            wb = WB[ctx["c"] % 3]
            ctx["c"] += 1
            S.dma("pool", wb[:, :, :cw], wv[:, :, c0:c0 + cw])
            for tt in range(NTT):
                n = ctx["n"]
                ctx["n"] += 1
                ps = banks[n % len(banks)]
                with S.atomic():
                    for k in range(16):
                        S.mm(ps[:, :cw], XT[:, k, tt * 128:(tt + 1) * 128], wb[:, k, :cw], start=(k == 0), stop=(k == 15))
                sg = stg[n % 4]
                S.cp("act" if n % 2 else "dve", sg[:, :cw], ps[:, :cw])
                S.dma("sp", self.P[tt * 128:(tt + 1) * 128, c0:c0 + cw], sg[:, :cw])

    def phase1(self, l, xsrc):
        S = self.S
        first = self.col_tiles(OB, OC) + self.col_tiles(OZ + W, OZ + 2 * W)
        rest = self.col_tiles(0, OB) + self.col_tiles(OC, OZ + W) + self.col_tiles(OZ + 2 * W, IN_COLS)
        if self.dbg.get("ncol") is not None:
            rest = rest[:self.dbg["ncol"]]
        with contextlib.ExitStack() as es:
            ctx = self.phase1_setup(l, xsrc, es)
            self.phase1_cols(ctx, l, first, self.PS)
            S.barrier()
            if not self.dbg.get("p2", True) or "B" not in self.dbg.get("branches", "ABCD") or self.dbg.get("nooverlapB"):
                self.phase1_cols(ctx, l, rest, self.PS)
                self.b_done = False
                return
            with contextlib.ExitStack() as esB:
                sB = S.record(lambda: self.branchB(l, esB))
                sP = S.record(lambda: self.phase1_cols(ctx, l, rest, [self.PS[6], self.PS[7]]))
                S.play([sP, sB])
            self.b_done = True

    def seqs(self):
        out = [(0, LP, 64, 0, 0)]
        for j in range(NSS):
            out.append((LP + j * LS, LS, LS, j + 1, LP))
        ns = self.dbg.get("nseq", len(out))
        return out[:ns]

    def bload(self, es, src_row, n=W, rows=64):
        t = self.S.sb_in(es, [rows, n], F32, "bc")
        self.S.dma("sp", t, src_row.bc([rows, n]))
        return t

    def headnorm(self, x, T, H, dv, gt, bt, sc, on_act=False):
        S = self.S
        x3 = x[:T, :].re("p (h d) -> p h d", h=H)
        sq, ssum, ssq, mean, msq, var, rstd = sc["sq"], sc["s0"], sc["s1"], sc["s2"], sc["s3"], sc["s4"], sc["s5"]
        S.red(ssum[:T, :H], x3, ALU.add)
        S.act(sq[:T, :], x[:T, :], AF.Square)
        S.red(ssq[:T, :H], sq[:T, :].re("p (h d) -> p h d", h=H), ALU.add)
        S.ts("dve", mean[:T, :H], ssum[:T, :H], 1.0 / dv, ALU.mult)
        S.tt("dve", msq[:T, :H], mean[:T, :H], mean[:T, :H], ALU.mult)
        S.stt(var[:T, :H], ssq[:T, :H], 1.0 / dv, msq[:T, :H], ALU.mult, ALU.subtract)
        S.rsqrt(rstd[:T, :H], var[:T, :H], LN_EPS, ALU.add)
        if on_act:
            S.stt(msq[:T, :H], mean[:T, :H], -1.0, rstd[:T, :H], ALU.mult, ALU.mult)
            for h in range(H):
                S.act(x3[:, h, :], x3[:, h, :], AF.Identity, bias=msq[:T, h:h + 1], scale=rstd[:T, h:h + 1])
        else:
            S.tt("pool", x3, x3, mean[:T, :H].un(2).bc([T, H, dv]), ALU.subtract)
            S.tt("pool", x3, x3, rstd[:T, :H].un(2).bc([T, H, dv]), ALU.mult)
        S.tt("dve", x[:T, :], x[:T, :], gt[:T, :], ALU.mult)
        S.tt("pool", x[:T, :], x[:T, :], bt[:T, :], ALU.add)

    def gate_store(self, x, z, T, t0, bi):
        S = self.S
        S.act(z[:T, :], z[:T, :], AF.Silu)
        S.tt("dve", x[:T, :], x[:T, :], z[:T, :], ALU.mult)
        S.dma("sp", self.YZ[t0:t0 + T, bi * W:(bi + 1) * W], x[:T, :])

    def rope(self, out, x, cos, sin, T, nh, hd, t1, t2):
        S = self.S
        n = nh * hd
        t1 = out
        x4 = x[:T, :n].re("p (h two d) -> p h two d", h=nh, two=2)
        c4 = cos[:T, :hd].re("p (two d) -> p two d", two=2).un(1).bc([T, nh, 2, hd // 2])
        s4 = sin[:T, :hd].re("p (two d) -> p two d", two=2).un(1).bc([T, nh, 2, hd // 2])
        S.tt("pool", t1[:T, :n].re("p (h two d) -> p h two d", h=nh, two=2), x4, c4, ALU.mult)
        o4 = t2[:T, :n].re("p (h two d) -> p h two d", h=nh, two=2)
        S.tt("dve", o4[:, :, 0, :], x4[:, :, 1, :], s4[:, :, 0, :], ALU.mult)
        S.tt("dve", o4[:, :, 1, :], x4[:, :, 0, :], s4[:, :, 1, :], ALU.mult)
        S.tt("pool", out[:T, :n], t1[:T, :n], t2[:T, :n], ALU.add)

    def small(self, es, n=8, cols=16):
        return {"s%d" % i: self.S.sb_in(es, [128, cols], F32, "sm") for i in range(n)}

    def phase2(self, l):
        S = self.S
        br = self.dbg.get("branches", "ABCD")
        P_ = self.PS
        self.psmap = {}
        for name in br:
            if name in "CD" and "C" in br and "D" in br and not self.dbg.get("nointer"):
                continue
            if name == "B" and getattr(self, "b_done", False):
                continue
            with contextlib.ExitStack() as es:
                getattr(self, "branch" + name)(l, es)
            S.barrier()
        if "C" in br and "D" in br and not self.dbg.get("nointer"):
            self.psmap["D"] = [P_[0], P_[1], P_[0], P_[0], P_[1], P_[2], P_[3], None]
            self.psmap["C"] = [P_[4], P_[4], P_[4], P_[5], P_[6], P_[7], P_[5], P_[6]]
            with contextlib.ExitStack() as es:
                recs = []
                for name in "CD":
                    S.rec = Stream()
                    getattr(self, "branch" + name)(l, es)
                    recs.append(S.rec)
                S.rec = None
                S.play(recs)
            self.psmap = {}
            S.barrier()

    def branchD(self, l, es):
        S, I, O, PS = self.S, self.I, self.O, self.psmap.get("D", self.PS)
        sb = lambda shape, dt=F32, n="d": S.sb_in(es, shape, dt, n)
        dmt = sb([64, 8, 64]); S.dma("sp", dmt, I["c_dmt"])
        qdec = sb([128, 8, 64]); S.dma("sp", qdec, I["c_qdec"])
        kdec = {64: sb([64, W]), 8: sb([64, W])}
        cdec = {64: sb([128, W]), 8: sb([128, W])}
        for T in (64, 8):
            S.dma("sp", kdec[T], I["c_kdec%d" % T]); S.dma("sp", cdec[T], I["c_cdec%d" % T])
        gng = self.bload(es, I["d_ln_g"][l:l + 1, :]); gnb = self.bload(es, I["d_ln_b"][l:l + 1, :])
        Sf = sb([128, 8, 128]); Sb = sb([128, 8, 128], BF16)
        PD = sb([64, 3072]); cosT = sb([64, 128]); sinT = sb([64, 128]); zz = [sb([64, W]) for _ in range(3)]
        t1 = None; t2 = sb([64, 2048]); qkr = sb([64, 2048])
        vb = sb([64, 8, 128], BF16); ktb = sb([64, 8, 128], BF16)
        qT = sb([128, 8, 64], BF16); qdT = sb([128, 8, 64], BF16); kT = sb([128, 8, 64], BF16)
        inT = sb([64, 8, 64], BF16)
        os_ = [sb([64, W]) for _ in range(2)]
        sc = self.small(es); sc["sq"] = sb([64, W])
        items = []
        for (tok0, L, T, sidx, prow0) in self.seqs():
            nch = min(L // T, self.dbg.get('maxch', 10 ** 9))
            for ci in range(nch):
                items.append((tok0, L, T, sidx, prow0, ci, ci == 0, ci == nch - 1))

        def loads(k):
            tok0, L, T, sidx, prow0, ci, first, last = items[k]
            t0 = tok0 + ci * T
            pr = prow0 + ci * T
            S.dma("sp", PD[:T, :], self.P[t0:t0 + T, OD:OD + 3072])
            S.dma("sp", cosT[:T, :], I["c_rdc"][pr:pr + T, :])
            S.dma("sp", sinT[:T, :], I["c_rds"][pr:pr + T, :])
            S.dma("sp", zz[k % 3][:T, :], self.P[t0:t0 + T, OZ + 3 * W:OZ + 4 * W])

        loads(0)
        for k, (tok0, L, T, sidx, prow0, ci, first, last) in enumerate(items):
            t0 = tok0 + ci * T
            z = zz[k % 3]
            o = os_[k % 2]
            if k >= 2:
                S.wait(("Ddone", l, k - 2))
            if first:
                S.dma("sp", Sf, I["dS"][l, sidx].re("h d v -> d h v"))
                S.cp("act", Sb, Sf)
            self.rope(qkr, PD, cosT, sinT, T, 16, 128, t1, t2)
            S.cp("act", vb[:T].re("p h d -> p (h d)"), PD[:T, 2048:3072])
            S.tt("pool", ktb[:T].re("p h d -> p (h d)"), qkr[:T, W:2 * W], kdec[T][:T, :], ALU.mult)
            for h in range(8):
                S.tr(PS[0][:, h * T:(h + 1) * T], qkr[:T, h * 128:(h + 1) * 128], self.ident[:T, :T])
                S.tr(PS[1][:, h * T:(h + 1) * T], qkr[:T, W + h * 128:W + (h + 1) * 128], self.ident[:T, :T])
            q3 = PS[0][:, :8 * T].re("p (h t) -> p h t", h=8)
            S.cp("act", qT[:, :, :T], q3)
            S.tt("dve", qdT[:, :, :T], q3, qdec[:, :, :T], ALU.mult)
            S.cp("act", kT[:, :, :T], PS[1][:, :8 * T].re("p (h t) -> p h t", h=8))
            if k + 1 < len(items):
                loads(k + 1)
            for h in range(8):
                S.mm(PS[2][:T, h * T:(h + 1) * T], kT[:, h, :T], qT[:, h, :T])
            S.tt("dve", inT[:T, :, :T], PS[2][:T, :8 * T].re("p (h t) -> p h t", h=8), dmt[:T, :, :T], ALU.mult)
            for h in range(8):
                pso = PS[3 + h // 4][:T, (h % 4) * 128:(h % 4 + 1) * 128]
                with S.atomic():
                    S.mm(pso, inT[:T, h, :T], vb[:T, h, :], start=True, stop=False)
                    S.mm(pso, qdT[:, h, :T], Sb[:, h, :], start=False, stop=True)
            S.cp("act", o[:T, 0:512], PS[3][:T, :])
            S.cp("act", o[:T, 512:1024], PS[4][:T, :])
            for h in range(8):
                S.mm(PS[5 + h // 4][:, (h % 4) * 128:(h % 4 + 1) * 128], ktb[:T, h, :], vb[:T, h, :])
            Sf2 = Sf.re("p h d -> p (h d)")
            S.tt("pool", Sf2, Sf2, cdec[T], ALU.mult)
            S.tt("dve", Sf2[:, 0:512], Sf2[:, 0:512], PS[5], ALU.add)
            S.tt("dve", Sf2[:, 512:1024], Sf2[:, 512:1024], PS[6], ALU.add)
            S.cp("act", Sb, Sf)
            S.signal(("Dmain", l, k))
            with S.sub("post"):
                S.wait(("Dmain", l, k))
                self.headnorm(o, T, 8, 128, gng, gnb, sc, on_act=True)
                self.gate_store(o, z, T, t0, 3)
                S.signal(("Ddone", l, k))
            if last:
                S.dma("sp", O["o_dS"][l, sidx].re("h d v -> d h v"), Sf)

    def branchB(self, l, es):
        S, I, O, PS = self.S, self.I, self.O, self.PS
        sb = lambda shape, dt=F32, n="b": S.sb_in(es, shape, dt, n)
        mp = sb([128, 256]); S.dma("sp", mp, I["c_mp"])
        mp0 = sb([128, 256]); S.dma("sp", mp0, I["c_mp0"])
        ms = sb([128, 256]); S.dma("sp", ms, I["c_ms"])
        sink = self.bload(es, I["b_sinks"][l:l + 1, :], 16, 128)
        kTp = [sb([64, 2, 128], BF16) for _ in range(2)]
        vp = [sb([128, 2, 64], BF16) for _ in range(2)]
        ckf = sb([128, 128]); ckb = sb([128, 128], BF16)
        PB = sb([128, 1280]); cosT = sb([128, 64]); sinT = sb([128, 64]); zz = [sb([128, W]) for _ in range(2)]
        t1 = None; t2 = sb([128, 1152]); qkr = sb([128, 1152]); qkb = sb([128, 1152], BF16)
        qT = sb([64, 16, 128], BF16)
        s_sbs = [sb([128, 2, 256]) for _ in range(4)]; p_bfs = [sb([128, 2, 256], BF16) for _ in range(4)]
        pTs = [sb([128, 4, 128], BF16) for _ in range(4)]
        sms = [self.small(es, 6, 2) for _ in range(4)]
        rden = sb([128, 16])
        yb = sb([128, W])
        bitems = []
        for (tok0, L, T, sidx, prow0) in self.seqs():
            Lb = 128 if L % 128 == 0 else L
            for bi in range(min(L // Lb, self.dbg.get('maxch', 10 ** 9))):
                bitems.append((tok0 + bi * Lb, prow0 + bi * Lb, Lb))

        def loadsB(k):
            t0_, pr_, Lb_ = bitems[k]
            S.dma("sp", PB[:Lb_, :], self.P[t0_:t0_ + Lb_, OB:OB + 1280])
            S.dma("sp", cosT[:Lb_, :], I["c_rbc"][pr_:pr_ + Lb_, :])
            S.dma("sp", sinT[:Lb_, :], I["c_rbs"][pr_:pr_ + Lb_, :])
            S.dma("sp", zz[k % 2][:Lb_, :], self.P[t0_:t0_ + Lb_, OZ + W:OZ + 2 * W])

        loadsB(0)
        kB = 0
        for (tok0, L, T, sidx, prow0) in self.seqs():
            Lb = 128 if L % 128 == 0 else L
            nb = L // Lb
            prompt = (sidx == 0)
            if prompt:
                S.memset("pool", kTp[0], 0.0)
                S.memset("pool", vp[0], 0.0)
            else:
                S.dma("sp", ckf, I["ck"][l, sidx])
                S.cp("act", ckb, ckf)
                psb = PS[0].bits(BF16)
                for g in range(2):
                    S.tr(psb[:64, g * 128:(g + 1) * 128], ckb[:, g * 64:(g + 1) * 64], self.identb)
                S.cp("dve", kTp[0], psb[:64, 0:256].re("p (g t) -> p g t", g=2))
                S.dma("sp", ckf, I["cv"][l, sidx])
                S.cp("act", vp[0].re("p g d -> p (g d)"), ckf)
            for bi in range(min(nb, self.dbg.get('maxch', 10 ** 9))):
                t0 = tok0 + bi * Lb
                pr = prow0 + bi * Lb
                prev, cur = (bi % 2, (bi + 1) % 2)
                Wk = 128 + Lb
                mask = (mp0 if bi == 0 else mp) if prompt else ms
                z = zz[kB % 2]
                self.rope(qkr, PB, cosT, sinT, Lb, 18, 64, t1, t2)
                S.cp("act", qkb[:Lb, :], qkr[:Lb, :])
                S.cp("act", vp[cur][:Lb].re("p g d -> p (g d)"), PB[:Lb, 1152:1280])
                psq = [PS[0].bits(BF16), PS[1].bits(BF16), PS[2].bits(BF16)]
                for h in range(18):
                    S.tr(psq[h // 8][:64, (h % 8) * 128:(h % 8) * 128 + Lb], qkb[:Lb, h * 64:(h + 1) * 64],
                         self.identb[:Lb, :Lb])
                for g in range(2):
                    S.cp("act" if g else "dve", qT[:, g * 8:(g + 1) * 8, :Lb],
                         psq[g][:64, :].re("p (h t) -> p h t", h=8)[:, :, :Lb])
                S.cp("dve", kTp[cur][:, :, :Lb], psq[2][:64, 0:256].re("p (g t) -> p g t", g=2)[:, :, :Lb])
                sbanks = [PS[0], PS[1], PS[3], PS[4]]

                def pair_chain(hp, slot):
                    g = hp // 4
                    pss = sbanks[slot]
                    s_sb, p_bf, pT, sm = s_sbs[slot], p_bfs[slot], pTs[slot], sms[slot]
                    for j in range(2):
                        h = hp * 2 + j
                        S.mm(pss[:Lb, j * 256:j * 256 + 128], qT[:, h, :Lb], kTp[prev][:, g, :])
                        S.mm(pss[:Lb, j * 256 + 128:j * 256 + 128 + Lb], qT[:, h, :Lb], kTp[cur][:, g, :Lb])
                    yield
                    ps3 = pss[:Lb, :].re("p (j c) -> p j c", j=2)[:, :, :Wk]
                    S.stt(s_sb[:Lb, :, :Wk], ps3, 0.125, mask[:Lb, :Wk].un(1).bc([Lb, 2, Wk]), ALU.mult, ALU.add)
                    yield
                    mx, m, negm, rs, dd, den = sm["s0"], sm["s1"], sm["s2"], sm["s3"], sm["s4"], sm["s5"]
                    S.red(mx[:Lb, :], s_sb[:Lb, :, :Wk], ALU.max)
                    yield
                    S.tt("dve", m[:Lb, :], mx[:Lb, :], sink[:Lb, hp * 2:hp * 2 + 2], ALU.max)
                    yield
                    S.ts("dve", negm[:Lb, :], m[:Lb, :], -1.0, ALU.mult)
                    S.tt("pool", dd[:Lb, :], sink[:Lb, hp * 2:hp * 2 + 2], m[:Lb, :], ALU.subtract)
                    yield
                    for j in range(2):
                        S.act(p_bf[:Lb, j, :Wk], s_sb[:Lb, j, :Wk], AF.Exp, bias=negm[:Lb, j:j + 1], accum=rs[:Lb, j:j + 1])
                    S.act(dd[:Lb, :], dd[:Lb, :], AF.Exp)
                    yield
                    pst = sbanks[slot].bits(BF16)
                    po = 0
                    for j in range(2):
                        S.tr(pst[:128, po + (2 * j) * 128:po + (2 * j) * 128 + Lb], p_bf[:Lb, j, 0:128], self.identb[:Lb, :Lb])
                        S.tr(pst[:Lb, po + (2 * j + 1) * 128:po + (2 * j + 1) * 128 + Lb], p_bf[:Lb, j, 128:128 + Lb],
                             self.identb[:Lb, :Lb])
                    S.tt("dve", den[:Lb, :], rs[:Lb, :], dd[:Lb, :], ALU.add)
                    yield
                    S.cp("act", pT[:, :, :Lb], pst[:, po:po + 512].re("p (s t) -> p s t", s=4)[:, :, :Lb])
                    S.recip(rden[:Lb, hp * 2:hp * 2 + 2], den[:Lb, :])
                    yield
                    for j in range(2):
                        h = hp * 2 + j
                        pso = PS[5 if h >= 8 else 2][:Lb, (h % 8) * 64:(h % 8 + 1) * 64]
                        S.mm(pso, pT[:, 2 * j, :Lb], vp[prev][:, g, :], start=True, stop=False)
                        S.mm(pso, pT[:Lb, 2 * j + 1, :Lb], vp[cur][:Lb, g, :], start=False, stop=True)
                    yield

                for hp0 in range(0, 8, 4):
                    lockstep([pair_chain(hp0 + s, s) for s in range(4)])
                for hh in range(2):
                    S.tt("dve", yb[:Lb, hh * 512:(hh + 1) * 512].re("p (h d) -> p h d", h=8),
                         PS[5 if hh else 2][:Lb, :].re("p (h d) -> p h d", h=8),
                         rden[:Lb, hh * 8:(hh + 1) * 8].un(2).bc([Lb, 8, 64]), ALU.mult)
                if bi == nb - 1:
                    if prompt:
                        S.dma("sp", O["o_ck"][l, sidx], qkr[:, 1024:1152])
                        S.dma("sp", O["o_cv"][l, sidx], PB[:, 1152:1280])
                    else:
                        S.dma("sp", O["o_ck"][l, sidx][0:128 - Lb, :], I["ck"][l, sidx][Lb:128, :])
                        S.dma("sp", O["o_cv"][l, sidx][0:128 - Lb, :], I["cv"][l, sidx][Lb:128, :])
                        S.dma("sp", O["o_ck"][l, sidx][128 - Lb:128, :], qkr[:Lb, 1024:1152])
                        S.dma("sp", O["o_cv"][l, sidx][128 - Lb:128, :], PB[:Lb, 1152:1280])
                kB += 1
                if kB < len(bitems):
                    loadsB(kB)
                self.gate_store(yb, z, Lb, t0, 1)

    def branchC(self, l, es):
        S, I, O, PS = self.S, self.I, self.O, self.psmap.get("C", self.PS)
        sb = lambda shape, dt=F32, n="c": S.sb_in(es, shape, dt, n)
        tri = sb([64, 64]); S.dma("sp", tri, I["c_tri"])
        madd = sb([64, 64]); S.dma("sp", madd, I["c_madd"])
        cw = [self.bload(es, I["c_conv_w"][l, j:j + 1, :]) for j in range(4)]
        cb = self.bload(es, I["c_conv_b"][l:l + 1, :])
        gng = self.bload(es, I["c_ln_g"][l:l + 1, :]); gnb = self.bload(es, I["c_ln_b"][l:l + 1, :])
        ibias = sb([8, 1]); S.dma("sp", ibias, I["c_i_bias"][l].re("(h o) -> h o", o=1))
        fbias = sb([8, 1]); S.dma("sp", fbias, I["c_f_bias"][l].re("(h o) -> h o", o=1))
        negfb = sb([8, 1]); S.ts("dve", negfb, fbias, -1.0, ALU.mult)
        ones8 = sb([8, 64]); S.memset("pool", ones8, 1.0)
        zeros8 = sb([8, 64]); S.memset("pool", zeros8, 0.0)
        onesb = sb([64, 1], BF16); S.memset("pool", onesb, 1.0)
        Cf = sb([64, 8, 128]); Cb = sb([64, 8, 128], BF16)
        nf = sb([64, 8]); nb_ = sb([64, 8], BF16)
        mfm = sb([8, 1])
        U = [sb([64, W]) for _ in range(4)]
        Vt = sb([64, W]); gates = sb([64, 16]); zz = [sb([64, W]) for _ in range(3)]
        conv = sb([64, W]); tmp = sb([64, W])
        FR = sb([8, 8, 64]); tf1 = sb([8, 64]); tf2 = sb([8, 64]); negMl = sb([8, 1]); dg = sb([8, 8])
        BD = sb([8, 8, 64]); tm = sb([64, 4, 8])
        wpre = sb([64, 8, 64]); wts = sb([64, 8, 64])
        vb = sb([64, 8, 128], BF16); qT = sb([64, 8, 64], BF16); kT = sb([64, 8, 64], BF16)
        AT = sb([64, 8, 64], BF16); ktb = sb([64, 8, 64], BF16)
        n1s = [sb([64, W]) for _ in range(2)]; dens = [sb([128, 16]) for _ in range(2)]; dq = sb([64, 16]); sR = sb([64, 8])
        sc = self.small(es); sc["sq"] = sb([64, W])
        id8 = self.ident[:8, :8]
        citems = []
        for (tok0, L, T, sidx, prow0) in self.seqs():
            for ci in range(min(L // T, self.dbg.get('maxch', 10 ** 9))):
                citems.append((tok0, T, sidx, ci))

        def loadsC(k):
            tok0_, T_, sidx_, ci_ = citems[k]
            t0_ = tok0_ + ci_ * T_
            for j in range(4):
                sh = 3 - j
                if ci_ == 0:
                    if sh > 0:
                        S.dma("sp", U[j][0:sh, :], I["mconv"][l, sidx_][j:3, :])
                    S.dma("sp", U[j][sh:T_, :], self.P[t0_:t0_ + T_ - sh, OC:OC + W])
                else:
                    S.dma("sp", U[j][:T_, :], self.P[t0_ - sh:t0_ - sh + T_, OC:OC + W])
            S.dma("sp", Vt[:T_, :], self.P[t0_:t0_ + T_, OC + W:OC + 2 * W])
            S.dma("sp", gates[:T_, :], self.P[t0_:t0_ + T_, OC + 2 * W:OC + 2 * W + 16])
            S.dma("sp", zz[k % 3][:T_, :], self.P[t0_:t0_ + T_, OZ + 2 * W:OZ + 3 * W])

        loadsC(0)
        kC = 0
        for (tok0, L, T, sidx, prow0) in self.seqs():
            S.dma("sp", Cf, I["mC"][l, sidx].re("h d v -> d h v"))
            S.cp("act", Cb, Cf)
            S.dma("sp", nf, I["mn"][l, sidx].re("h d -> d h"), _nc=True)
            S.cp("act", nb_, nf)
            S.dma("sp", mfm, I["mm"][l, sidx].re("(h o) -> h o", o=1))
            for ci in range(min(L // T, self.dbg.get('maxch', 10 ** 9))):
                t0 = tok0 + ci * T
                z = zz[kC % 3]
                n1 = n1s[kC % 2]
                kcur = kC
                if kC >= 2:
                    S.wait(("Cdone", l, kC - 2))
                S.tt("pool", conv[:T, :], U[0][:T, :], cw[0][:T, :], ALU.mult)
                for j in range(1, 4):
                    S.tt("dve" if j % 2 else "pool", tmp[:T, :], U[j][:T, :], cw[j][:T, :], ALU.mult)
                    S.tt("pool", conv[:T, :], conv[:T, :], tmp[:T, :], ALU.add)
                S.tt("dve", conv[:T, :], conv[:T, :], cb[:T, :], ALU.add)
                S.act(conv[:T, :], conv[:T, :], AF.Silu)
                S.cp("act", vb[:T].re("p h d -> p (h d)"), Vt[:T, :])
                S.tr(PS[0][:8, 0:T], gates[:T, 0:8], self.ident[:T, :T])
                S.tr(PS[0][:8, T:2 * T], gates[:T, 8:16], self.ident[:T, :T])
                li, lf, bb, gg, MM, scf, emt, wj = [FR[:, i, :T] for i in range(8)]
                S.ts("dve", li, PS[0][:8, 0:T], ibias, ALU.add)
                S.act(tf1[:, :T], PS[0][:8, T:2 * T], AF.Exp, bias=negfb, scale=-1.0)
                S.act(tf2[:, :T], tf1[:, :T], AF.Ln, bias=1.0)
                S.ts("pool", lf, tf2[:, :T], -1.0, ALU.mult)
                S.scan(bb, lf, zeros8[:, :T], 0.0, ALU.add, ALU.add)
                S.tt("dve", gg, li, bb, ALU.subtract)
                S.scan(MM, gg, gg, mfm, ALU.max, ALU.max)
                S.act(scf, MM, AF.Exp, bias=mfm, scale=-1.0)
                S.tt("dve", tf1[:, :T], bb, MM, ALU.add)
                S.act(emt, tf1[:, :T], AF.Exp, scale=-1.0)
                S.ts("dve", negMl, FR[:, 4, T - 1:T], -1.0, ALU.mult)
                S.act(wj, gg, AF.Exp, bias=negMl)
                for i, src in enumerate((gg, scf, emt, wj)):
                    S.tr(PS[1][:T, i * 8:(i + 1) * 8], src, id8)
                S.cp("dve", tm[:T].re("p a h -> p (a h)"), PS[1][:T, 0:32])
                g_tm, sc_tm, emt_tm, wj_tm = [tm[:T, i, :] for i in range(4)]
                S.tt("pool", BD[:, :, :T], MM.un(1).bc([8, 8, T]), id8.un(2).bc([8, 8, T]), ALU.mult)
                S.mm(PS[2][:T, :8 * T], ones8[:, :T], BD[:, :, :T])
                S.stt(wpre[:T, :, :T], PS[2][:T, :8 * T].re("p (h t) -> p h t", h=8), -1.0,
                      madd[:T, :T].un(1).bc([T, 8, T]), ALU.mult, ALU.add)
                for h in range(8):
                    S.act(wts[:T, h, :T], wpre[:T, h, :T], AF.Exp, bias=g_tm[:, h:h + 1])
                for h in range(8):
                    S.tr(PS[3][:64, h * T:(h + 1) * T], conv[:T, h * 64:(h + 1) * 64], self.ident[:T, :T])
                    S.tr(PS[4][:64, h * T:(h + 1) * T], conv[:T, 512 + h * 64:512 + (h + 1) * 64], self.ident[:T, :T])
                S.cp("act", qT[:, :, :T], PS[3][:64, :8 * T].re("p (h t) -> p h t", h=8))
                S.cp("dve", kT[:, :, :T], PS[4][:64, :8 * T].re("p (h t) -> p h t", h=8))
                kC += 1
                if kC < len(citems):
                    loadsC(kC)
                for h in range(8):
                    S.mm(PS[5][:T, h * T:(h + 1) * T], kT[:, h, :T], qT[:, h, :T])
                S.stt(AT[:T, :, :T], PS[5][:T, :8 * T].re("p (h t) -> p h t", h=8), 0.125, wts[:T, :, :T], ALU.mult, ALU.mult)
                for h in range(8):
                    S.mm(PS[6 + h // 4][:T, (h % 4) * 128:(h % 4 + 1) * 128], AT[:T, h, :T], vb[:T, h, :])
                    S.mm(PS[0][:T, h:h + 1], AT[:T, h, :T], onesb[:T, :])
                    S.mm(PS[0][:T, 8 + h:9 + h], qT[:, h, :T], nb_[:, h:h + 1])
                S.cp("act", n1[:T, 0:512], PS[6][:T, :])
                S.cp("act", n1[:T, 512:1024], PS[7][:T, :])
                S.cp("dve", dq[:T, :], PS[0][:T, 0:16])
                for h in range(8):
                    S.mm(PS[6 + h // 4][:T, (h % 4) * 128:(h % 4 + 1) * 128], qT[:, h, :T], Cb[:, h, :])
                den, aden = dens[kcur % 2], sc["s7"]
                S.tt("dve", den[:T, :8], sc_tm, dq[:T, 8:16], ALU.mult)
                S.tt("dve", den[:T, :8], den[:T, :8], dq[:T, 0:8], ALU.add)
                S.ts("dve", aden[:T, :8], den[:T, :8], -1.0, ALU.mult)
                S.tt("dve", aden[:T, :8], aden[:T, :8], den[:T, :8], ALU.max)
                S.tt("dve", aden[:T, :8], aden[:T, :8], emt_tm, ALU.max)
                S.recip(den[:T, :8], aden[:T, :8])
                for hh in range(2):
                    S.tt("dve", tmp[:T, hh * 512:(hh + 1) * 512].re("p (h d) -> p h d", h=4),
                         PS[6 + hh][:T, :].re("p (h d) -> p h d", h=4),
                         sc_tm[:, hh * 4:(hh + 1) * 4].un(2).bc([T, 4, 128]), ALU.mult)
                S.tt("pool", n1[:T, :], n1[:T, :], tmp[:T, :], ALU.add)
                n13 = n1[:T, :].re("p (h d) -> p h d", h=8)
                S.tt("pool", n13, n13, den[:T, :8].un(2).bc([T, 8, 128]), ALU.mult)
                S.stt(ktb[:T], conv[:T, 512:1024].re("p (h d) -> p h d", h=8), 0.125,
                      wj_tm.un(2).bc([T, 8, 64]), ALU.mult, ALU.mult)
                for h in range(8):
                    S.mm(PS[3 + h // 4][:64, (h % 4) * 128:(h % 4 + 1) * 128], ktb[:T, h, :], vb[:T, h, :])
                    S.mm(PS[5][:64, h:h + 1], ktb[:T, h, :], onesb[:T, :])
                S.ts("dve", dg, id8, FR[:, 5, T - 1:T], ALU.mult)
                S.mm(PS[5][:64, 16:24], ones8[:, :64], dg)
                S.cp("dve", sR, PS[5][:64, 16:24])
                S.tt("pool", Cf, Cf, sR.un(2).bc([64, 8, 128]), ALU.mult)
                Cf2 = Cf.re("p h d -> p (h d)")
                S.tt("dve", Cf2[:, 0:512], Cf2[:, 0:512], PS[3][:64, :], ALU.add)
                S.tt("dve", Cf2[:, 512:1024], Cf2[:, 512:1024], PS[4][:64, :], ALU.add)
                S.cp("act", Cb, Cf)
                S.tt("dve", nf, nf, sR, ALU.mult)
                S.tt("dve", nf, nf, PS[5][:64, 0:8], ALU.add)
                S.cp("act", nb_, nf)
                S.tt("dve", mfm, FR[:, 2, T - 1:T], FR[:, 4, T - 1:T], ALU.add)
                S.signal(("Cmain", l, kcur))
                with S.sub("post"):
                    S.wait(("Cmain", l, kcur))
                    self.headnorm(n1, T, 8, 128, gng, gnb, sc, on_act=True)
                    self.gate_store(n1, z, T, t0, 2)
                    S.signal(("Cdone", l, kcur))
            S.dma("sp", O["o_mC"][l, sidx].re("h d v -> d h v"), Cf)
            S.dma("sp", O["o_mn"][l, sidx].re("h d -> d h"), nf, _nc=True)
            S.dma("sp", O["o_mm"][l, sidx].re("(h o) -> h o", o=1), mfm)
            S.dma("sp", O["o_mconv"][l, sidx], self.P[tok0 + L - 3:tok0 + L, OC:OC + W])

    def branchA(self, l, es):
        S, I, O, PS = self.S, self.I, self.O, self.PS
        sb = lambda shape, dt=F32, n="a": S.sb_in(es, shape, dt, n)
        tri = sb([64, 64]); S.dma("sp", tri, I["c_tri"])
        m5 = {64: sb([64, 320]), 8: sb([64, 320])}
        S.dma("sp", m5[64], I["c_m5_64"]); S.dma("sp", m5[8], I["c_m5_8"])
        mu = self.bload(es, I["a_mu"][l:l + 1, :], 3200)
        w0 = self.bload(es, I["a_w0"][l:l + 1, :]); a0 = self.bload(es, I["a_a0"][l:l + 1, :])
        kk_ = self.bload(es, I["a_k_k"][l:l + 1, :]); ka = self.bload(es, I["a_k_a"][l:l + 1, :])
        rk_ = self.bload(es, I["a_r_k"][l:l + 1, :])
        gng = self.bload(es, I["a_ln_g"][l:l + 1, :]); gnb = self.bload(es, I["a_ln_b"][l:l + 1, :])
        WA = sb([128, W])
        S.dma("sp", WA[0:64, :], I["a_w_up"][l]); S.dma("sp", WA[64:128, :], I["a_a_up"][l])
        onesf = sb([64, 1]); S.memset("pool", onesf, 1.0)
        Hf = sb([64, 16, 64]); Hb = sb([64, 16, 64], BF16)
        Sin = sb([64, 16, 64])
        PA = sb([64, 3200]); PV = sb([64, 3200])
        L2 = sb([64, 128]); LT = sb([128, 64])
        sw = sb([64, W]); aa = sb([64, W]); kk = sb([64, W]); km = sb([64, W]); tA = sb([64, W]); tB = sb([64, W])
        eP = sb([64, W]); eN = sb([64, W]); ePm = sb([64, W])
        psm = self.small(es, 3)
        sets = []
        for _ in range(2):
            sets.append(dict(Xb=[sb([64, W], BF16) for _ in range(4)],
                             vb=sb([64, W], BF16), FM=sb([64, 16, 4, 64], BF16), PT=sb([64, 16])))
        zb3 = [dict(z=sb([64, W]), bonus=sb([64, W])) for _ in range(3)]
        Gm = [sb([64, 5, 64], BF16) for _ in range(8)]
        Xs = [[sb([64, 64], BF16)] for _ in range(8)]
        XL = [sb([64, 192], BF16) for _ in range(8)]
        ys = [sb([64, W]) for _ in range(2)]
        sc = self.small(es); sc["sq"] = sb([64, W])
        PQ = [PS[6], PS[7]]
        YB, HB = PS[0], PS[1]
        work = [PS[2], PS[3], PS[4], PS[5]]

        items = []
        for (tok0, L, T, sidx, prow0) in self.seqs():
            nch = min(L // T, self.dbg.get('maxch', 10 ** 9))
            for ci in range(nch):
                items.append((tok0, L, T, sidx, ci, ci == 0, ci == nch - 1))

        def prep(item, B, ZB):
            tok0, L, T, sidx, ci, first, last = item
            z, bonus, Xb, vb, FM, PT = ZB["z"], ZB["bonus"], B["Xb"], B["vb"], B["FM"], B["PT"]
            t0 = tok0 + ci * T
            S.dma("sp", PA[:T, :], self.P[t0:t0 + T, 0:3200])
            if ci == 0:
                S.dma("sp", PV[0:1, :], I["rshift"][l, sidx:sidx + 1, :])
                S.dma("sp", PV[1:T, :], self.P[t0:t0 + T - 1, 0:3200])
            else:
                S.dma("sp", PV[:T, :], self.P[t0 - 1:t0 - 1 + T, 0:3200])
            S.dma("sp", z[:T, :], self.P[t0:t0 + T, OZ:OZ + W])
            def shift(en, c0, c1):
                S.tt(en, PV[:T, c0:c1], PV[:T, c0:c1], PA[:T, c0:c1], ALU.subtract)
                S.tt(en, PV[:T, c0:c1], PV[:T, c0:c1], mu[:T, c0:c1], ALU.mult)
                S.tt(en, PV[:T, c0:c1], PV[:T, c0:c1], PA[:T, c0:c1], ALU.add)
            shift("dve", 3 * W, 3200)
            r, k, v = PV[:T, 0:W], PV[:T, W:2 * W], PV[:T, 2 * W:3 * W]
            S.act(L2[:T, 0:64], PV[:T, 3072:3136], AF.Tanh)
            S.cp("dve", L2[:T, 64:128], PV[:T, 3136:3200])
            S.tr(PQ[0][:, 0:T], L2[:T, :], self.ident[:T, :T])
            S.cp("act", LT[:, :T], PQ[0][:, 0:T])
            shift("pool", 0, 3 * W)
            for hh in range(2):
                S.mm(PQ[hh][:T, :], LT[0:64, :T], WA[0:64, hh * 512:(hh + 1) * 512])
            for hh in range(2):
                cs = slice(hh * 512, (hh + 1) * 512)
                S.tt("dve", sw[:T, cs], PQ[hh][:T, :], w0[:T, cs], ALU.add)
            for hh in range(2):
                S.mm(PQ[hh][:T, :], LT[64:128, :T], WA[64:128, hh * 512:(hh + 1) * 512])
            for hh in range(2):
                cs = slice(hh * 512, (hh + 1) * 512)
                S.tt("dve", aa[:T, cs], PQ[hh][:T, :], a0[:T, cs], ALU.add)
            S.act(sw[:T, :], sw[:T, :], AF.Sigmoid)
            S.act(aa[:T, :], aa[:T, :], AF.Sigmoid)
            S.tt("pool", kk[:T, :], k, kk_[:T, :], ALU.mult)
            S.act(tA[:T, :], kk[:T, :], AF.Square)
            ss, rn, bs = psm["s0"], psm["s1"], psm["s2"]
            S.red(ss[:T, :16], tA[:T, :].re("p (h d) -> p h d", h=16), ALU.add)
            S.rsqrt(rn[:T, :16], ss[:T, :16], 1e-24, ALU.max)
            kk3 = kk[:T, :].re("p (h d) -> p h d", h=16)
            S.tt("pool", kk3, kk3, rn[:T, :16].un(2).bc([T, 16, 64]), ALU.mult)
            S.stt(tA[:T, :], aa[:T, :], -1.0, ka[:T, :], ALU.add, ALU.mult)
            S.stt(km[:T, :], tA[:T, :], 1.0, k, ALU.add, ALU.mult)
            S.tt("dve", tB[:T, :], r, km[:T, :], ALU.mult)
            S.tt("dve", tB[:T, :], tB[:T, :], rk_[:T, :], ALU.mult)
            S.red(bs[:T, :16], tB[:T, :].re("p (h d) -> p h d", h=16), ALU.add)
            S.tt("pool", bonus[:T, :].re("p (h d) -> p h d", h=16), v.re("p (h d) -> p h d", h=16),
                 bs[:T, :16].un(2).bc([T, 16, 64]), ALU.mult)
            for hh in range(2):
                S.mm(PQ[hh][:T, :], tri[:T, :T], sw[:T, hh * 512:(hh + 1) * 512])
            for hh in range(2):
                cs = slice(hh * 512, (hh + 1) * 512)
                S.act(eP[:T, cs], PQ[hh][:T, :], AF.Exp, scale=C0)
                S.act(eN[:T, cs], PQ[hh][:T, :], AF.Exp, scale=-C0)
                S.tt("dve", ePm[:T, cs], PQ[hh][:T, :], sw[:T, cs], ALU.subtract)
            S.act(ePm[:T, :], ePm[:T, :], AF.Exp, scale=C0)
            S.tt("pool", tB[:T, :], kk[:T, :], aa[:T, :], ALU.mult)
            S.tt("pool", Xb[0][:T, :], tB[:T, :], eN[:T, :], ALU.mult)
            S.tt("dve", Xb[1][:T, :], km[:T, :], eN[:T, :], ALU.mult)
            S.stt(Xb[2][:T, :], kk[:T, :], -1.0, ePm[:T, :], ALU.mult, ALU.mult)
            S.tt("pool", Xb[3][:T, :], r, eP[:T, :], ALU.mult)
            S.cp("act", vb[:T, :], v)
            for h in range(16):
                S.mm(PQ[0][:64, h:h + 1], sw[:T, h * 64:(h + 1) * 64], onesf[:T, :])
            S.act(PT, PQ[0][:64, 0:16], AF.Exp, scale=C0)
            for g in range(4):
                psb = PQ[(g + 1) % 2].bits(BF16)
                for hq in range(4):
                    h = g * 4 + hq
                    for q in range(4):
                        S.tr(psb[:64, (hq * 4 + q) * 64:(hq * 4 + q) * 64 + T], Xb[q][:T, h * 64:(h + 1) * 64],
                             self.identb[:T, :T])
                S.cp("act" if g % 2 else "dve", FM[:, g * 4:(g + 1) * 4, :, :T],
                     psb[:64, :].re("p (h q t) -> p h q t", h=4, q=4)[:, :, :, :T])

        def back(item, B, y):
            tok0, L, T, sidx, ci, first, last = item
            Xb, vb, FM, PT = B["Xb"], B["vb"], B["FM"], B["PT"]
            t0 = tok0 + ci * T
            nlev = {64: 6, 8: 3}[T]
            if first:
                S.dma("sp", Sin, I["rS"][l, sidx].re("h v k -> v h k"))
                for h in range(16):
                    S.tr(PS[h // 8][:64, (h % 8) * 64:(h % 8 + 1) * 64], Sin[:, h, :], self.ident[:64, :64])
                for hh in range(2):
                    S.cp("act", Hf[:, hh * 8:(hh + 1) * 8, :].re("p h v -> p (h v)"), PS[hh][:64, :])
                S.cp("dve", Hb, Hf)

            def head_chain(h, slot):
                cs = slice(h * 64, (h + 1) * 64)
                gm = Gm[slot]
                reg = work[slot // 2]
                xo = (slot % 2) * 256
                S.mm(reg[:T, xo:xo + 2 * T], FM[:, h, 0, :T], FM[:, h, 2:4, :T])
                S.mm(reg[:T, xo + 2 * T:xo + 4 * T], FM[:, h, 1, :T], FM[:, h, 2:4, :T])
                yield
                S.tt("dve", gm[:T, 0:4, :T], reg[:T, xo:xo + 4 * T].re("p (q t) -> p q t", q=4),
                     m5[T][:T, :4 * T].re("p (q t) -> p q t", q=4), ALU.mult)
                yield
                labT, rabT, lakT, rakT = [gm[:T, q, :T] for q in range(4)]
                S.mm(reg[:T, xo:xo + T], FM[:, h, 2, :T], FM[:, h, 0, :T])
                S.mm(reg[:T, xo + 64:xo + 128], FM[:, h, 2, :T], Hb[:, h, :], start=True, stop=False)
                S.mm(reg[:T, xo + 64:xo + 128], lakT, vb[:T, cs], start=False, stop=True)
                yield
                S.tt("dve", gm[:T, 4, :T], reg[:T, xo:xo + T], m5[T][:T, 4 * T:5 * T], ALU.mult)
                X = Xs[slot][0]
                S.cp("act", X[:T, :], reg[:T, xo + 64:xo + 128])
                yield
                Lt, Ln = labT, gm[:T, 4, :T]
                for lev in range(nlev):
                    with S.atomic():
                        S.mm(reg[:T, xo:xo + 64], self.identb[:T, :T], X[:T, :], start=True, stop=False)
                        S.mm(reg[:T, xo:xo + 64], Lt, X[:T, :], start=False, stop=True)
                    if lev < nlev - 1:
                        S.mm(reg[:T, xo + 64:xo + 64 + T], Ln, Lt)
                        S.mm(reg[:T, xo + 64 + T:xo + 64 + 2 * T], Lt, Ln)
                    yield
                    xl = XL[slot]
                    wdt = 64 + 2 * T if lev < nlev - 1 else 64
                    S.cp("act", xl[:T, :wdt], reg[:T, xo:xo + wdt])
                    X = xl[:, 0:64]
                    if lev < nlev - 1:
                        Lt, Ln = xl[:T, 64:64 + T], xl[:T, 64 + T:64 + 2 * T]
                    yield
                psy = YB[:T, (h % 8) * 64:(h % 8 + 1) * 64]
                S.mm(psy, FM[:, h, 3, :T], Hb[:, h, :], start=True, stop=False)
                S.mm(psy, rabT, X[:T, :], start=False, stop=False)
                S.mm(psy, rakT, vb[:T, cs], start=False, stop=True)
                psh = HB[:64, (h % 8) * 64:(h % 8 + 1) * 64]
                S.mm(psh, Xb[0][:T, cs], X[:T, :], start=True, stop=False)
                S.mm(psh, Xb[1][:T, cs], vb[:T, cs], start=False, stop=True)
                yield

            for hh in range(2):
                lockstep([head_chain(hh * 8 + s, s) for s in range(8)])
                S.cp("act", y[:T, hh * 512:(hh + 1) * 512], YB[:T, :])
                Hh = Hf[:, hh * 8:(hh + 1) * 8, :]
                S.tt("dve", Hh.re("p h v -> p (h v)"), Hh.re("p h v -> p (h v)"), HB[:64, :], ALU.add)
                S.tt("pool", Hh, Hh, PT[:, hh * 8:(hh + 1) * 8].un(2).bc([64, 8, 64]), ALU.mult)
                if not (hh == 1 and last):
                    S.cp("act", Hb[:, hh * 8:(hh + 1) * 8, :], Hh)
            if last:
                for h in range(16):
                    S.tr(PS[h // 8][:64, (h % 8) * 64:(h % 8 + 1) * 64], Hf[:, h, :], self.ident[:64, :64])
                for hh in range(2):
                    S.cp("act", Sin[:, hh * 8:(hh + 1) * 8, :].re("p h v -> p (h v)"), PS[hh][:64, :])
                S.dma("sp", O["o_rS"][l, sidx].re("h v k -> v h k"), Sin)
                S.dma("sp", O["o_rshift"][l, sidx:sidx + 1, :], self.P[tok0 + L - 1:tok0 + L, 0:3200])

        def post(item, ZB, y):
            tok0, L, T, sidx, ci, first, last = item
            self.headnorm(y, T, 16, 64, gng, gnb, sc)
            S.tt("pool", y[:T, :], y[:T, :], ZB["bonus"][:T, :], ALU.add)
            self.gate_store(y, ZB["z"], T, tok0 + ci * T, 0)

        prep(items[0], sets[0], zb3[0])
        for k, item in enumerate(items):
            recs = []
            if k + 1 < len(items):
                recs.append(S.record(lambda: prep(items[k + 1], sets[(k + 1) % 2], zb3[(k + 1) % 3])))
            recs.append(S.record(lambda: back(item, sets[k % 2], ys[k % 2])))
            if k >= 1:
                recs.append(S.record(lambda: post(items[k - 1], zb3[(k - 1) % 3], ys[(k - 1) % 2])))
            S.play(recs)
        post(items[-1], zb3[(len(items) - 1) % 3], ys[(len(items) - 1) % 2])


    def phase3a(self, l):
        S = self.S
        with contextlib.ExitStack() as es:
            WBRs = [S.sb_in(es, [128, 8, D_MODEL], BF16, "WBR") for _ in range(4)]
            for i in range(4):
                S.dma("pool", WBRs[i], self.I["w_branch"][l, i].re("(k p) c -> p k c", p=128))
            yz = S.sb_in(es, [128, 4 * W], F32, "yz")
            yzT = [S.sb_in(es, [128, 32, 128], BF16, "yzT") for _ in range(2)]
            Gb = [S.sb_in(es, [128, D_MODEL], F32, "G") for _ in range(2)]
            mg = [S.sb_in(es, [128, D_MODEL], F32, "mg") for _ in range(1)]
            mgb = [S.sb_in(es, [128, D_MODEL], BF16, "mgb") for _ in range(1)]
            tmp = [S.sb_in(es, [128, 512], F32, "tmp3") for _ in range(2)]

            def rows_(tt):
                return slice(tt * 128, (tt + 1) * 128)

            def load_yz(tt):
                S.dma("sp", yz, self.YZ[rows_(tt), :])

            def load_G(q):
                tt, i = divmod(q, 4)
                S.dma("sp", Gb[q % 2], self.P[rows_(tt), OG + i * D_MODEL:OG + (i + 1) * D_MODEL])

            load_yz(0)
            load_G(0)
            n = 0
            for tt in range(NTT):
                yT = yzT[tt % 2]
                for g in range(8):
                    ps = self.PS[g % 8]
                    for j in range(4):
                        k = g * 4 + j
                        S.tr(ps[:, j * 128:(j + 1) * 128], yz[:, k * 128:(k + 1) * 128], self.ident)
                    S.cp("act" if g % 2 else "dve", yT[:, g * 4:(g + 1) * 4, :], ps.re("p (j c) -> p j c", j=4))
                if tt + 1 < NTT:
                    load_yz(tt + 1)
                m_ = mg[0]
                for i in range(4):
                    q = tt * 4 + i
                    G = Gb[q % 2]
                    if q + 1 < NTT * 4:
                        load_G(q + 1)
                    S.act(G, G, AF.Sigmoid)
                    for j in range(4):
                        cs = slice(j * 512, (j + 1) * 512)
                        ps = self.PS[n % 8]
                        n += 1
                        for k in range(8):
                            S.mm(ps, yT[:, i * 8 + k, :], WBRs[i][:, k, cs], start=(k == 0), stop=(k == 7))
                        if i == 0:
                            S.tt("dve", m_[:, cs], ps, G[:, cs], ALU.mult)
                        else:
                            t_ = tmp[n % 2]
                            S.tt("dve", t_, ps, G[:, cs], ALU.mult)
                            S.tt("pool", m_[:, cs], m_[:, cs], t_, ALU.add)
                mb = mgb[0]
                S.cp("act", mb, m_)
                S.dma("sp", self.MG[rows_(tt), :], mb)

    def phase3b(self, l, xsrc, xdst):
        S = self.S
        with contextlib.ExitStack() as es:
            WOs = [S.sb_in(es, [128, 16, 512], BF16, "WO") for _ in range(4)]
            wov = self.I["w_out"][l].re("(k p) c -> p k c", p=128)
            for j in range(4):
                S.dma("pool", WOs[j], wov[:, :, j * 512:(j + 1) * 512])
            lng = S.sb_in(es, [128, D_MODEL], F32, "lng")
            lnb = S.sb_in(es, [128, D_MODEL], F32, "lnb")
            S.dma("sp", lng, self.I["ln_g"][l:l + 1, :].bc([128, D_MODEL]))
            S.dma("sp", lnb, self.I["ln_b"][l:l + 1, :].bc([128, D_MODEL]))
            mb = [S.sb_in(es, [128, D_MODEL], BF16, "mb") for _ in range(2)]
            mT = [S.sb_in(es, [128, 16, 128], BF16, "mT") for _ in range(2)]
            xr = [S.sb_in(es, [128, D_MODEL], F32, "xr") for _ in range(2)]
            pre = [S.sb_in(es, [128, D_MODEL], F32, "pre") for _ in range(2)]
            st = S.sb_in(es, [128, 4, 6], F32, "bst")
            mv = S.sb_in(es, [128, 2], F32, "bmv")
            rstd = S.sb_in(es, [128, 1], F32, "rstd")
            def loads3b(tt):
                r_ = slice(tt * 128, (tt + 1) * 128)
                S.dma("sp", mb[tt % 2], self.MG[r_, :])
                S.dma("sp", xr[tt % 2], xsrc[r_, :])

            loads3b(0)
            n = 0
            for tt in range(NTT):
                rows = slice(tt * 128, (tt + 1) * 128)
                b_ = mb[tt % 2]
                t_ = mT[tt % 2]
                x_ = xr[tt % 2]
                p_ = pre[tt % 2]
                if tt + 1 < NTT:
                    loads3b(tt + 1)
                for g in range(2):
                    ps = self.PS[n % 8].bits(BF16)
                    n += 1
                    for j in range(8):
                        k = g * 8 + j
                        S.tr(ps[:, j * 128:(j + 1) * 128], b_[:, k * 128:(k + 1) * 128], self.identb)
                    S.cp("act" if g % 2 else "dve", t_[:, g * 8:(g + 1) * 8, :], ps.re("p (j c) -> p j c", j=8))
                for j in range(4):
                    cs = slice(j * 512, (j + 1) * 512)
                    ps = self.PS[n % 8]
                    n += 1
                    for k in range(16):
                        S.mm(ps, t_[:, k, :], WOs[j][:, k, :], start=(k == 0), stop=(k == 15))
                    S.stt(p_[:, cs], x_[:, cs], ALPHA, ps, ALU.mult, ALU.add)
                    S.op("dve", lambda e, j=j, p_=p_, cs=cs: e.bn_stats(out=st.ap[:, j, :], in_=p_.ap[:, cs]), [p_], [st])
                S.op("dve", lambda e: e.bn_aggr(out=mv.ap, in_=st.ap), [st], [mv])
                S.rsqrt(rstd, mv[:, 1:2], LN_EPS, ALU.add)
                S.ts("dve", p_, p_, mv[:, 0:1], ALU.subtract, rstd, ALU.mult)
                S.tt("pool", p_, p_, lng, ALU.mult)
                S.tt("pool", p_, p_, lnb, ALU.add)
                S.dma("sp", xdst[rows, :], p_)


WSHAPES = {
    "w_in": [DEPTH, D_MODEL, IN_COLS], "a_mu": [DEPTH, 3200], "a_w0": [DEPTH, W], "a_w_up": [DEPTH, 64, W],
    "a_a0": [DEPTH, W], "a_a_up": [DEPTH, 64, W], "a_k_k": [DEPTH, W], "a_k_a": [DEPTH, W], "a_r_k": [DEPTH, W],
    "a_ln_g": [DEPTH, W], "a_ln_b": [DEPTH, W], "b_sinks": [DEPTH, 16], "c_conv_w": [DEPTH, 4, W],
    "c_conv_b": [DEPTH, W], "c_i_bias": [DEPTH, 8], "c_f_bias": [DEPTH, 8], "c_ln_g": [DEPTH, W],
    "c_ln_b": [DEPTH, W], "d_ln_g": [DEPTH, W], "d_ln_b": [DEPTH, W], "w_branch": [DEPTH, 4, W, D_MODEL],
    "w_out": [DEPTH, D_MODEL, D_MODEL], "ln_g": [DEPTH, D_MODEL], "ln_b": [DEPTH, D_MODEL],
}
CONST_SHAPES = {
    "c_ident": [128, 128], "c_tri": [64, 64], "c_madd": [64, 64],
    "c_rbc": [NPOSROW, 64], "c_rbs": [NPOSROW, 64], "c_rdc": [NPOSROW, 128], "c_rds": [NPOSROW, 128],
    "c_dmt": [64, 8, 64], "c_qdec": [128, 8, 64], "c_kdec64": [64, 1024], "c_kdec8": [64, 1024],
    "c_cdec64": [128, 1024], "c_cdec8": [128, 1024], "c_mp": [128, 256], "c_mp0": [128, 256], "c_ms": [128, 256],
    "c_m5_64": [64, 320], "c_m5_8": [64, 320],
}


def make_consts():
    c = {}
    f32 = np.float32
    c["c_ident"] = np.eye(128, dtype=f32)
    s = np.arange(64)[:, None]
    t = np.arange(64)[None, :]
    c["c_tri"] = (s <= t).astype(f32)
    c["c_madd"] = np.where(s <= t, 0.0, -1e30).astype(f32)
    pos = np.concatenate([np.arange(LP), 8192 + np.arange(LS)]).astype(f32)

    def rope_tab(d):
        inv = (f32(10000.0) ** (-np.arange(0, d, 2, dtype=f32) / f32(d))).astype(f32)
        ang = (pos[:, None] * inv[None, :]).astype(f32)
        cs = np.cos(ang).astype(f32)
        sn = np.sin(ang).astype(f32)
        ct = np.stack([cs, cs], 1)
        st = np.stack([-sn, sn], 1)
        return ct.reshape(NPOSROW, -1), st.reshape(NPOSROW, -1)
    c["c_rbc"], c["c_rbs"] = rope_tab(64)
    c["c_rdc"], c["c_rds"] = rope_tab(128)
    ksc = float(f32(128.0) ** f32(-0.5))
    lg = np.log1p(-np.exp2(-5.0 - np.arange(8, dtype=np.float64)))
    rel = (t - s).astype(np.float64)
    dmt = np.zeros((64, 8, 64), f32)
    for h in range(8):
        dmt[:, h, :] = np.where(rel >= 0, np.exp(np.maximum(rel, 0.0) * lg[h]), 0.0) * ksc
    c["c_dmt"] = dmt
    idx = np.arange(64, dtype=np.float64)
    qd = np.exp((idx + 1.0)[None, :] * lg[:, None])
    c["c_qdec"] = np.broadcast_to(qd[None], (128, 8, 64)).astype(f32).copy()
    for Lc in (64, 8):
        kd = np.zeros((64, 8, 128), f32)
        for h in range(8):
            kd[:Lc, h, :] = (np.exp((Lc - 1.0 - idx[:Lc]) * lg[h]) * ksc)[:, None]
        c["c_kdec%d" % Lc] = kd.reshape(64, 1024)
        cd = np.zeros((128, 8, 128), f32)
        for h in range(8):
            cd[:, h, :] = np.exp(Lc * lg[h])
        c["c_cdec%d" % Lc] = cd.reshape(128, 1024)
    a = np.arange(128)[:, None]
    cc = np.arange(256)[None, :]
    ok = (cc >= a) & (cc <= 128 + a)
    c["c_mp"] = np.where(ok, 0.0, -1e30).astype(f32)
    c["c_mp0"] = np.where(ok & (cc >= 128), 0.0, -1e30).astype(f32)
    c["c_ms"] = np.where(ok & (cc < 136) & (a < 8), 0.0, -1e30).astype(f32)
    for T in (64, 8):
        ss = np.arange(64)[:, None]
        tt_ = np.arange(T)[None, :]
        su = (ss < tt_).astype(f32)
        iu = (ss <= tt_).astype(f32)
        sl = (ss > tt_).astype(f32)
        m5 = np.zeros((64, 320), f32)
        m5[:, :5 * T] = np.concatenate([su, iu, su, iu, sl], 1)
        m5[T:, :] = 0.0
        c["c_m5_%d" % T] = m5
    return c


_PROG = {}


def get_prog(dbg=None):
    key = repr(sorted((dbg or {}).items()))
    if key not in _PROG:
        _PROG[key] = Prog(dbg)
    return _PROG[key]


def make_in_maps(inp):
    f = lambda a: np.ascontiguousarray(np.asarray(a, dtype=np.float32))
    consts = make_consts()
    shared = {n: f(inp[n]).reshape(WSHAPES[n]) for n in WSHAPES}
    st_names = [("rS", "state_rwkv_S"), ("rshift", "state_rwkv_shift"), ("ck", "cache_swa_k"), ("cv", "cache_swa_v"),
                ("mC", "state_mlstm_C"), ("mn", "state_mlstm_n"), ("mm", "state_mlstm_m"),
                ("mconv", "state_mlstm_conv"), ("dS", "state_ret_S")]
    xp = f(inp["x_prompt"])
    xs = f(inp["x_sample"])
    maps = []
    for c in range(NCORES):
        m = dict(shared)
        m.update(consts)
        m["xin"] = np.concatenate([xp[c % 4], xs[c * NSS:(c + 1) * NSS].reshape(NSS * LS, D_MODEL)], 0)
        for kn, full in st_names:
            a = f(inp[full])[:, c * NSS:(c + 1) * NSS]
            z = np.zeros((DEPTH, 1) + a.shape[2:], np.float32)
            a = np.concatenate([z, a], 1)
            if kn in ("ck", "cv"):
                a = a.reshape(DEPTH, NSEQ, 128, 128)
            m[kn] = np.ascontiguousarray(a)
        maps.append(m)
    return maps


def assemble(results):
    y = [r["y"] for r in results]
    y_prompt = np.stack([y[c][:LP] for c in range(4)], 0)
    y_sample = np.concatenate([y[c][LP:].reshape(NSS, LS, D_MODEL) for c in range(NCORES)], 0)
    outs = [y_prompt, y_sample]
    shp = {"o_rS": (16, 64, 64), "o_rshift": (3200,), "o_ck": (128, 2, 64), "o_cv": (128, 2, 64),
           "o_mC": (8, 64, 128), "o_mn": (8, 64), "o_mm": (8,), "o_mconv": (3, 1024), "o_dS": (8, 128, 128)}
    for n in ["o_rS", "o_rshift", "o_ck", "o_cv", "o_mC", "o_mn", "o_mm", "o_mconv", "o_dS"]:
        p = np.stack([results[c][n][:, 0] for c in range(4)], 1).reshape((DEPTH, 4) + shp[n])
        s = np.concatenate([results[c][n][:, 1:] for c in range(NCORES)], 1).reshape((DEPTH, NCORES * NSS) + shp[n])
        outs += [np.ascontiguousarray(p, dtype=np.float32), np.ascontiguousarray(s, dtype=np.float32)]
    return tuple(outs)


def kernel(**inputs):
    prog = get_prog()
    maps = make_in_maps(inputs)
    res = run_bass_kernel_spmd(prog.nc, maps, core_ids=list(range(NCORES)))
    return assemble(res.results)
```
